# Optimizing a Trainium2 kernel written in Bass

```python
import jax, jax.numpy as jnp
from jax import lax
import numpy as np

D_MODEL = 1024
BATCH = 16
SEQ = 2048
DEPTH = 1

N_HEADS = 8
N_KV_HEADS = 2
HEAD_DIM = 128
Q_BLOCK = 128
GRID_W = 64
ROPE_THETA = 10000.0
AXIS_ROPE_DIM = HEAD_DIM // 2
D_CONV = D_MODEL
CONV_W = 3
D_FF = 2816
EPS = 1e-6
Q_W = N_HEADS * HEAD_DIM
KV_W = N_KV_HEADS * HEAD_DIM
IN_COLS = Q_W + 2 * KV_W + 3 * D_CONV + 2 * D_MODEL

kernel_name = "hybrid_gqa_shortconv_convffn_block"


def rmsnorm(x, g):
    xf = x.astype(jnp.float32)
    y = xf * lax.rsqrt(jnp.mean(xf * xf, axis=-1, keepdims=True) + EPS)
    return (y * g.astype(jnp.float32)).astype(x.dtype)


def dwconv3(x, w):
    xp = jnp.pad(x, ((0, 0), (1, 1), (0, 0)))
    return xp[:, :-2] * w[0] + xp[:, 1:-1] * w[1] + xp[:, 2:] * w[2]


def rope_tables(pos):
    half = AXIS_ROPE_DIM // 2
    freqs = ROPE_THETA ** (-jnp.arange(half, dtype=jnp.float32) / half)
    ang = pos[:, None] * freqs[None, :]
    ang = jnp.concatenate([ang, ang], axis=-1)
    return jnp.cos(ang), jnp.sin(ang)


def apply_rot(x, cos, sin):
    d = x.shape[-1] // 2
    rot = jnp.concatenate([-x[..., d:], x[..., :d]], axis=-1)
    c = cos[:, None, :].astype(x.dtype)
    s = sin[:, None, :].astype(x.dtype)
    return x * c + rot * s


def rope2d(x, tabs):
    cos_r, sin_r, cos_c, sin_c = tabs
    xr = apply_rot(x[..., :AXIS_ROPE_DIM], cos_r, sin_r)
    xc = apply_rot(x[..., AXIS_ROPE_DIM:], cos_c, sin_c)
    return jnp.concatenate([xr, xc], axis=-1)


def blocked_gqa(q, k, v):
    B, S = q.shape[0], q.shape[1]
    nblk = S // Q_BLOCK
    G = N_HEADS // N_KV_HEADS
    scale = HEAD_DIM ** -0.5
    qb = q.reshape(B, nblk, Q_BLOCK, N_KV_HEADS, G, HEAD_DIM).transpose(1, 0, 3, 4, 2, 5)
    kt = k.transpose(0, 2, 1, 3)
    vt = v.transpose(0, 2, 1, 3)

    def one_block(qblk):
        s = jnp.einsum('bkgqd,bksd->bkgqs', qblk, kt, preferred_element_type=jnp.float32) * scale
        p = jax.nn.softmax(s, axis=-1)
        return jnp.einsum('bkgqs,bksd->bkgqd', p.astype(vt.dtype), vt)

    o = lax.map(one_block, qb)
    return o.transpose(1, 0, 4, 2, 3, 5).reshape(B, S, N_HEADS * HEAD_DIM)


def mixer_sublayer(x, tabs, pre_g, w_in, gate_b, q_norm_g, k_norm_g, conv_w,
                   w_attn_proj, w_conv_proj, w_out, post_g):
    B, S, _ = x.shape
    h = rmsnorm(x, pre_g)
    z = h @ w_in
    splits = np.cumsum([Q_W, KV_W, KV_W, D_CONV, D_CONV, D_CONV, D_MODEL]).tolist()
    q, k, v, u, b_gate, c_gate, ga, gb = jnp.split(z, splits, axis=-1)
    q = rmsnorm(q.reshape(B, S, N_HEADS, HEAD_DIM), q_norm_g)
    k = rmsnorm(k.reshape(B, S, N_KV_HEADS, HEAD_DIM), k_norm_g)
    v = v.reshape(B, S, N_KV_HEADS, HEAD_DIM)
    q = rope2d(q, tabs)
    k = rope2d(k, tabs)
    y_a = blocked_gqa(q, k, v) @ w_attn_proj
    y_b = (b_gate * dwconv3(c_gate * u, conv_w)) @ w_conv_proj
    g_a = jax.nn.sigmoid(ga + gate_b[:D_MODEL])
    g_b = jax.nn.sigmoid(gb + gate_b[D_MODEL:])
    out = (g_a * y_a + g_b * y_b) @ w_out
    return rmsnorm(out, post_g)


def ffn_sublayer(x, pre_g, w_up, conv_w, w_down, post_g):
    h = rmsnorm(x, pre_g)
    up = h @ w_up
    a, b = jnp.split(up, 2, axis=-1)
    hidden = jax.nn.gelu(dwconv3(a, conv_w), approximate=True) * b
    return rmsnorm(hidden @ w_down, post_g)


def setup_inputs(seed: int = 0) -> dict:
    key = jax.random.key(seed)
    ks = jax.random.split(key, 16)
    nrm = lambda k, shape, s: jax.random.normal(k, shape, jnp.float32) * s
    gain = lambda k, n: 1.0 + nrm(k, (DEPTH, n), 0.02)
    return {
        "x": nrm(ks[0], (BATCH, SEQ, D_MODEL), 1.0),
        "mix_pre_g": gain(ks[1], D_MODEL),
        "w_in": nrm(ks[2], (DEPTH, D_MODEL, IN_COLS), D_MODEL ** -0.5),
        "gate_b": nrm(ks[3], (DEPTH, 2 * D_MODEL), 0.01),
        "q_norm_g": gain(ks[4], HEAD_DIM),
        "k_norm_g": gain(ks[5], HEAD_DIM),
        "mix_conv_w": nrm(ks[6], (DEPTH, CONV_W, D_CONV), CONV_W ** -0.5),
        "w_attn_proj": nrm(ks[7], (DEPTH, Q_W, D_MODEL), Q_W ** -0.5),
        "w_conv_proj": nrm(ks[8], (DEPTH, D_CONV, D_MODEL), D_CONV ** -0.5),
        "w_out": nrm(ks[9], (DEPTH, D_MODEL, D_MODEL), D_MODEL ** -0.5),
        "mix_post_g": gain(ks[10], D_MODEL),
        "ffn_pre_g": gain(ks[11], D_MODEL),
        "w_up": nrm(ks[12], (DEPTH, D_MODEL, 2 * D_FF), D_MODEL ** -0.5),
        "ffn_conv_w": nrm(ks[13], (DEPTH, CONV_W, D_FF), CONV_W ** -0.5),
        "w_down": nrm(ks[14], (DEPTH, D_FF, D_MODEL), D_FF ** -0.5),
        "ffn_post_g": gain(ks[15], D_MODEL),
    }


def reference(x, mix_pre_g, w_in, gate_b, q_norm_g, k_norm_g, mix_conv_w,
              w_attn_proj, w_conv_proj, w_out, mix_post_g, ffn_pre_g, w_up,
              ffn_conv_w, w_down, ffn_post_g):
    S = x.shape[1]
    ROWS = S // GRID_W
    rows, cols = jnp.meshgrid(jnp.arange(ROWS), jnp.arange(GRID_W), indexing='ij')
    row_pos = rows.reshape(-1).astype(jnp.float32)
    col_pos = cols.reshape(-1).astype(jnp.float32)
    cos_r, sin_r = rope_tables(row_pos)
    cos_c, sin_c = rope_tables(col_pos)
    tabs = (cos_r, sin_r, cos_c, sin_c)
    for l in range(DEPTH):
        x = x + mixer_sublayer(x, tabs, mix_pre_g[l], w_in[l], gate_b[l], q_norm_g[l],
                               k_norm_g[l], mix_conv_w[l], w_attn_proj[l], w_conv_proj[l],
                               w_out[l], mix_post_g[l])
        x = x + ffn_sublayer(x, ffn_pre_g[l], w_up[l], ffn_conv_w[l], w_down[l], ffn_post_g[l])
    return x
```

```python
import numpy as np
from contextlib import ExitStack
import concourse.bass as bass
import concourse.mybir as mybir
from concourse.bass_utils import run_bass_kernel_spmd

F32 = mybir.dt.float32
BF16 = mybir.dt.bfloat16
AF = mybir.ActivationFunctionType
ALU = mybir.AluOpType

NCORES = 8
SEQ = 2048
D = 1024
NSEQ = 2
TT = 4
HD = 128
NH = 8
DFF = 2816
NJ = 22
EPS = 1e-6
SCALE = HD ** -0.5
NSLOT = 3

V_PRE, V_POST, V_FPRE, V_FPOST, V_GBA, V_GBB = 0, 8, 16, 24, 32, 40
V_QG, V_QGP, V_KG, V_KGP = 48, 49, 50, 51
V_MCW = 52
V_FCW = 76
NV = 142


class Res:
    __slots__ = ("name", "w", "r", "x")

    def __init__(self, name, excl=False):
        self.name = name
        self.w = None
        self.r = []
        self.x = excl


class Sched:
    ENGS = ("pe", "act", "dve", "pool", "sp")

    def __init__(self):
        self.q = {e: [] for e in self.ENGS}
        self.cnt = {e: 0 for e in self.ENGS}
        self.seen = {e: {} for e in self.ENGS}
        self.dcnt = {}

    def _waits(self, eng, reads, writes):
        evs = []
        for r in reads:
            if r.w is not None:
                evs.append(r.w)
        same_ok = (eng == "pe")
        for w in writes:
            if w.w is not None and (w.w[0] != eng or not same_ok):
                evs.append(w.w)
            for ev in w.r:
                if ev[0] != eng or not same_ok:
                    evs.append(ev)
        need = {}
        for k, v in evs:
            if k in self.cnt:
                assert v <= self.cnt[k], (eng, k, v, self.cnt[k])
            if v > self.seen[eng].get(k, 0):
                need[k] = max(need.get(k, 0), v)
        for k, v in need.items():
            self.seen[eng][k] = v
        return list(need.items())

    def op(self, eng, fn, reads=(), writes=(), inc=True):
        xr = [r for r in reads if r.x]
        if xr:
            writes = list(writes) + [r for r in xr if r not in writes]
            reads = [r for r in reads if not r.x]
        waits = self._waits(eng, reads, writes)
        if inc:
            self.cnt[eng] += 1
            ev = (eng, self.cnt[eng])
        else:
            ev = (eng, self.cnt[eng] + 1)
        for r in reads:
            r.r.append(ev)
        for w in writes:
            w.w = ev
            w.r = []
        self.q[eng].append((waits, fn, (eng, 1) if inc else None))
        return ev

    def dma(self, eng, fn, semkey, reads=(), writes=()):
        waits = self._waits(eng, reads, writes)
        self.dcnt[semkey] = self.dcnt.get(semkey, 0) + 16
        ev = (semkey, self.dcnt[semkey])
        for r in reads:
            r.r.append(ev)
        for w in writes:
            w.w = ev
            w.r = []
        self.q[eng].append((waits, fn, (semkey, 16)))
        return ev

    def wait_all(self, eng, evs):
        need = {}
        for k, v in evs:
            if v > self.seen[eng].get(k, 0):
                need[k] = max(need.get(k, 0), v)
        for k, v in need.items():
            self.seen[eng][k] = v
        self.q[eng].append((list(need.items()), None, None))

    def build(self, nc, es):
        sems = {}
        for k in list(self.ENGS) + sorted(self.dcnt.keys()):
            sems[k] = es.enter_context(nc.semaphore("s_" + k))
        block = es.enter_context(nc.Block())
        q = self.q

        def run(eng_name):
            def body(e):
                for waits, fn, inc in q[eng_name]:
                    for k, v in waits:
                        e.wait_ge(sems[k], v)
                    if fn is not None:
                        ins = fn(e)
                        if inc is not None:
                            ins.then_inc(sems[inc[0]], inc[1])
            return body

        block.tensor(run("pe"))
        block.scalar(run("act"))
        block.vector(run("dve"))
        block.gpsimd(run("pool"))
        block.sync(run("sp"))


def build_nc():
    nc = bass.Bass("TRN2", target_bir_lowering=False)
    xT_d = nc.dram_tensor("xT", [NSEQ, TT, 128, 4096], F32, kind="ExternalInput")
    wqkv_d = nc.dram_tensor("w_qkv", [3, 128, 4096], F32, kind="ExternalInput")
    wubc_d = nc.dram_tensor("w_ubc", [8, 128, 3072], F32, kind="ExternalInput")
    wpj_d = nc.dram_tensor("w_pj", [8, 128, 4096], F32, kind="ExternalInput")
    wo_d = nc.dram_tensor("w_o", [2, 128, 4096], F32, kind="ExternalInput")
    wup_d = nc.dram_tensor("w_upg", [11, 128, 4096], F32, kind="ExternalInput")
    wdn_d = nc.dram_tensor("w_dn", [8, 128, 2816], F32, kind="ExternalInput")
    vec_d = nc.dram_tensor("vecs", [128, NV], F32, kind="ExternalInput")
    tab_d = nc.dram_tensor("tabs", [128, 2 * SEQ], F32, kind="ExternalInput")
    cm_d = nc.dram_tensor("cmat", [128, 512], F32, kind="ExternalInput")
    yT_d = nc.dram_tensor("yT", [NSEQ, TT, 128, 4096], F32, kind="ExternalOutput")
    x1_d = nc.dram_tensor("x1s", [NSEQ, TT, 128, 4096], F32)
    wdnb_d = nc.dram_tensor("wdn_bf16", [8, 128, 2816], BF16)

    es = ExitStack()
    S = Sched()
    with es:
        sb = lambda name, shape, dt: es.enter_context(nc.sbuf_tensor(name, shape, dt))
        A = sb("A", [128, 8 * SEQ], BF16)
        O = sb("O", [128, 8 * SEQ], BF16)
        R1 = sb("R1", [128, 8 * SEQ], BF16)
        M = sb("M", [128, 8 * SEQ], BF16)
        tabs = sb("tabs_sb", [128, 2 * SEQ], F32)
        ring = [sb(f"ring{i}", [128, 4096], BF16) for i in range(NSLOT)]
        T = [sb(f"T{i}", [128, 4608], F32) for i in range(2)]
        vecs = sb("vecs_sb", [128, NV], F32)
        cmat = sb("cmat_sb", [128, 512], BF16)
        epsb = sb("epsb", [128, 1], F32)
        PS = [es.enter_context(nc.psum_tensor(f"ps{i}", [128, 1024], F32)) for i in range(4)]

        A_f = A.bitcast(F32)
        O_f = O.bitcast(F32)
        R1_f = R1.bitcast(F32)
        M_f = M.bitcast(F32)
        T_b = [t.bitcast(BF16) for t in T]

        r_A = [Res(f"A{t}") for t in range(TT)]
        r_O = [[Res(f"O{h}_{t}") for t in range(TT)] for h in range(8)]
        r_R1 = [Res(f"R1_{c}") for c in range(8)]
        r_Mb = [Res(f"Mb{b}") for b in range(16)]
        r_ring = [Res(f"ring{i}") for i in range(NSLOT)]
        r_T = [[Res(f"T{i}_{b}") for b in range(9)] for i in range(2)]
        r_bank = [[Res(f"ps{i}_{h}", excl=True) for h in range(2)] for i in range(4)]
        r_consts = [Res("eps"), Res("vec"), Res("cm")]
        r_tabs = Res("tabs")
        r_x1d = [[Res(f"x1d{s}_{t}") for t in range(TT)] for s in range(NSEQ)]
        r_wdnb = [Res(f"wdnb{d}") for d in range(8)]

        ones1024 = cmat[:, 0:128]
        ones128 = cmat[:, 128:256]
        ones1 = cmat[:, 256:384]
        pmatT = cmat[:, 384:512]

        def vcol(i):
            return vecs[:, i:i + 1]

        bank_ap = lambda i, h: PS[i][:, h * 512:(h + 1) * 512]

        ring_state = {"n": 0}

        ring_pinned = set()

        def load_slot(src_ap, L, reads=(), nb=None):
            while True:
                s = ring_state["n"] % NSLOT
                ring_state["n"] += 1
                if s not in ring_pinned:
                    break
            if nb is None:
                nb = L // 1024 if L % 1024 == 0 else L // 704
            bl = L // nb
            S.dma("pool", lambda e, s=s: e.dma_start(
                out=ring[s][:, 0:L].rearrange("p (a b) -> p a b", b=bl),
                in_=src_ap.rearrange("p (a b) -> p a b", b=bl)), f"ring{s}", reads=list(reads), writes=[r_ring[s]])
            return s

        bank_state = {"n": 0}

        bank_reserved = set()

        def next_bank():
            while True:
                n = bank_state["n"] % 8
                bank_state["n"] += 1
                if (n // 2, n % 2) not in bank_reserved:
                    return n // 2, n % 2

        def mm_group(out_ap, out_res, pairs, reads, inc_last=True):
            n = len(pairs)
            for idx, (l, r) in enumerate(pairs):
                S.op("pe", lambda e, l=l, r=r, idx=idx: e.matmul(out_ap, lhsT=l, rhs=r, start=(idx == 0), stop=(idx == n - 1)),
                     reads=reads, writes=[out_res], inc=(inc_last and idx == n - 1))

        def rstd_from(ps_ap, ps_res, out_ap, out_res):
            S.op("act", lambda e: e.activation(out=out_ap, in_=ps_ap, func=AF.Ln, bias=epsb[:, 0:1]),
                 reads=[ps_res, *r_consts], writes=[out_res])
            S.op("act", lambda e: e.activation(out=out_ap, in_=out_ap, func=AF.Exp, scale=-0.5),
                 reads=[out_res], writes=[out_res])

        S.op("dve", lambda e: e.memset(epsb[:], EPS), writes=[r_consts[0]])
        S.dma("sp", lambda e: e.dma_start(out=vecs[:], in_=vec_d[:, :]), "ld_c0", writes=[r_consts[1]])
        S.dma("sp", lambda e: e.dma_start(out=tabs[:], in_=tab_d[:, :]), "ld_c1", writes=[r_tabs])
        S.dma("pool", lambda e: e.dma_start(out=cmat[:], in_=cm_d[:, :]), "ld_c2", writes=[r_consts[2]])
        Ctab = tabs[:, 0:SEQ]
        Stab = tabs[:, SEQ:2 * SEQ]

        out_evs = []
        qk_ctr = {"n": 0}

        qk_pend = {"p": None}

        def qk_pre(pairs, reads):
            k = qk_ctr["n"] % 2
            qk_ctr["n"] += 1
            base = k * 4
            bi, bh = next_bank()
            bank_reserved.add((bi, bh))
            ps_ap, ps_res = bank_ap(bi, bh), r_bank[bi][bh]
            mm_group(ps_ap, ps_res, pairs, reads)
            qb = M[:, base * 1024: base * 1024 + 512]
            sq = M[:, base * 1024 + 512: base * 1024 + 1024]
            S.op("act", lambda e: e.activation(out=qb, in_=ps_ap, func=AF.Copy), reads=[ps_res], writes=[r_Mb[base]])
            S.op("act", lambda e: e.activation(out=sq, in_=ps_ap, func=AF.Square), reads=[ps_res], writes=[r_Mb[base]])
            return base, bi, bh

        def qk_fin(st, gcol, gpcol, dest_ap, dest_res, tt):
            base, pi_, ph_ = st
            ps_ap, ps_res = bank_ap(pi_, ph_), r_bank[pi_][ph_]
            qb = M[:, base * 1024: base * 1024 + 512]
            sq = M[:, base * 1024 + 512: base * 1024 + 1024]
            rs = M_f[:, (base + 1) * 512:(base + 2) * 512]
            t1 = M_f[:, (base + 2) * 512:(base + 3) * 512]
            t2 = M_f[:, (base + 3) * 512:(base + 4) * 512]
            r_qb = r_sq = r_Mb[base]
            r_rs, r_t1, r_t2 = r_Mb[base + 1], r_Mb[base + 2], r_Mb[base + 3]
            bi, bh = next_bank()
            mm_group(bank_ap(bi, bh), r_bank[bi][bh], [(ones128, sq)], [r_sq, *r_consts])
            ci, ch = next_bank()
            mm_group(bank_ap(ci, ch), r_bank[ci][ch], [(pmatT, qb)], [r_qb, *r_consts])
            rstd_from(bank_ap(bi, bh), r_bank[bi][bh], rs, r_rs)
            sl = slice(tt * 512, (tt + 1) * 512)
            S.op("dve", lambda e: e.scalar_tensor_tensor(out=t1, in0=ps_ap, scalar=vcol(gcol), in1=Ctab[:, sl],
                                                         op0=ALU.mult, op1=ALU.mult),
                 reads=[ps_res, *r_consts, r_tabs], writes=[r_t1])
            S.op("dve", lambda e: e.scalar_tensor_tensor(out=t2, in0=bank_ap(ci, ch), scalar=vcol(gpcol), in1=Stab[:, sl],
                                                         op0=ALU.mult, op1=ALU.mult),
                 reads=[r_bank[ci][ch], *r_consts, r_tabs], writes=[r_t2])
            S.op("dve", lambda e: e.tensor_tensor(out=t1, in0=t1, in1=t2, op=ALU.add), reads=[r_t1, r_t2], writes=[r_t1])
            S.op("dve", lambda e: e.tensor_tensor(out=dest_ap, in0=t1, in1=rs, op=ALU.mult),
                 reads=[r_t1, r_rs], writes=[dest_res])
            bank_reserved.discard((pi_, ph_))

        def qk_item(pairs, reads, gcol, gpcol, dest_ap, dest_res, tt):
            st = qk_pre(pairs, reads)
            if qk_pend["p"] is not None:
                qk_fin(*qk_pend["p"])
            qk_pend["p"] = (st, gcol, gpcol, dest_ap, dest_res, tt)

        def qk_flush():
            if qk_pend["p"] is not None:
                qk_fin(*qk_pend["p"])
                qk_pend["p"] = None

        def norm_tile(xt_f, xt_res, g0, dest_ap_fn, dest_res, sq_b, sq_res, rs_ap, rs_res):
            S.op("act", lambda e: e.activation(out=sq_b, in_=xt_f, func=AF.Square), reads=xt_res, writes=sq_res)
            bi, bh = next_bank()
            mm_group(bank_ap(bi, bh), r_bank[bi][bh],
                     [(ones1024, sq_b[:, c * 512:(c + 1) * 512]) for c in range(8)], list(sq_res) + [*r_consts])
            rstd_from(bank_ap(bi, bh), r_bank[bi][bh], rs_ap, rs_res)
            for c in range(8):
                S.op("dve", lambda e, c=c: e.scalar_tensor_tensor(
                    out=dest_ap_fn(c), in0=xt_f[:, c * 512:(c + 1) * 512], scalar=vcol(g0 + c), in1=rs_ap,
                    op0=ALU.mult, op1=ALU.mult),
                    reads=list(xt_res) + [rs_res, *r_consts], writes=[dest_res])

        def emit_ssq(d, sbi, sbh, sqd):
            S.op("pe", lambda e: e.matmul(bank_ap(sbi, sbh), lhsT=ones1024, rhs=sqd[:, d * 512:(d + 1) * 512],
                                          start=(d == 0), stop=(d == 7)),
                 reads=[r_T[1][d // 2], *r_consts], writes=[r_bank[sbi][sbh]], inc=True)

        x_bufs = [(M_f[:, 0:4096], r_Mb[0:8]),
                  (O_f[:, 4096:8192], [r_O[h][t] for h in range(4, 8) for t in range(TT)]),
                  (R1_f[:, 4096:8192], r_R1[4:8]),
                  (O_f[:, 0:4096], [r_O[h][t] for h in range(4) for t in range(TT)])]

        def emit_x_loads(sq_, eng):
            for tt in range(TT):
                xt, xres = x_bufs[tt]
                S.dma(eng, lambda e, xt=xt, sq_=sq_, tt=tt: e.dma_start(out=xt, in_=xT_d[sq_, tt]), f"ld_x{tt}_{eng}",
                      reads=(list(x_bufs[tt - 1][1]) if tt > 0 else []), writes=xres)

        for s in range(NSEQ):
            if s == 0:
                emit_x_loads(0, "sp")
            for tt in range(TT):
                xt, xres = x_bufs[tt]
                sqb = T_b[1][:, 0:4096]
                norm_tile(xt, xres, V_PRE,
                          lambda c, tt=tt: A[:, c * SEQ + tt * 512: c * SEQ + (tt + 1) * 512], r_A[tt],
                          sqb, r_T[1][0:4], T[1][:, 2560:3072], r_T[1][5])

            kT = lambda kvh: R1[:, kvh * SEQ:(kvh + 1) * SEQ]
            Vt = R1[:, 2 * SEQ:4 * SEQ]
            sl_kv = load_slot(wqkv_d[2], 4096)
            rg = ring[sl_kv]
            for tt in range(TT):
                for kvh in range(2):
                    qk_item([(rg[:, c * 512 + kvh * 128: c * 512 + (kvh + 1) * 128],
                              A[:, c * SEQ + tt * 512: c * SEQ + (tt + 1) * 512]) for c in range(8)],
                            [r_ring[sl_kv], r_A[tt]], V_KG, V_KGP,
                            kT(kvh)[:, tt * 512:(tt + 1) * 512], r_R1[kvh], tt)
            for kt2 in range(8):
                bi, bh = next_bank()
                for q in range(2):
                    kt = kt2 * 2 + q
                    mm_group(PS[bi][:, bh * 512 + q * 256: bh * 512 + (q + 1) * 256], r_bank[bi][bh],
                             [(A[:, c * SEQ + kt * 128: c * SEQ + (kt + 1) * 128],
                               rg[:, c * 512 + 256: c * 512 + 512]) for c in range(8)],
                             [r_ring[sl_kv], r_A[kt // 4]])
                S.op("act", lambda e, bi=bi, bh=bh, kt2=kt2: e.activation(
                    out=Vt[:, kt2 * 512:(kt2 + 1) * 512], in_=bank_ap(bi, bh), func=AF.Copy),
                    reads=[r_bank[bi][bh]], writes=[r_R1[2 + kt2 // 4]])

            def p_ap(j):
                return R1[:, (4 + j) * SEQ:(5 + j) * SEQ] if j < 4 else M[:, j * SEQ:(j + 1) * SEQ]

            def p_res(j):
                return [r_R1[4 + j]] if j < 4 else [r_Mb[2 * j], r_Mb[2 * j + 1]]

            S.op("dve", lambda e: e.memset(T[0][:, 0:1], 0.0), writes=[r_T[0][0]])
            S.op("dve", lambda e: e.memset(T[0][:, 2049:2050], 0.0), writes=[r_T[0][4]])
            c_wrow = T[0]
            c_yrow = T[0][:, 2560:4608]
            c_brow = T[1][:, 2560:4608]
            sl_q = None
            for j in range(8):
                if j % 4 == 0:
                    if sl_q is not None:
                        ring_pinned.discard(sl_q)
                    sl_q = load_slot(wqkv_d[j // 4], 4096)
                    ring_pinned.add(sl_q)
                rgq = ring[sl_q]
                hq = j % 4
                sl = load_slot(wubc_d[j], 3072)
                rg = ring[sl]
                for tt in range(TT):
                    qk_item([(rgq[:, c * 512 + hq * 128: c * 512 + (hq + 1) * 128],
                              A[:, c * SEQ + tt * 512: c * SEQ + (tt + 1) * 512]) for c in range(8)],
                            [r_ring[sl_q], r_A[tt]], V_QG, V_QGP,
                            O[:, j * SEQ + tt * 512: j * SEQ + (tt + 1) * 512], r_O[j][tt], tt)
                    bks = []
                    for k in (2, 0, 1):
                        bi, bh = next_bank()
                        mm_group(bank_ap(bi, bh), r_bank[bi][bh],
                                 [(rg[:, c * 384 + k * 128: c * 384 + (k + 1) * 128],
                                   A[:, c * SEQ + tt * 512: c * SEQ + (tt + 1) * 512]) for c in range(8)],
                                 [r_ring[sl], r_A[tt]])
                        bks.append((bi, bh))
                    (cbi, cbh), (ubi, ubh), (bbi, bbh) = bks
                    cs = T[1][:, (tt % 2) * 512:(tt % 2 + 1) * 512]
                    r_cs = r_T[1][tt % 2]
                    S.op("act", lambda e, cs=cs, cbi=cbi, cbh=cbh: e.activation(out=cs, in_=bank_ap(cbi, cbh), func=AF.Copy),
                         reads=[r_bank[cbi][cbh]], writes=[r_cs])
                    S.op("dve", lambda e, cs=cs, ubi=ubi, ubh=ubh, tt=tt: e.tensor_tensor(
                        out=c_wrow[:, 1 + tt * 512: 1 + (tt + 1) * 512], in0=bank_ap(ubi, ubh), in1=cs, op=ALU.mult),
                        reads=[r_bank[ubi][ubh], r_cs], writes=[r_T[0][tt], r_T[0][tt + 1]])
                    S.op("act", lambda e, bbi=bbi, bbh=bbh, tt=tt: e.activation(
                        out=c_brow[:, tt * 512:(tt + 1) * 512], in_=bank_ap(bbi, bbh), func=AF.Copy),
                        reads=[r_bank[bbi][bbh]], writes=[r_T[1][5 + tt]])
                rw = r_T[0][0:5]
                ry = r_T[0][5:9]
                S.op("dve", lambda e, j=j: e.tensor_scalar(out=c_yrow, in0=c_wrow[:, 0:2048], scalar1=vcol(V_MCW + j), scalar2=None,
                                                           op0=ALU.mult), reads=rw + [*r_consts], writes=ry)
                S.op("dve", lambda e, j=j: e.scalar_tensor_tensor(out=c_yrow, in0=c_wrow[:, 1:2049], scalar=vcol(V_MCW + 8 + j),
                                                                  in1=c_yrow, op0=ALU.mult, op1=ALU.add),
                     reads=rw + ry + [*r_consts], writes=ry)
                S.op("dve", lambda e, j=j: e.scalar_tensor_tensor(out=c_yrow, in0=c_wrow[:, 2:2050], scalar=vcol(V_MCW + 16 + j),
                                                                  in1=c_yrow, op0=ALU.mult, op1=ALU.add),
                     reads=rw + ry + [*r_consts], writes=ry)
                S.op("dve", lambda e, j=j: e.tensor_tensor(out=p_ap(j), in0=c_yrow, in1=c_brow, op=ALU.mult),
                     reads=ry + r_T[1][5:9], writes=p_res(j))
            qk_flush()
            ring_pinned.discard(sl_q)

            if s == 0:
                for d in range(8):
                    S.dma("pool", lambda e, d=d: e.dma_start(
                        out=wdnb_d[d].rearrange("p (a b) -> p a b", b=704),
                        in_=wdn_d[d].rearrange("p (a b) -> p a b", b=704)), f"cv_wdn{d}", reads=[r_O[7][3]], writes=[r_wdnb[d]])

            for hh in range(NH):
                kvh = hh // 4
                for m2 in range(2):
                    qts = (2 * m2, 2 * m2 + 1)
                    qaps = [O[:, hh * SEQ + qt * 512: hh * SEQ + (qt + 1) * 512] for qt in qts]
                    r_qs = [r_O[hh][qt] for qt in qts]

                    def emit_qk(st, kp):
                        for q in range(2):
                            kt = kp * 2 + q
                            S.op("pe", lambda e, st=st, q=q, kt=kt, kvh=kvh, qap=qaps[st]: e.matmul(
                                PS[st][:, q * 512:(q + 1) * 512], lhsT=kT(kvh)[:, kt * 128:(kt + 1) * 128], rhs=qap,
                                start=True, stop=True),
                                reads=[r_R1[kvh], r_qs[st]], writes=[r_bank[st][0], r_bank[st][1]], inc=(q == 1))

                    def pt_buf(st, kp):
                        b = st * 2 + (kp % 2)
                        return T_b[0][:, b * 1024:(b + 1) * 1024], r_T[0][b]

                    def pa_buf(st, kp):
                        b = 4 + st * 2 + (kp % 2)
                        return T_b[0][:, b * 1024:b * 1024 + 512], r_T[0][b]

                    def emit_exp(st, kp):
                        PT, r_PT = pt_buf(st, kp)
                        S.op("act", lambda e, PT=PT, st=st: e.activation(out=PT, in_=PS[st][:, :], func=AF.Exp, scale=SCALE),
                             reads=[r_bank[st][0], r_bank[st][1]], writes=[r_PT])
                        pa, r_pa = pa_buf(st, kp)
                        S.op("dve", lambda e, PT=PT, pa=pa: e.tensor_tensor(out=pa, in0=PT[:, 0:512], in1=PT[:, 512:1024],
                                                                           op=ALU.add), reads=[r_PT], writes=[r_pa])

                    def emit_pv(st, kp):
                        PT, r_PT = pt_buf(st, kp)
                        oacc = bank_ap(2, st)
                        for q in range(2):
                            kt = kp * 2 + q
                            S.op("pe", lambda e, q=q, kt=kt, oacc=oacc, kvh=kvh, PT=PT: e.matmul(
                                oacc, lhsT=Vt[:, kt * 256 + kvh * 128: kt * 256 + (kvh + 1) * 128],
                                rhs=PT[:, q * 512:(q + 1) * 512], start=(kt == 0), stop=(kt == 15)),
                                reads=[r_PT, r_R1[2 + kt // 8]], writes=[r_bank[2][st]], inc=(q == 1))

                    def emit_den(st, kp):
                        pa, r_pa = pa_buf(st, kp)
                        den = bank_ap(3, st)
                        S.op("pe", lambda e, pa=pa, den=den, kp=kp: e.matmul(den, lhsT=ones1, rhs=pa, start=(kp == 0), stop=(kp == 7)),
                             reads=[r_pa, *r_consts], writes=[r_bank[3][st]], inc=True)

                    for st in range(2):
                        emit_qk(st, 0)
                    for kp in range(8):
                        for st in range(2):
                            emit_exp(st, kp)
                            if kp + 1 < 8:
                                emit_qk(st, kp + 1)
                            emit_pv(st, kp)
                            if kp >= 1:
                                emit_den(st, kp - 1)
                    for st in range(2):
                        emit_den(st, 7)
                        rd = T[1][:, st * 512:(st + 1) * 512]
                        r_rd = r_T[1][st]
                        S.op("act", lambda e, rd=rd, st=st: e.activation(out=rd, in_=bank_ap(3, st), func=AF.Ln),
                             reads=[r_bank[3][st]], writes=[r_rd])
                        S.op("act", lambda e, rd=rd: e.activation(out=rd, in_=rd, func=AF.Exp, scale=-1.0),
                             reads=[r_rd], writes=[r_rd])
                        S.op("dve", lambda e, rd=rd, st=st, qap=qaps[st]: e.tensor_tensor(out=qap, in0=bank_ap(2, st), in1=rd,
                                                                                       op=ALU.mult),
                             reads=[r_bank[2][st], r_rd], writes=[r_qs[st]])

            def mg_ap(j, tt):
                if j < 4:
                    return R1[:, j * SEQ + tt * 512: j * SEQ + (tt + 1) * 512]
                return M[:, (j - 4) * SEQ + tt * 512:(j - 4) * SEQ + (tt + 1) * 512]

            def mg_res(j, tt):
                return [r_R1[j]] if j < 4 else [r_Mb[((j - 4) * 4 + tt) // 2]]

            def p_tile_res(c, tt):
                return [r_R1[4 + c]] if c < 4 else [r_Mb[(c * 4 + tt) // 2]]

            for j in range(8):
                sl = load_slot(wpj_d[j], 4096)
                rg = ring[sl]
                for tt in range(TT):
                    tsl = lambda c, tt=tt: slice(c * SEQ + tt * 512, c * SEQ + (tt + 1) * 512)
                    src_fns = (
                        (lambda c, tt=tt: O[:, tsl(c)], lambda c, tt=tt: [r_O[c][tt]]),
                        (lambda c, tt=tt: p_ap(c)[:, tt * 512:(tt + 1) * 512], lambda c, tt=tt: p_tile_res(c, tt)),
                        (lambda c, tt=tt: A[:, tsl(c)], lambda c, tt=tt: [r_A[tt]]),
                        (lambda c, tt=tt: A[:, tsl(c)], lambda c, tt=tt: [r_A[tt]]),
                    )
                    bk = []
                    for k, (ap_fn, rs_fn) in enumerate(src_fns):
                        bi, bh = next_bank()
                        rds = [r_ring[sl]]
                        for c in range(8):
                            for r_ in rs_fn(c):
                                if r_ not in rds:
                                    rds.append(r_)
                        mm_group(bank_ap(bi, bh), r_bank[bi][bh],
                                 [(rg[:, c * 512 + k * 128: c * 512 + (k + 1) * 128], ap_fn(c)) for c in range(8)], rds)
                        bk.append((bi, bh))
                    sa = T[tt % 2][:, 0:512]
                    sbg = T[tt % 2][:, 512:1024]
                    m1 = T[tt % 2][:, 1024:1536]
                    r_sa, r_sb, r_m1 = r_T[tt % 2][0], r_T[tt % 2][1], r_T[tt % 2][2]
                    S.op("act", lambda e, sa=sa, b=bk[2], j=j: e.activation(out=sa, in_=bank_ap(*b), func=AF.Sigmoid,
                                                                             bias=vcol(V_GBA + j)),
                         reads=[r_bank[bk[2][0]][bk[2][1]], *r_consts], writes=[r_sa])
                    S.op("act", lambda e, sbg=sbg, b=bk[3], j=j: e.activation(out=sbg, in_=bank_ap(*b), func=AF.Sigmoid,
                                                                               bias=vcol(V_GBB + j)),
                         reads=[r_bank[bk[3][0]][bk[3][1]], *r_consts], writes=[r_sb])
                    S.op("dve", lambda e, sa=sa, m1=m1, b=bk[0]: e.tensor_tensor(out=m1, in0=bank_ap(*b), in1=sa, op=ALU.mult),
                         reads=[r_bank[bk[0][0]][bk[0][1]], r_sa], writes=[r_m1])
                    S.op("dve", lambda e, sbg=sbg, b=bk[1]: e.tensor_tensor(out=sbg, in0=bank_ap(*b), in1=sbg, op=ALU.mult),
                         reads=[r_bank[bk[1][0]][bk[1][1]], r_sb], writes=[r_sb])
                    S.op("dve", lambda e, sbg=sbg, m1=m1, j=j, tt=tt: e.tensor_tensor(
                        out=mg_ap(j, tt), in0=m1, in1=sbg, op=ALU.add),
                        reads=[r_m1, r_sb], writes=mg_res(j, tt))

            for blk in range(2):
                S.dma("pool", lambda e, blk=blk: e.dma_start(
                    out=O[:, blk * 4096:(blk + 1) * 4096].rearrange("p (a b) -> p a b", b=1024),
                    in_=wo_d[blk].rearrange("p (a b) -> p a b", b=1024)), f"ld_wo{blk}",
                    writes=[r_O[h][t] for h in (2 * blk, 2 * blk + 1) for t in range(TT)])
            wo_res = lambda d: [r_O[h][t] for h in (2 * (d // 4), 2 * (d // 4) + 1) for t in range(TT)]
            out_state = {}

            def mo_tile(tt):
                if tt % 2 == 0:
                    return T[0][:, 0:4096], r_T[0][0:8]
                return O_f[:, 4096:8192], [r_O[h][t] for h in range(4, 8) for t in range(TT)]

            def mo_buf(tt, d):
                if tt % 2 == 0:
                    return T[0][:, d * 512:(d + 1) * 512], [r_T[0][d]]
                return (O_f[:, 4096 + d * 512: 4096 + (d + 1) * 512],
                        [r_O[4 + d // 2][(d % 2) * 2], r_O[4 + d // 2][(d % 2) * 2 + 1]])

            SQB = (4, 7, 8)

            def out_ssq(tt, d):
                xt2, xres, sbi, sbh = out_state[tt]
                b = SQB[d % 3]
                S.op("pe", lambda e: e.matmul(bank_ap(sbi, sbh), lhsT=ones1024, rhs=T_b[1][:, b * 1024: b * 1024 + 512],
                                              start=(d == 0), stop=(d == 7)),
                     reads=[r_T[1][b], *r_consts], writes=[r_bank[sbi][sbh]], inc=True)

            def out_A(tt, d0, d1):
                hb = tt % 2
                xt2 = R1_f[:, 4096:8192] if hb == 0 else M_f[:, 4096:8192]
                xres = r_R1[4:8] if hb == 0 else r_Mb[8:16]
                if d0 == 0:
                    sbi, sbh = next_bank()
                    bank_reserved.add((sbi, sbh))
                    out_state[tt] = (xt2, xres, sbi, sbh)
                if d1 == 8:
                    S.dma("sp", lambda e, xt2=xt2, s=s, tt=tt: e.dma_start(out=xt2, in_=xT_d[s, tt]), f"ld_y{hb}", writes=xres)
                for d in range(d0, d1):
                    bi, bh = next_bank()
                    mm_group(bank_ap(bi, bh), r_bank[bi][bh],
                             [(O[:, (d // 4) * 4096 + c * 512 + (d % 4) * 128: (d // 4) * 4096 + c * 512 + (d % 4 + 1) * 128],
                               mg_ap(c, tt)) for c in range(8)],
                             wo_res(d) + [r_ for c in range(8) for r_ in mg_res(c, tt)])
                    if d >= 1:
                        out_ssq(tt, d - 1)
                    mo_ap, mo_res = mo_buf(tt, d)
                    if d % 2 == 0:
                        S.op("act", lambda e, mo_ap=mo_ap, bi=bi, bh=bh, d=d: e.activation(
                            out=mo_ap, in_=bank_ap(bi, bh), func=AF.Copy, scale=vcol(V_POST + d)),
                            reads=[r_bank[bi][bh], *r_consts], writes=mo_res)
                    else:
                        S.op("dve", lambda e, mo_ap=mo_ap, bi=bi, bh=bh, d=d: e.tensor_scalar(
                            out=mo_ap, in0=bank_ap(bi, bh), scalar1=vcol(V_POST + d), scalar2=None, op0=ALU.mult),
                            reads=[r_bank[bi][bh], *r_consts], writes=mo_res)
                    b = SQB[d % 3]
                    S.op("act", lambda e, b=b, bi=bi, bh=bh: e.activation(out=T_b[1][:, b * 1024: b * 1024 + 512],
                                                                          in_=bank_ap(bi, bh), func=AF.Square),
                         reads=[r_bank[bi][bh]], writes=[r_T[1][b]])
                if d1 == 8:
                    out_ssq(tt, 7)

            def out_B1(tt):
                xt2, xres, sbi, sbh = out_state[tt]
                hb = tt % 2
                rs = T[1][:, 2560:3072]
                rstd_from(bank_ap(sbi, sbh), r_bank[sbi][sbh], rs, r_T[1][5])
                bank_reserved.discard((sbi, sbh))
                mt, mt_res = mo_tile(tt)
                mt3 = mt.rearrange("p (c n) -> p c n", n=512)
                rs3 = rs.unsqueeze(1).to_broadcast([128, 8, 512])
                S.op("dve", lambda e, mt3=mt3, rs3=rs3: e.tensor_tensor(out=mt3, in0=mt3, in1=rs3, op=ALU.mult),
                     reads=mt_res + [r_T[1][5]], writes=mt_res)
                S.op("dve", lambda e, mt=mt, xt2=xt2: e.tensor_tensor(out=xt2, in0=mt, in1=xt2, op=ALU.add),
                     reads=mt_res + xres, writes=xres)
                S.dma("sp", lambda e, xt2=xt2, s=s, tt=tt: e.dma_start(out=x1_d[s, tt], in_=xt2), f"st_x1{hb}",
                      reads=xres, writes=[r_x1d[s][tt]])

            def out_C1(tt):
                xt2, xres, sbi, sbh = out_state[tt]
                S.op("act", lambda e, xt2=xt2: e.activation(out=T_b[1][:, 0:4096], in_=xt2, func=AF.Square),
                     reads=xres, writes=r_T[1][0:4])

            def out_C2(tt):
                bi, bh = next_bank()
                bank_reserved.add((bi, bh))
                mm_group(bank_ap(bi, bh), r_bank[bi][bh],
                         [(ones1024, T_b[1][:, c * 512:(c + 1) * 512]) for c in range(8)], r_T[1][0:4] + [*r_consts])
                out_state[("c", tt)] = (bi, bh)

            def out_C3(tt):
                xt2, xres, sbi, sbh = out_state[tt]
                bi, bh = out_state[("c", tt)]
                rs2 = T[1][:, 3072:3584]
                rstd_from(bank_ap(bi, bh), r_bank[bi][bh], rs2, r_T[1][6])
                bank_reserved.discard((bi, bh))
                for c in range(8):
                    S.op("dve", lambda e, c=c, xt2=xt2, tt=tt: e.scalar_tensor_tensor(
                        out=A[:, c * SEQ + tt * 512: c * SEQ + (tt + 1) * 512], in0=xt2[:, c * 512:(c + 1) * 512],
                        scalar=vcol(V_FPRE + c), in1=rs2, op0=ALU.mult, op1=ALU.mult),
                        reads=list(xres) + [r_T[1][6], *r_consts], writes=[r_A[tt]])

            out_A(0, 0, 8)
            out_B1(0)
            out_A(1, 0, 8)
            out_C1(0)
            out_B1(1)
            for tt in range(2, TT):
                out_A(tt, 0, 4)
                out_C2(tt - 2)
                out_C3(tt - 2)
                out_A(tt, 4, 8)
                out_C1(tt - 1)
                out_B1(tt)
            out_C2(TT - 2)
            out_C3(TT - 2)
            out_C1(TT - 1)
            out_C2(TT - 1)
            out_C3(TT - 1)

            hid_buf = lambda j: (O, R1, M)[j // 8]

            def hid_res(j, tt):
                if j < 8:
                    return [r_O[j][tt]]
                if j < 16:
                    return [r_R1[j % 8]]
                return [r_Mb[((j % 8) * 4 + tt) // 2]]

            for i in range(2):
                S.op("dve", lambda e, i=i: e.memset(T[i][:, 0:1], 0.0), writes=[r_T[i][0]])
                S.op("dve", lambda e, i=i: e.memset(T[i][:, 2049:2050], 0.0), writes=[r_T[i][4]])
            for g in range(11):
                sl = load_slot(wup_d[g], 4096)
                rg = ring[sl]
                for jj in range(2):
                    j = 2 * g + jj
                    ti = j % 2
                    arow = T[ti]
                    yrow = T[ti][:, 2560:4608]
                    ra = r_T[ti][0:5]
                    ry = r_T[ti][5:9]
                    for tt in range(TT):
                        bi, bh = next_bank()
                        mm_group(bank_ap(bi, bh), r_bank[bi][bh],
                                 [(rg[:, c * 512 + jj * 128: c * 512 + (jj + 1) * 128],
                                   A[:, c * SEQ + tt * 512: c * SEQ + (tt + 1) * 512]) for c in range(8)],
                                 [r_ring[sl], r_A[tt]])
                        S.op("act", lambda e, bi=bi, bh=bh, tt=tt, arow=arow: e.activation(
                            out=arow[:, 1 + tt * 512: 1 + (tt + 1) * 512], in_=bank_ap(bi, bh), func=AF.Copy),
                            reads=[r_bank[bi][bh]], writes=[r_T[ti][tt], r_T[ti][tt + 1]])
                    bb = []
                    for tt in range(TT):
                        bi, bh = next_bank()
                        mm_group(bank_ap(bi, bh), r_bank[bi][bh],
                                 [(rg[:, c * 512 + 256 + jj * 128: c * 512 + 256 + (jj + 1) * 128],
                                   A[:, c * SEQ + tt * 512: c * SEQ + (tt + 1) * 512]) for c in range(8)],
                                 [r_ring[sl], r_A[tt]])
                        bb.append((bi, bh))
                    S.op("dve", lambda e, j=j, arow=arow, yrow=yrow: e.tensor_scalar(
                        out=yrow, in0=arow[:, 0:2048], scalar1=vcol(V_FCW + j), scalar2=None, op0=ALU.mult),
                        reads=ra + [*r_consts], writes=ry)
                    S.op("dve", lambda e, j=j, arow=arow, yrow=yrow: e.scalar_tensor_tensor(
                        out=yrow, in0=arow[:, 1:2049], scalar=vcol(V_FCW + NJ + j), in1=yrow, op0=ALU.mult, op1=ALU.add),
                        reads=ra + ry + [*r_consts], writes=ry)
                    S.op("dve", lambda e, j=j, arow=arow, yrow=yrow: e.scalar_tensor_tensor(
                        out=yrow, in0=arow[:, 2:2050], scalar=vcol(V_FCW + 2 * NJ + j), in1=yrow, op0=ALU.mult, op1=ALU.add),
                        reads=ra + ry + [*r_consts], writes=ry)
                    S.op("act", lambda e, yrow=yrow: e.activation(out=yrow, in_=yrow, func=AF.Gelu_apprx_tanh),
                         reads=ry, writes=ry)
                    hb_ = hid_buf(j)
                    for tt in range(TT):
                        bi, bh = bb[tt]
                        S.op("dve", lambda e, bi=bi, bh=bh, tt=tt, yrow=yrow, hb_=hb_, j=j: e.tensor_tensor(
                            out=hb_[:, (j % 8) * SEQ + tt * 512:(j % 8) * SEQ + (tt + 1) * 512],
                            in0=bank_ap(bi, bh), in1=yrow[:, tt * 512:(tt + 1) * 512], op=ALU.mult),
                            reads=[r_bank[bi][bh], r_T[ti][5 + tt]], writes=hid_res(j, tt))

            def o2_buf(tt, d):
                if tt % 2 == 0:
                    return T[0][:, d * 512:(d + 1) * 512], [r_T[0][d]]
                if d < 4:
                    return M_f[:, (12 + d) * 512:(13 + d) * 512], [r_Mb[12 + d]]
                b = 4 if d == 7 else 6 + (d - 4)
                return T[1][:, b * 512:(b + 1) * 512], [r_T[1][b]]

            r_x1h = [Res(f"x1h{s}_0"), Res(f"x1h{s}_1")]
            for tt in range(TT):
                hb = tt % 2
                x1t = A_f[:, hb * 4096:(hb + 1) * 4096]
                x1res = [r_x1h[hb]]
                S.dma("sp", lambda e, x1t=x1t, s=s, tt=tt: e.dma_start(out=x1t, in_=x1_d[s, tt]), f"ld_x1{hb}",
                      reads=[r_x1d[s][tt]], writes=x1res + (r_A if tt < 2 else []))
                sqd = T_b[1][:, 0:4096]
                sbi, sbh = next_bank()
                for d in range(8):
                    sl = load_slot(wdnb_d[d], 2816, reads=[r_wdnb[d]], nb=1)
                    rg = ring[sl]
                    bi, bh = next_bank()
                    if (bi, bh) == (sbi, sbh):
                        bi, bh = next_bank()
                    rds = [r_ring[sl]]
                    for j in range(NJ):
                        for r_ in hid_res(j, tt):
                            if r_ not in rds:
                                rds.append(r_)
                    mm_group(bank_ap(bi, bh), r_bank[bi][bh],
                             [(rg[:, j * 128:(j + 1) * 128],
                               hid_buf(j)[:, (j % 8) * SEQ + tt * 512:(j % 8) * SEQ + (tt + 1) * 512]) for j in range(NJ)], rds)
                    if d >= 1:
                        emit_ssq(d - 1, sbi, sbh, sqd)
                    o2_ap, o2_res = o2_buf(tt, d)
                    S.op("act", lambda e, o2_ap=o2_ap, bi=bi, bh=bh: e.activation(out=o2_ap, in_=bank_ap(bi, bh), func=AF.Copy),
                         reads=[r_bank[bi][bh]], writes=o2_res)
                    S.op("act", lambda e, d=d, bi=bi, bh=bh: e.activation(out=sqd[:, d * 512:(d + 1) * 512], in_=bank_ap(bi, bh),
                                                                          func=AF.Square),
                         reads=[r_bank[bi][bh]], writes=[r_T[1][d // 2]])
                emit_ssq(7, sbi, sbh, sqd)
                if tt == TT - 1 and s + 1 < NSEQ:
                    emit_x_loads(s + 1, "pool")
                rs = T[1][:, 2560:3072]
                rstd_from(bank_ap(sbi, sbh), r_bank[sbi][sbh], rs, r_T[1][5])
                for d in range(8):
                    o2_ap, o2_res = o2_buf(tt, d)
                    S.op("dve", lambda e, d=d, o2_ap=o2_ap: e.scalar_tensor_tensor(
                        out=o2_ap, in0=o2_ap, scalar=vcol(V_FPOST + d), in1=rs,
                        op0=ALU.mult, op1=ALU.mult), reads=o2_res + [r_T[1][5], *r_consts], writes=o2_res)
                    S.op("dve", lambda e, d=d, x1t=x1t, o2_ap=o2_ap: e.tensor_tensor(
                        out=x1t[:, d * 512:(d + 1) * 512], in0=o2_ap, in1=x1t[:, d * 512:(d + 1) * 512],
                        op=ALU.add), reads=o2_res + x1res, writes=x1res)
                out_evs.append(S.dma("sp", lambda e, x1t=x1t, s=s, tt=tt: e.dma_start(out=yT_d[s, tt], in_=x1t),
                                     f"st_y{hb}", reads=x1res + r_A))

        S.wait_all("sp", out_evs)
        S.build(nc, es)
    return nc


def _host_layout(inp):
    f = lambda a: np.ascontiguousarray(np.asarray(a, dtype=np.float32))
    w_in = f(inp["w_in"])[0]
    w_ap = f(inp["w_attn_proj"])[0]
    w_cp = f(inp["w_conv_proj"])[0]
    w_out = f(inp["w_out"])[0]
    w_up = f(inp["w_up"])[0]
    w_down = f(inp["w_down"])[0]
    sh = {}
    sh["w_qkv"] = f(w_in[:, 0:1536].reshape(8, 128, 3, 512).transpose(2, 1, 0, 3).reshape(3, 128, 4096))
    sh["w_ubc"] = f(w_in[:, 1536:4608].reshape(8, 128, 3, 8, 128).transpose(3, 1, 0, 2, 4).reshape(8, 128, 3072))
    W4 = np.stack([w_ap, w_cp, w_in[:, 4608:5632], w_in[:, 5632:6656]], axis=0)
    sh["w_pj"] = f(W4.reshape(4, 8, 128, 8, 128).transpose(3, 2, 1, 0, 4).reshape(8, 128, 4096))
    sh["w_o"] = f(w_out.reshape(8, 128, 2, 512).transpose(2, 1, 0, 3).reshape(2, 128, 4096))
    sh["w_upg"] = f(w_up.reshape(8, 128, 2, 11, 2, 128).transpose(3, 1, 0, 2, 4, 5).reshape(11, 128, 4096))
    sh["w_dn"] = f(w_down.reshape(22, 128, 8, 128).transpose(2, 1, 0, 3).reshape(8, 128, 2816))
    vec = np.zeros((128, NV), np.float32)
    col8 = lambda v: f(v).reshape(8, 128).T
    vec[:, V_PRE:V_PRE + 8] = col8(inp["mix_pre_g"][0])
    vec[:, V_POST:V_POST + 8] = col8(inp["mix_post_g"][0])
    vec[:, V_FPRE:V_FPRE + 8] = col8(inp["ffn_pre_g"][0])
    vec[:, V_FPOST:V_FPOST + 8] = col8(inp["ffn_post_g"][0])
    gb = f(inp["gate_b"])[0]
    vec[:, V_GBA:V_GBA + 8] = col8(gb[:D])
    vec[:, V_GBB:V_GBB + 8] = col8(gb[D:])
    perm = np.arange(128)
    perm = np.where((perm % 64) < 32, perm + 32, perm - 32)
    qg = f(inp["q_norm_g"])[0]
    kg = f(inp["k_norm_g"])[0]
    vec[:, V_QG] = qg
    vec[:, V_QGP] = qg[perm]
    vec[:, V_KG] = kg
    vec[:, V_KGP] = kg[perm]
    mcw = f(inp["mix_conv_w"])[0]
    for k in range(3):
        vec[:, V_MCW + 8 * k:V_MCW + 8 * (k + 1)] = mcw[k].reshape(8, 128).T
    fcw = f(inp["ffn_conv_w"])[0]
    for k in range(3):
        vec[:, V_FCW + NJ * k:V_FCW + NJ * (k + 1)] = fcw[k].reshape(NJ, 128).T
    sh["vecs"] = vec
    t = np.arange(SEQ)
    row_pos = (t // 64).astype(np.float64)
    col_pos = (t % 64).astype(np.float64)
    freqs = 10000.0 ** (-(np.arange(32, dtype=np.float64) / 32.0))
    tabs = np.zeros((128, 2 * SEQ), np.float32)
    for d in range(128):
        pos = row_pos if d < 64 else col_pos
        ang = pos * freqs[d % 32]
        tabs[d, 0:SEQ] = np.cos(ang)
        tabs[d, SEQ:] = np.sin(ang)
    sh["tabs"] = tabs
    cm = np.zeros((128, 512), np.float32)
    cm[:, 0:128] = 1.0 / 1024.0
    cm[:, 128:256] = 1.0 / 128.0
    cm[:, 256:384] = 1.0
    for m in range(128):
        if (m % 64) < 32:
            cm[m + 32, 384 + m] = -1.0
        else:
            cm[m - 32, 384 + m] = 1.0
    sh["cmat"] = cm
    return sh


_NC_CACHE = {}


def kernel(**inputs):
    x = np.asarray(inputs["x"], dtype=np.float32)
    shared = _host_layout(inputs)
    in_maps = []
    for core in range(NCORES):
        xs = x[core * NSEQ:(core + 1) * NSEQ]
        xT = np.ascontiguousarray(xs.reshape(NSEQ, TT, 512, 8, 128).transpose(0, 1, 4, 3, 2)).reshape(NSEQ, TT, 128, 4096)
        m = dict(shared)
        m["xT"] = xT
        in_maps.append(m)
    if "nc" not in _NC_CACHE:
        _NC_CACHE["nc"] = build_nc()
    nc = _NC_CACHE["nc"]
    res = run_bass_kernel_spmd(nc, in_maps, core_ids=list(range(NCORES)))
    outs = []
    for core in range(NCORES):
        yT = np.asarray(res.results[core]["yT"]).reshape(NSEQ, TT, 128, 8, 512)
        outs.append(yT.transpose(0, 1, 4, 3, 2).reshape(NSEQ, SEQ, D))
    return np.ascontiguousarray(np.concatenate(outs, axis=0).astype(np.float32))
```

```python
import numpy as np
from contextlib import ExitStack
import concourse.bass as bass
import concourse.mybir as mybir
from concourse.bass_utils import run_bass_kernel_spmd

F32 = mybir.dt.float32
BF16 = mybir.dt.bfloat16
AF = mybir.ActivationFunctionType
ALU = mybir.AluOpType

NCORES = 8
SEQ = 2048
D = 1024
NSEQ = 2
TT = 4
HD = 128
NH = 8
DFF = 2816
NJ = 22
EPS = 1e-6
SCALE = HD ** -0.5
NSLOT = 3

V_PRE, V_POST, V_FPRE, V_FPOST, V_GBA, V_GBB = 0, 8, 16, 24, 32, 40
V_QG, V_QGP, V_KG, V_KGP = 48, 49, 50, 51
V_MCW = 52
V_FCW = 76
NV = 142


class Res:
    __slots__ = ("name", "w", "r", "x")

    def __init__(self, name, excl=False):
        self.name = name
        self.w = None
        self.r = []
        self.x = excl


class Sched:
    ENGS = ("pe", "act", "dve", "pool", "sp")

    def __init__(self):
        self.q = {e: [] for e in self.ENGS}
        self.cnt = {e: 0 for e in self.ENGS}
        self.seen = {e: {} for e in self.ENGS}
        self.dcnt = {}

    def _waits(self, eng, reads, writes):
        evs = []
        for r in reads:
            if r.w is not None:
                evs.append(r.w)
        same_ok = (eng == "pe")
        for w in writes:
            if w.w is not None and (w.w[0] != eng or not same_ok):
                evs.append(w.w)
            for ev in w.r:
                if ev[0] != eng or not same_ok:
                    evs.append(ev)
        need = {}
        for k, v in evs:
            if k in self.cnt:
                assert v <= self.cnt[k], (eng, k, v, self.cnt[k])
            if v > self.seen[eng].get(k, 0):
                need[k] = max(need.get(k, 0), v)
        for k, v in need.items():
            self.seen[eng][k] = v
        return list(need.items())

    def op(self, eng, fn, reads=(), writes=(), inc=True):
        xr = [r for r in reads if r.x]
        if xr:
            writes = list(writes) + [r for r in xr if r not in writes]
            reads = [r for r in reads if not r.x]
        waits = self._waits(eng, reads, writes)
        if inc:
            self.cnt[eng] += 1
            ev = (eng, self.cnt[eng])
        else:
            ev = (eng, self.cnt[eng] + 1)
        for r in reads:
            r.r.append(ev)
        for w in writes:
            w.w = ev
            w.r = []
        self.q[eng].append((waits, fn, (eng, 1) if inc else None))
        return ev

    def dma(self, eng, fn, semkey, reads=(), writes=()):
        waits = self._waits(eng, reads, writes)
        self.dcnt[semkey] = self.dcnt.get(semkey, 0) + 16
        ev = (semkey, self.dcnt[semkey])
        for r in reads:
            r.r.append(ev)
        for w in writes:
            w.w = ev
            w.r = []
        self.q[eng].append((waits, fn, (semkey, 16)))
        return ev

    def wait_all(self, eng, evs):
        need = {}
        for k, v in evs:
            if v > self.seen[eng].get(k, 0):
                need[k] = max(need.get(k, 0), v)
        for k, v in need.items():
            self.seen[eng][k] = v
        self.q[eng].append((list(need.items()), None, None))

    def build(self, nc, es):
        sems = {}
        for k in list(self.ENGS) + sorted(self.dcnt.keys()):
            sems[k] = es.enter_context(nc.semaphore("s_" + k))
        block = es.enter_context(nc.Block())
        q = self.q

        def run(eng_name):
            def body(e):
                for waits, fn, inc in q[eng_name]:
                    for k, v in waits:
                        e.wait_ge(sems[k], v)
                    if fn is not None:
                        ins = fn(e)
                        if inc is not None:
                            ins.then_inc(sems[inc[0]], inc[1])
            return body

        block.tensor(run("pe"))
        block.scalar(run("act"))
        block.vector(run("dve"))
        block.gpsimd(run("pool"))
        block.sync(run("sp"))


def build_nc():
    nc = bass.Bass("TRN2", target_bir_lowering=False)
    xT_d = nc.dram_tensor("xT", [NSEQ, TT, 128, 4096], F32, kind="ExternalInput")
    wqkv_d = nc.dram_tensor("w_qkv", [3, 128, 4096], F32, kind="ExternalInput")
    wubc_d = nc.dram_tensor("w_ubc", [8, 128, 3072], F32, kind="ExternalInput")
    wpj_d = nc.dram_tensor("w_pj", [8, 128, 4096], F32, kind="ExternalInput")
    wo_d = nc.dram_tensor("w_o", [2, 128, 4096], F32, kind="ExternalInput")
    wup_d = nc.dram_tensor("w_upg", [11, 128, 4096], F32, kind="ExternalInput")
    wdn_d = nc.dram_tensor("w_dn", [8, 128, 2816], F32, kind="ExternalInput")
    vec_d = nc.dram_tensor("vecs", [128, NV], F32, kind="ExternalInput")
    tab_d = nc.dram_tensor("tabs", [128, 2 * SEQ], F32, kind="ExternalInput")
    cm_d = nc.dram_tensor("cmat", [128, 512], F32, kind="ExternalInput")
    yT_d = nc.dram_tensor("yT", [NSEQ, TT, 128, 4096], F32, kind="ExternalOutput")
    x1_d = nc.dram_tensor("x1s", [NSEQ, TT, 128, 4096], F32)
    wdnb_d = nc.dram_tensor("wdn_bf16", [8, 128, 2816], BF16)

    es = ExitStack()
    S = Sched()
    with es:
        sb = lambda name, shape, dt: es.enter_context(nc.sbuf_tensor(name, shape, dt))
        A = sb("A", [128, 8 * SEQ], BF16)
        O = sb("O", [128, 8 * SEQ], BF16)
        R1 = sb("R1", [128, 8 * SEQ], BF16)
        M = sb("M", [128, 8 * SEQ], BF16)
        tabs = sb("tabs_sb", [128, 2 * SEQ], F32)
        ring = [sb(f"ring{i}", [128, 4096], BF16) for i in range(NSLOT)]
        T = [sb(f"T{i}", [128, 4608], F32) for i in range(2)]
        vecs = sb("vecs_sb", [128, NV], F32)
        cmat = sb("cmat_sb", [128, 512], BF16)
        epsb = sb("epsb", [128, 1], F32)
        PS = [es.enter_context(nc.psum_tensor(f"ps{i}", [128, 1024], F32)) for i in range(4)]

        A_f = A.bitcast(F32)
        O_f = O.bitcast(F32)
        R1_f = R1.bitcast(F32)
        M_f = M.bitcast(F32)
        T_b = [t.bitcast(BF16) for t in T]

        r_A = [Res(f"A{t}") for t in range(TT)]
        r_O = [[Res(f"O{h}_{t}") for t in range(TT)] for h in range(8)]
        r_R1 = [Res(f"R1_{c}") for c in range(8)]
        r_Mb = [Res(f"Mb{b}") for b in range(16)]
        r_ring = [Res(f"ring{i}") for i in range(NSLOT)]
        r_T = [[Res(f"T{i}_{b}") for b in range(9)] for i in range(2)]
        r_bank = [[Res(f"ps{i}_{h}", excl=True) for h in range(2)] for i in range(4)]
        r_consts = [Res("eps"), Res("vec"), Res("cm")]
        r_tabs = Res("tabs")
        r_x1d = [[Res(f"x1d{s}_{t}") for t in range(TT)] for s in range(NSEQ)]
        r_wdnb = [Res(f"wdnb{d}") for d in range(8)]

        ones1024 = cmat[:, 0:128]
        ones128 = cmat[:, 128:256]
        ones1 = cmat[:, 256:384]
        pmatT = cmat[:, 384:512]

        def vcol(i):
            return vecs[:, i:i + 1]

        bank_ap = lambda i, h: PS[i][:, h * 512:(h + 1) * 512]

        ring_state = {"n": 0}

        ring_pinned = set()

        def load_slot(src_ap, L, reads=(), nb=None):
            while True:
                s = ring_state["n"] % NSLOT
                ring_state["n"] += 1
                if s not in ring_pinned:
                    break
            if nb is None:
                nb = L // 1024 if L % 1024 == 0 else L // 704
            bl = L // nb
            S.dma("pool", lambda e, s=s: e.dma_start(
                out=ring[s][:, 0:L].rearrange("p (a b) -> p a b", b=bl),
                in_=src_ap.rearrange("p (a b) -> p a b", b=bl)), f"ring{s}", reads=list(reads), writes=[r_ring[s]])
            return s

        bank_state = {"n": 0}

        bank_reserved = set()

        def next_bank():
            while True:
                n = bank_state["n"] % 8
                bank_state["n"] += 1
                if (n // 2, n % 2) not in bank_reserved:
                    return n // 2, n % 2

        def mm_group(out_ap, out_res, pairs, reads, inc_last=True):
            n = len(pairs)
            for idx, (l, r) in enumerate(pairs):
                S.op("pe", lambda e, l=l, r=r, idx=idx: e.matmul(out_ap, lhsT=l, rhs=r, start=(idx == 0), stop=(idx == n - 1)),
                     reads=reads, writes=[out_res], inc=(inc_last and idx == n - 1))

        def rstd_from(ps_ap, ps_res, out_ap, out_res):
            S.op("act", lambda e: e.activation(out=out_ap, in_=ps_ap, func=AF.Ln, bias=epsb[:, 0:1]),
                 reads=[ps_res, *r_consts], writes=[out_res])
            S.op("act", lambda e: e.activation(out=out_ap, in_=out_ap, func=AF.Exp, scale=-0.5),
                 reads=[out_res], writes=[out_res])

        S.op("dve", lambda e: e.memset(epsb[:], EPS), writes=[r_consts[0]])
        S.dma("sp", lambda e: e.dma_start(out=vecs[:], in_=vec_d[:, :]), "ld_c0", writes=[r_consts[1]])
        S.dma("sp", lambda e: e.dma_start(out=tabs[:], in_=tab_d[:, :]), "ld_c1", writes=[r_tabs])
        S.dma("pool", lambda e: e.dma_start(out=cmat[:], in_=cm_d[:, :]), "ld_c2", writes=[r_consts[2]])
        Ctab = tabs[:, 0:SEQ]
        Stab = tabs[:, SEQ:2 * SEQ]

        out_evs = []
        qk_ctr = {"n": 0}

        qk_pend = {"p": None}

        def qk_pre(pairs, reads):
            k = qk_ctr["n"] % 2
            qk_ctr["n"] += 1
            base = k * 4
            bi, bh = next_bank()
            bank_reserved.add((bi, bh))
            ps_ap, ps_res = bank_ap(bi, bh), r_bank[bi][bh]
            mm_group(ps_ap, ps_res, pairs, reads)
            qb = M[:, base * 1024: base * 1024 + 512]
            sq = M[:, base * 1024 + 512: base * 1024 + 1024]
            S.op("act", lambda e: e.activation(out=qb, in_=ps_ap, func=AF.Copy), reads=[ps_res], writes=[r_Mb[base]])
            S.op("act", lambda e: e.activation(out=sq, in_=ps_ap, func=AF.Square), reads=[ps_res], writes=[r_Mb[base]])
            return base, bi, bh

        def qk_fin(st, gcol, gpcol, dest_ap, dest_res, tt):
            base, pi_, ph_ = st
            ps_ap, ps_res = bank_ap(pi_, ph_), r_bank[pi_][ph_]
            qb = M[:, base * 1024: base * 1024 + 512]
            sq = M[:, base * 1024 + 512: base * 1024 + 1024]
            rs = M_f[:, (base + 1) * 512:(base + 2) * 512]
            t1 = M_f[:, (base + 2) * 512:(base + 3) * 512]
            t2 = M_f[:, (base + 3) * 512:(base + 4) * 512]
            r_qb = r_sq = r_Mb[base]
            r_rs, r_t1, r_t2 = r_Mb[base + 1], r_Mb[base + 2], r_Mb[base + 3]
            bi, bh = next_bank()
            mm_group(bank_ap(bi, bh), r_bank[bi][bh], [(ones128, sq)], [r_sq, *r_consts])
            ci, ch = next_bank()
            mm_group(bank_ap(ci, ch), r_bank[ci][ch], [(pmatT, qb)], [r_qb, *r_consts])
            rstd_from(bank_ap(bi, bh), r_bank[bi][bh], rs, r_rs)
            sl = slice(tt * 512, (tt + 1) * 512)
            S.op("dve", lambda e: e.scalar_tensor_tensor(out=t1, in0=ps_ap, scalar=vcol(gcol), in1=Ctab[:, sl],
                                                         op0=ALU.mult, op1=ALU.mult),
                 reads=[ps_res, *r_consts, r_tabs], writes=[r_t1])
            S.op("dve", lambda e: e.scalar_tensor_tensor(out=t2, in0=bank_ap(ci, ch), scalar=vcol(gpcol), in1=Stab[:, sl],
                                                         op0=ALU.mult, op1=ALU.mult),
                 reads=[r_bank[ci][ch], *r_consts, r_tabs], writes=[r_t2])
            S.op("dve", lambda e: e.tensor_tensor(out=t1, in0=t1, in1=t2, op=ALU.add), reads=[r_t1, r_t2], writes=[r_t1])
            S.op("dve", lambda e: e.tensor_tensor(out=dest_ap, in0=t1, in1=rs, op=ALU.mult),
                 reads=[r_t1, r_rs], writes=[dest_res])
            bank_reserved.discard((pi_, ph_))

        def qk_item(pairs, reads, gcol, gpcol, dest_ap, dest_res, tt):
            st = qk_pre(pairs, reads)
            if qk_pend["p"] is not None:
                qk_fin(*qk_pend["p"])
            qk_pend["p"] = (st, gcol, gpcol, dest_ap, dest_res, tt)

        def qk_flush():
            if qk_pend["p"] is not None:
                qk_fin(*qk_pend["p"])
                qk_pend["p"] = None

        def norm_tile(xt_f, xt_res, g0, dest_ap_fn, dest_res, sq_b, sq_res, rs_ap, rs_res):
            S.op("act", lambda e: e.activation(out=sq_b, in_=xt_f, func=AF.Square), reads=xt_res, writes=sq_res)
            bi, bh = next_bank()
            mm_group(bank_ap(bi, bh), r_bank[bi][bh],
                     [(ones1024, sq_b[:, c * 512:(c + 1) * 512]) for c in range(8)], list(sq_res) + [*r_consts])
            rstd_from(bank_ap(bi, bh), r_bank[bi][bh], rs_ap, rs_res)
            for c in range(8):
                S.op("dve", lambda e, c=c: e.scalar_tensor_tensor(
                    out=dest_ap_fn(c), in0=xt_f[:, c * 512:(c + 1) * 512], scalar=vcol(g0 + c), in1=rs_ap,
                    op0=ALU.mult, op1=ALU.mult),
                    reads=list(xt_res) + [rs_res, *r_consts], writes=[dest_res])

        def emit_ssq(d, sbi, sbh, sqd):
            S.op("pe", lambda e: e.matmul(bank_ap(sbi, sbh), lhsT=ones1024, rhs=sqd[:, d * 512:(d + 1) * 512],
                                          start=(d == 0), stop=(d == 7)),
                 reads=[r_T[1][d // 2], *r_consts], writes=[r_bank[sbi][sbh]], inc=True)

        x_bufs = [(M_f[:, 0:4096], r_Mb[0:8]),
                  (O_f[:, 4096:8192], [r_O[h][t] for h in range(4, 8) for t in range(TT)]),
                  (R1_f[:, 4096:8192], r_R1[4:8]),
                  (O_f[:, 0:4096], [r_O[h][t] for h in range(4) for t in range(TT)])]

        def emit_x_loads(sq_, eng):
            for tt in range(TT):
                xt, xres = x_bufs[tt]
                S.dma(eng, lambda e, xt=xt, sq_=sq_, tt=tt: e.dma_start(out=xt, in_=xT_d[sq_, tt]), f"ld_x{tt}_{eng}",
                      reads=(list(x_bufs[tt - 1][1]) if (tt > 0 and eng == "pool") else []), writes=xres)

        for s in range(NSEQ):
            if s == 0:
                emit_x_loads(0, "sp")
            for tt in range(TT):
                xt, xres = x_bufs[tt]
                sqb = T_b[1][:, 0:4096]
                norm_tile(xt, xres, V_PRE,
                          lambda c, tt=tt: A[:, c * SEQ + tt * 512: c * SEQ + (tt + 1) * 512], r_A[tt],
                          sqb, r_T[1][0:4], T[1][:, 2560:3072], r_T[1][5])

            kT = lambda kvh: R1[:, kvh * SEQ:(kvh + 1) * SEQ]
            Vt = R1[:, 2 * SEQ:4 * SEQ]
            sl_kv = load_slot(wqkv_d[2], 4096)
            rg = ring[sl_kv]
            for tt in range(TT):
                for kvh in range(2):
                    qk_item([(rg[:, c * 512 + kvh * 128: c * 512 + (kvh + 1) * 128],
                              A[:, c * SEQ + tt * 512: c * SEQ + (tt + 1) * 512]) for c in range(8)],
                            [r_ring[sl_kv], r_A[tt]], V_KG, V_KGP,
                            kT(kvh)[:, tt * 512:(tt + 1) * 512], r_R1[kvh], tt)
            for kt2 in range(8):
                bi, bh = next_bank()
                for q in range(2):
                    kt = kt2 * 2 + q
                    mm_group(PS[bi][:, bh * 512 + q * 256: bh * 512 + (q + 1) * 256], r_bank[bi][bh],
                             [(A[:, c * SEQ + kt * 128: c * SEQ + (kt + 1) * 128],
                               rg[:, c * 512 + 256: c * 512 + 512]) for c in range(8)],
                             [r_ring[sl_kv], r_A[kt // 4]])
                S.op("act", lambda e, bi=bi, bh=bh, kt2=kt2: e.activation(
                    out=Vt[:, kt2 * 512:(kt2 + 1) * 512], in_=bank_ap(bi, bh), func=AF.Copy),
                    reads=[r_bank[bi][bh]], writes=[r_R1[2 + kt2 // 4]])

            def p_ap(j):
                return R1[:, (4 + j) * SEQ:(5 + j) * SEQ] if j < 4 else M[:, j * SEQ:(j + 1) * SEQ]

            def p_res(j):
                return [r_R1[4 + j]] if j < 4 else [r_Mb[2 * j], r_Mb[2 * j + 1]]

            S.op("dve", lambda e: e.memset(T[0][:, 0:1], 0.0), writes=[r_T[0][0]])
            S.op("dve", lambda e: e.memset(T[0][:, 2049:2050], 0.0), writes=[r_T[0][4]])
            c_wrow = T[0]
            c_yrow = T[0][:, 2560:4608]
            c_brow = T[1][:, 2560:4608]
            sl_q = None
            for j in range(8):
                if j % 4 == 0:
                    if sl_q is not None:
                        ring_pinned.discard(sl_q)
                    sl_q = load_slot(wqkv_d[j // 4], 4096)
                    ring_pinned.add(sl_q)
                rgq = ring[sl_q]
                hq = j % 4
                sl = load_slot(wubc_d[j], 3072)
                rg = ring[sl]
                for tt in range(TT):
                    qk_item([(rgq[:, c * 512 + hq * 128: c * 512 + (hq + 1) * 128],
                              A[:, c * SEQ + tt * 512: c * SEQ + (tt + 1) * 512]) for c in range(8)],
                            [r_ring[sl_q], r_A[tt]], V_QG, V_QGP,
                            O[:, j * SEQ + tt * 512: j * SEQ + (tt + 1) * 512], r_O[j][tt], tt)
                    bks = []
                    for k in (2, 0, 1):
                        bi, bh = next_bank()
                        mm_group(bank_ap(bi, bh), r_bank[bi][bh],
                                 [(rg[:, c * 384 + k * 128: c * 384 + (k + 1) * 128],
                                   A[:, c * SEQ + tt * 512: c * SEQ + (tt + 1) * 512]) for c in range(8)],
                                 [r_ring[sl], r_A[tt]])
                        bks.append((bi, bh))
                    (cbi, cbh), (ubi, ubh), (bbi, bbh) = bks
                    cs = T[1][:, (tt % 2) * 512:(tt % 2 + 1) * 512]
                    r_cs = r_T[1][tt % 2]
                    S.op("act", lambda e, cs=cs, cbi=cbi, cbh=cbh: e.activation(out=cs, in_=bank_ap(cbi, cbh), func=AF.Copy),
                         reads=[r_bank[cbi][cbh]], writes=[r_cs])
                    S.op("dve", lambda e, cs=cs, ubi=ubi, ubh=ubh, tt=tt: e.tensor_tensor(
                        out=c_wrow[:, 1 + tt * 512: 1 + (tt + 1) * 512], in0=bank_ap(ubi, ubh), in1=cs, op=ALU.mult),
                        reads=[r_bank[ubi][ubh], r_cs], writes=[r_T[0][tt], r_T[0][tt + 1]])
                    S.op("act", lambda e, bbi=bbi, bbh=bbh, tt=tt: e.activation(
                        out=c_brow[:, tt * 512:(tt + 1) * 512], in_=bank_ap(bbi, bbh), func=AF.Copy),
                        reads=[r_bank[bbi][bbh]], writes=[r_T[1][5 + tt]])
                rw = r_T[0][0:5]
                ry = r_T[0][5:9]
                S.op("dve", lambda e, j=j: e.tensor_scalar(out=c_yrow, in0=c_wrow[:, 0:2048], scalar1=vcol(V_MCW + j), scalar2=None,
                                                           op0=ALU.mult), reads=rw + [*r_consts], writes=ry)
                S.op("dve", lambda e, j=j: e.scalar_tensor_tensor(out=c_yrow, in0=c_wrow[:, 1:2049], scalar=vcol(V_MCW + 8 + j),
                                                                  in1=c_yrow, op0=ALU.mult, op1=ALU.add),
                     reads=rw + ry + [*r_consts], writes=ry)
                S.op("dve", lambda e, j=j: e.scalar_tensor_tensor(out=c_yrow, in0=c_wrow[:, 2:2050], scalar=vcol(V_MCW + 16 + j),
                                                                  in1=c_yrow, op0=ALU.mult, op1=ALU.add),
                     reads=rw + ry + [*r_consts], writes=ry)
                S.op("dve", lambda e, j=j: e.tensor_tensor(out=p_ap(j), in0=c_yrow, in1=c_brow, op=ALU.mult),
                     reads=ry + r_T[1][5:9], writes=p_res(j))
            qk_flush()
            ring_pinned.discard(sl_q)

            if s == 0:
                for d in range(8):
                    S.dma("pool", lambda e, d=d: e.dma_start(
                        out=wdnb_d[d].rearrange("p (a b) -> p a b", b=704),
                        in_=wdn_d[d].rearrange("p (a b) -> p a b", b=704)), f"cv_wdn{d}", reads=[r_O[7][3]], writes=[r_wdnb[d]])

            for hh in range(NH):
                kvh = hh // 4
                for m2 in range(2):
                    qts = (2 * m2, 2 * m2 + 1)
                    qaps = [O[:, hh * SEQ + qt * 512: hh * SEQ + (qt + 1) * 512] for qt in qts]
                    r_qs = [r_O[hh][qt] for qt in qts]

                    def emit_qk(st, kp):
                        for q in range(2):
                            kt = kp * 2 + q
                            S.op("pe", lambda e, st=st, q=q, kt=kt, kvh=kvh, qap=qaps[st]: e.matmul(
                                PS[st][:, q * 512:(q + 1) * 512], lhsT=kT(kvh)[:, kt * 128:(kt + 1) * 128], rhs=qap,
                                start=True, stop=True),
                                reads=[r_R1[kvh], r_qs[st]], writes=[r_bank[st][0], r_bank[st][1]], inc=(q == 1))

                    def pt_buf(st, kp):
                        b = st * 2 + (kp % 2)
                        return T_b[0][:, b * 1024:(b + 1) * 1024], r_T[0][b]

                    def pa_buf(st, kp):
                        b = 4 + st * 2 + (kp % 2)
                        return T_b[0][:, b * 1024:b * 1024 + 512], r_T[0][b]

                    def emit_exp(st, kp):
                        PT, r_PT = pt_buf(st, kp)
                        S.op("act", lambda e, PT=PT, st=st: e.activation(out=PT, in_=PS[st][:, :], func=AF.Exp, scale=SCALE),
                             reads=[r_bank[st][0], r_bank[st][1]], writes=[r_PT])
                        pa, r_pa = pa_buf(st, kp)
                        S.op("dve", lambda e, PT=PT, pa=pa: e.tensor_tensor(out=pa, in0=PT[:, 0:512], in1=PT[:, 512:1024],
                                                                           op=ALU.add), reads=[r_PT], writes=[r_pa])

                    def emit_pv(st, kp):
                        PT, r_PT = pt_buf(st, kp)
                        oacc = bank_ap(2, st)
                        for q in range(2):
                            kt = kp * 2 + q
                            S.op("pe", lambda e, q=q, kt=kt, oacc=oacc, kvh=kvh, PT=PT: e.matmul(
                                oacc, lhsT=Vt[:, kt * 256 + kvh * 128: kt * 256 + (kvh + 1) * 128],
                                rhs=PT[:, q * 512:(q + 1) * 512], start=(kt == 0), stop=(kt == 15)),
                                reads=[r_PT, r_R1[2 + kt // 8]], writes=[r_bank[2][st]], inc=(q == 1))

                    def emit_den(st, kp):
                        pa, r_pa = pa_buf(st, kp)
                        den = bank_ap(3, st)
                        S.op("pe", lambda e, pa=pa, den=den, kp=kp: e.matmul(den, lhsT=ones1, rhs=pa, start=(kp == 0), stop=(kp == 7)),
                             reads=[r_pa, *r_consts], writes=[r_bank[3][st]], inc=True)

                    for st in range(2):
                        emit_qk(st, 0)
                    for kp in range(8):
                        for st in range(2):
                            emit_exp(st, kp)
                            if kp + 1 < 8:
                                emit_qk(st, kp + 1)
                            emit_pv(st, kp)
                            if kp >= 1:
                                emit_den(st, kp - 1)
                    for st in range(2):
                        emit_den(st, 7)
                        rd = T[1][:, st * 512:(st + 1) * 512]
                        r_rd = r_T[1][st]
                        S.op("act", lambda e, rd=rd, st=st: e.activation(out=rd, in_=bank_ap(3, st), func=AF.Ln),
                             reads=[r_bank[3][st]], writes=[r_rd])
                        S.op("act", lambda e, rd=rd: e.activation(out=rd, in_=rd, func=AF.Exp, scale=-1.0),
                             reads=[r_rd], writes=[r_rd])
                        S.op("dve", lambda e, rd=rd, st=st, qap=qaps[st]: e.tensor_tensor(out=qap, in0=bank_ap(2, st), in1=rd,
                                                                                       op=ALU.mult),
                             reads=[r_bank[2][st], r_rd], writes=[r_qs[st]])

            def mg_ap(j, tt):
                if j < 4:
                    return R1[:, j * SEQ + tt * 512: j * SEQ + (tt + 1) * 512]
                return M[:, (j - 4) * SEQ + tt * 512:(j - 4) * SEQ + (tt + 1) * 512]

            def mg_res(j, tt):
                return [r_R1[j]] if j < 4 else [r_Mb[((j - 4) * 4 + tt) // 2]]

            def p_tile_res(c, tt):
                return [r_R1[4 + c]] if c < 4 else [r_Mb[(c * 4 + tt) // 2]]

            for j in range(8):
                sl = load_slot(wpj_d[j], 4096)
                rg = ring[sl]
                for tt in range(TT):
                    tsl = lambda c, tt=tt: slice(c * SEQ + tt * 512, c * SEQ + (tt + 1) * 512)
                    src_fns = (
                        (lambda c, tt=tt: O[:, tsl(c)], lambda c, tt=tt: [r_O[c][tt]]),
                        (lambda c, tt=tt: p_ap(c)[:, tt * 512:(tt + 1) * 512], lambda c, tt=tt: p_tile_res(c, tt)),
                        (lambda c, tt=tt: A[:, tsl(c)], lambda c, tt=tt: [r_A[tt]]),
                        (lambda c, tt=tt: A[:, tsl(c)], lambda c, tt=tt: [r_A[tt]]),
                    )
                    bk = []
                    for k, (ap_fn, rs_fn) in enumerate(src_fns):
                        bi, bh = next_bank()
                        rds = [r_ring[sl]]
                        for c in range(8):
                            for r_ in rs_fn(c):
                                if r_ not in rds:
                                    rds.append(r_)
                        mm_group(bank_ap(bi, bh), r_bank[bi][bh],
                                 [(rg[:, c * 512 + k * 128: c * 512 + (k + 1) * 128], ap_fn(c)) for c in range(8)], rds)
                        bk.append((bi, bh))
                    sa = T[tt % 2][:, 0:512]
                    sbg = T[tt % 2][:, 512:1024]
                    m1 = T[tt % 2][:, 1024:1536]
                    r_sa, r_sb, r_m1 = r_T[tt % 2][0], r_T[tt % 2][1], r_T[tt % 2][2]
                    S.op("act", lambda e, sa=sa, b=bk[2], j=j: e.activation(out=sa, in_=bank_ap(*b), func=AF.Sigmoid,
                                                                             bias=vcol(V_GBA + j)),
                         reads=[r_bank[bk[2][0]][bk[2][1]], *r_consts], writes=[r_sa])
                    S.op("act", lambda e, sbg=sbg, b=bk[3], j=j: e.activation(out=sbg, in_=bank_ap(*b), func=AF.Sigmoid,
                                                                               bias=vcol(V_GBB + j)),
                         reads=[r_bank[bk[3][0]][bk[3][1]], *r_consts], writes=[r_sb])
                    S.op("dve", lambda e, sa=sa, m1=m1, b=bk[0]: e.tensor_tensor(out=m1, in0=bank_ap(*b), in1=sa, op=ALU.mult),
                         reads=[r_bank[bk[0][0]][bk[0][1]], r_sa], writes=[r_m1])
                    S.op("dve", lambda e, sbg=sbg, b=bk[1]: e.tensor_tensor(out=sbg, in0=bank_ap(*b), in1=sbg, op=ALU.mult),
                         reads=[r_bank[bk[1][0]][bk[1][1]], r_sb], writes=[r_sb])
                    S.op("dve", lambda e, sbg=sbg, m1=m1, j=j, tt=tt: e.tensor_tensor(
                        out=mg_ap(j, tt), in0=m1, in1=sbg, op=ALU.add),
                        reads=[r_m1, r_sb], writes=mg_res(j, tt))

            for blk in range(2):
                S.dma("pool", lambda e, blk=blk: e.dma_start(
                    out=O[:, blk * 4096:(blk + 1) * 4096].rearrange("p (a b) -> p a b", b=1024),
                    in_=wo_d[blk].rearrange("p (a b) -> p a b", b=1024)), f"ld_wo{blk}",
                    writes=[r_O[h][t] for h in (2 * blk, 2 * blk + 1) for t in range(TT)])
            wo_res = lambda d: [r_O[h][t] for h in (2 * (d // 4), 2 * (d // 4) + 1) for t in range(TT)]
            out_state = {}

            def mo_tile(tt):
                if tt % 2 == 0:
                    return T[0][:, 0:4096], r_T[0][0:8]
                return O_f[:, 4096:8192], [r_O[h][t] for h in range(4, 8) for t in range(TT)]

            def mo_buf(tt, d):
                if tt % 2 == 0:
                    return T[0][:, d * 512:(d + 1) * 512], [r_T[0][d]]
                return (O_f[:, 4096 + d * 512: 4096 + (d + 1) * 512],
                        [r_O[4 + d // 2][(d % 2) * 2], r_O[4 + d // 2][(d % 2) * 2 + 1]])

            SQB = (4, 7, 8)

            def out_ssq(tt, d):
                xt2, xres, sbi, sbh = out_state[tt]
                b = SQB[d % 3]
                S.op("pe", lambda e: e.matmul(bank_ap(sbi, sbh), lhsT=ones1024, rhs=T_b[1][:, b * 1024: b * 1024 + 512],
                                              start=(d == 0), stop=(d == 7)),
                     reads=[r_T[1][b], *r_consts], writes=[r_bank[sbi][sbh]], inc=True)

            def out_A(tt, d0, d1):
                hb = tt % 2
                xt2 = R1_f[:, 4096:8192] if hb == 0 else M_f[:, 4096:8192]
                xres = r_R1[4:8] if hb == 0 else r_Mb[8:16]
                if d0 == 0:
                    sbi, sbh = next_bank()
                    bank_reserved.add((sbi, sbh))
                    out_state[tt] = (xt2, xres, sbi, sbh)
                if d1 == 8:
                    S.dma("sp", lambda e, xt2=xt2, s=s, tt=tt: e.dma_start(out=xt2, in_=xT_d[s, tt]), f"ld_y{hb}", writes=xres)
                for d in range(d0, d1):
                    bi, bh = next_bank()
                    mm_group(bank_ap(bi, bh), r_bank[bi][bh],
                             [(O[:, (d // 4) * 4096 + c * 512 + (d % 4) * 128: (d // 4) * 4096 + c * 512 + (d % 4 + 1) * 128],
                               mg_ap(c, tt)) for c in range(8)],
                             wo_res(d) + [r_ for c in range(8) for r_ in mg_res(c, tt)])
                    if d >= 1:
                        out_ssq(tt, d - 1)
                    mo_ap, mo_res = mo_buf(tt, d)
                    S.op("act", lambda e, mo_ap=mo_ap, bi=bi, bh=bh, d=d: e.activation(
                        out=mo_ap, in_=bank_ap(bi, bh), func=AF.Copy, scale=vcol(V_POST + d)),
                        reads=[r_bank[bi][bh], *r_consts], writes=mo_res)
                    b = SQB[d % 3]
                    S.op("act", lambda e, b=b, bi=bi, bh=bh: e.activation(out=T_b[1][:, b * 1024: b * 1024 + 512],
                                                                          in_=bank_ap(bi, bh), func=AF.Square),
                         reads=[r_bank[bi][bh]], writes=[r_T[1][b]])
                if d1 == 8:
                    out_ssq(tt, 7)

            def out_B1(tt):
                xt2, xres, sbi, sbh = out_state[tt]
                hb = tt % 2
                rs = T[1][:, 2560:3072]
                rstd_from(bank_ap(sbi, sbh), r_bank[sbi][sbh], rs, r_T[1][5])
                bank_reserved.discard((sbi, sbh))
                mt, mt_res = mo_tile(tt)
                mt3 = mt.rearrange("p (c n) -> p c n", n=512)
                rs3 = rs.unsqueeze(1).to_broadcast([128, 8, 512])
                S.op("dve", lambda e, mt3=mt3, rs3=rs3: e.tensor_tensor(out=mt3, in0=mt3, in1=rs3, op=ALU.mult),
                     reads=mt_res + [r_T[1][5]], writes=mt_res)
                S.op("dve", lambda e, mt=mt, xt2=xt2: e.tensor_tensor(out=xt2, in0=mt, in1=xt2, op=ALU.add),
                     reads=mt_res + xres, writes=xres)
                S.dma("sp", lambda e, xt2=xt2, s=s, tt=tt: e.dma_start(out=x1_d[s, tt], in_=xt2), f"st_x1{hb}",
                      reads=xres, writes=[r_x1d[s][tt]])

            def out_C1(tt):
                xt2, xres, sbi, sbh = out_state[tt]
                S.op("act", lambda e, xt2=xt2: e.activation(out=T_b[1][:, 0:4096], in_=xt2, func=AF.Square),
                     reads=xres, writes=r_T[1][0:4])

            def out_C2(tt):
                bi, bh = next_bank()
                bank_reserved.add((bi, bh))
                mm_group(bank_ap(bi, bh), r_bank[bi][bh],
                         [(ones1024, T_b[1][:, c * 512:(c + 1) * 512]) for c in range(8)], r_T[1][0:4] + [*r_consts])
                out_state[("c", tt)] = (bi, bh)

            def out_C3(tt):
                xt2, xres, sbi, sbh = out_state[tt]
                bi, bh = out_state[("c", tt)]
                rs2 = T[1][:, 3072:3584]
                rstd_from(bank_ap(bi, bh), r_bank[bi][bh], rs2, r_T[1][6])
                bank_reserved.discard((bi, bh))
                for c in range(8):
                    S.op("dve", lambda e, c=c, xt2=xt2, tt=tt: e.scalar_tensor_tensor(
                        out=A[:, c * SEQ + tt * 512: c * SEQ + (tt + 1) * 512], in0=xt2[:, c * 512:(c + 1) * 512],
                        scalar=vcol(V_FPRE + c), in1=rs2, op0=ALU.mult, op1=ALU.mult),
                        reads=list(xres) + [r_T[1][6], *r_consts], writes=[r_A[tt]])

            out_A(0, 0, 8)
            out_B1(0)
            out_A(1, 0, 8)
            out_C1(0)
            out_B1(1)
            for tt in range(2, TT):
                out_A(tt, 0, 4)
                out_C2(tt - 2)
                out_C3(tt - 2)
                out_A(tt, 4, 8)
                out_C1(tt - 1)
                out_B1(tt)
            out_C2(TT - 2)
            out_C3(TT - 2)
            out_C1(TT - 1)
            out_C2(TT - 1)
            out_C3(TT - 1)

            hid_buf = lambda j: (O, R1, M)[j // 8]

            def hid_res(j, tt):
                if j < 8:
                    return [r_O[j][tt]]
                if j < 16:
                    return [r_R1[j % 8]]
                return [r_Mb[((j % 8) * 4 + tt) // 2]]

            for i in range(2):
                S.op("dve", lambda e, i=i: e.memset(T[i][:, 0:1], 0.0), writes=[r_T[i][0]])
                S.op("dve", lambda e, i=i: e.memset(T[i][:, 2049:2050], 0.0), writes=[r_T[i][4]])
            for g in range(11):
                sl = load_slot(wup_d[g], 4096)
                rg = ring[sl]
                for jj in range(2):
                    j = 2 * g + jj
                    ti = j % 2
                    arow = T[ti]
                    yrow = T[ti][:, 2560:4608]
                    ra = r_T[ti][0:5]
                    ry = r_T[ti][5:9]
                    for tt in range(TT):
                        bi, bh = next_bank()
                        mm_group(bank_ap(bi, bh), r_bank[bi][bh],
                                 [(rg[:, c * 512 + jj * 128: c * 512 + (jj + 1) * 128],
                                   A[:, c * SEQ + tt * 512: c * SEQ + (tt + 1) * 512]) for c in range(8)],
                                 [r_ring[sl], r_A[tt]])
                        S.op("act", lambda e, bi=bi, bh=bh, tt=tt, arow=arow: e.activation(
                            out=arow[:, 1 + tt * 512: 1 + (tt + 1) * 512], in_=bank_ap(bi, bh), func=AF.Copy),
                            reads=[r_bank[bi][bh]], writes=[r_T[ti][tt], r_T[ti][tt + 1]])
                    bb = []
                    for tt in range(TT):
                        bi, bh = next_bank()
                        mm_group(bank_ap(bi, bh), r_bank[bi][bh],
                                 [(rg[:, c * 512 + 256 + jj * 128: c * 512 + 256 + (jj + 1) * 128],
                                   A[:, c * SEQ + tt * 512: c * SEQ + (tt + 1) * 512]) for c in range(8)],
                                 [r_ring[sl], r_A[tt]])
                        bb.append((bi, bh))
                    S.op("dve", lambda e, j=j, arow=arow, yrow=yrow: e.tensor_scalar(
                        out=yrow, in0=arow[:, 0:2048], scalar1=vcol(V_FCW + j), scalar2=None, op0=ALU.mult),
                        reads=ra + [*r_consts], writes=ry)
                    S.op("dve", lambda e, j=j, arow=arow, yrow=yrow: e.scalar_tensor_tensor(
                        out=yrow, in0=arow[:, 1:2049], scalar=vcol(V_FCW + NJ + j), in1=yrow, op0=ALU.mult, op1=ALU.add),
                        reads=ra + ry + [*r_consts], writes=ry)
                    S.op("dve", lambda e, j=j, arow=arow, yrow=yrow: e.scalar_tensor_tensor(
                        out=yrow, in0=arow[:, 2:2050], scalar=vcol(V_FCW + 2 * NJ + j), in1=yrow, op0=ALU.mult, op1=ALU.add),
                        reads=ra + ry + [*r_consts], writes=ry)
                    S.op("act", lambda e, yrow=yrow: e.activation(out=yrow, in_=yrow, func=AF.Gelu_apprx_tanh),
                         reads=ry, writes=ry)
                    hb_ = hid_buf(j)
                    for tt in range(TT):
                        bi, bh = bb[tt]
                        S.op("dve", lambda e, bi=bi, bh=bh, tt=tt, yrow=yrow, hb_=hb_, j=j: e.tensor_tensor(
                            out=hb_[:, (j % 8) * SEQ + tt * 512:(j % 8) * SEQ + (tt + 1) * 512],
                            in0=bank_ap(bi, bh), in1=yrow[:, tt * 512:(tt + 1) * 512], op=ALU.mult),
                            reads=[r_bank[bi][bh], r_T[ti][5 + tt]], writes=hid_res(j, tt))

            def o2_buf(tt, d):
                if tt % 2 == 0:
                    return T[0][:, d * 512:(d + 1) * 512], [r_T[0][d]]
                if d < 4:
                    return M_f[:, (12 + d) * 512:(13 + d) * 512], [r_Mb[12 + d]]
                b = 4 if d == 7 else 6 + (d - 4)
                return T[1][:, b * 512:(b + 1) * 512], [r_T[1][b]]

            r_x1h = [Res(f"x1h{s}_0"), Res(f"x1h{s}_1")]
            for tt in range(TT):
                hb = tt % 2
                x1t = A_f[:, hb * 4096:(hb + 1) * 4096]
                x1res = [r_x1h[hb]]
                S.dma("sp", lambda e, x1t=x1t, s=s, tt=tt: e.dma_start(out=x1t, in_=x1_d[s, tt]), f"ld_x1{hb}",
                      reads=[r_x1d[s][tt]], writes=x1res + (r_A if tt < 2 else []))
                sqd = T_b[1][:, 0:4096]
                sbi, sbh = next_bank()
                for d in range(8):
                    sl = load_slot(wdnb_d[d], 2816, reads=[r_wdnb[d]], nb=1)
                    rg = ring[sl]
                    bi, bh = next_bank()
                    if (bi, bh) == (sbi, sbh):
                        bi, bh = next_bank()
                    rds = [r_ring[sl]]
                    for j in range(NJ):
                        for r_ in hid_res(j, tt):
                            if r_ not in rds:
                                rds.append(r_)
                    mm_group(bank_ap(bi, bh), r_bank[bi][bh],
                             [(rg[:, j * 128:(j + 1) * 128],
                               hid_buf(j)[:, (j % 8) * SEQ + tt * 512:(j % 8) * SEQ + (tt + 1) * 512]) for j in range(NJ)], rds)
                    if d >= 1:
                        emit_ssq(d - 1, sbi, sbh, sqd)
                    o2_ap, o2_res = o2_buf(tt, d)
                    S.op("act", lambda e, o2_ap=o2_ap, bi=bi, bh=bh: e.activation(out=o2_ap, in_=bank_ap(bi, bh), func=AF.Copy),
                         reads=[r_bank[bi][bh]], writes=o2_res)
                    S.op("act", lambda e, d=d, bi=bi, bh=bh: e.activation(out=sqd[:, d * 512:(d + 1) * 512], in_=bank_ap(bi, bh),
                                                                          func=AF.Square),
                         reads=[r_bank[bi][bh]], writes=[r_T[1][d // 2]])
                emit_ssq(7, sbi, sbh, sqd)
                if tt == TT - 1 and s + 1 < NSEQ:
                    emit_x_loads(s + 1, "pool")
                rs = T[1][:, 2560:3072]
                rstd_from(bank_ap(sbi, sbh), r_bank[sbi][sbh], rs, r_T[1][5])
                for d in range(8):
                    o2_ap, o2_res = o2_buf(tt, d)
                    S.op("dve", lambda e, d=d, o2_ap=o2_ap: e.scalar_tensor_tensor(
                        out=o2_ap, in0=o2_ap, scalar=vcol(V_FPOST + d), in1=rs,
                        op0=ALU.mult, op1=ALU.mult), reads=o2_res + [r_T[1][5], *r_consts], writes=o2_res)
                    S.op("dve", lambda e, d=d, x1t=x1t, o2_ap=o2_ap: e.tensor_tensor(
                        out=x1t[:, d * 512:(d + 1) * 512], in0=o2_ap, in1=x1t[:, d * 512:(d + 1) * 512],
                        op=ALU.add), reads=o2_res + x1res, writes=x1res)
                out_evs.append(S.dma("sp", lambda e, x1t=x1t, s=s, tt=tt: e.dma_start(out=yT_d[s, tt], in_=x1t),
                                     f"st_y{hb}", reads=x1res + r_A))

        S.wait_all("sp", out_evs)
        S.build(nc, es)
    return nc


def _host_layout(inp):
    f = lambda a: np.ascontiguousarray(np.asarray(a, dtype=np.float32))
    w_in = f(inp["w_in"])[0]
    w_ap = f(inp["w_attn_proj"])[0]
    w_cp = f(inp["w_conv_proj"])[0]
    w_out = f(inp["w_out"])[0]
    w_up = f(inp["w_up"])[0]
    w_down = f(inp["w_down"])[0]
    sh = {}
    sh["w_qkv"] = f(w_in[:, 0:1536].reshape(8, 128, 3, 512).transpose(2, 1, 0, 3).reshape(3, 128, 4096))
    sh["w_ubc"] = f(w_in[:, 1536:4608].reshape(8, 128, 3, 8, 128).transpose(3, 1, 0, 2, 4).reshape(8, 128, 3072))
    W4 = np.stack([w_ap, w_cp, w_in[:, 4608:5632], w_in[:, 5632:6656]], axis=0)
    sh["w_pj"] = f(W4.reshape(4, 8, 128, 8, 128).transpose(3, 2, 1, 0, 4).reshape(8, 128, 4096))
    sh["w_o"] = f(w_out.reshape(8, 128, 2, 512).transpose(2, 1, 0, 3).reshape(2, 128, 4096))
    sh["w_upg"] = f(w_up.reshape(8, 128, 2, 11, 2, 128).transpose(3, 1, 0, 2, 4, 5).reshape(11, 128, 4096))
    sh["w_dn"] = f(w_down.reshape(22, 128, 8, 128).transpose(2, 1, 0, 3).reshape(8, 128, 2816))
    vec = np.zeros((128, NV), np.float32)
    col8 = lambda v: f(v).reshape(8, 128).T
    vec[:, V_PRE:V_PRE + 8] = col8(inp["mix_pre_g"][0])
    vec[:, V_POST:V_POST + 8] = col8(inp["mix_post_g"][0])
    vec[:, V_FPRE:V_FPRE + 8] = col8(inp["ffn_pre_g"][0])
    vec[:, V_FPOST:V_FPOST + 8] = col8(inp["ffn_post_g"][0])
    gb = f(inp["gate_b"])[0]
    vec[:, V_GBA:V_GBA + 8] = col8(gb[:D])
    vec[:, V_GBB:V_GBB + 8] = col8(gb[D:])
    perm = np.arange(128)
    perm = np.where((perm % 64) < 32, perm + 32, perm - 32)
    qg = f(inp["q_norm_g"])[0]
    kg = f(inp["k_norm_g"])[0]
    vec[:, V_QG] = qg
    vec[:, V_QGP] = qg[perm]
    vec[:, V_KG] = kg
    vec[:, V_KGP] = kg[perm]
    mcw = f(inp["mix_conv_w"])[0]
    for k in range(3):
        vec[:, V_MCW + 8 * k:V_MCW + 8 * (k + 1)] = mcw[k].reshape(8, 128).T
    fcw = f(inp["ffn_conv_w"])[0]
    for k in range(3):
        vec[:, V_FCW + NJ * k:V_FCW + NJ * (k + 1)] = fcw[k].reshape(NJ, 128).T
    sh["vecs"] = vec
    t = np.arange(SEQ)
    row_pos = (t // 64).astype(np.float64)
    col_pos = (t % 64).astype(np.float64)
    freqs = 10000.0 ** (-(np.arange(32, dtype=np.float64) / 32.0))
    tabs = np.zeros((128, 2 * SEQ), np.float32)
    for d in range(128):
        pos = row_pos if d < 64 else col_pos
        ang = pos * freqs[d % 32]
        tabs[d, 0:SEQ] = np.cos(ang)
        tabs[d, SEQ:] = np.sin(ang)
    sh["tabs"] = tabs
    cm = np.zeros((128, 512), np.float32)
    cm[:, 0:128] = 1.0 / 1024.0
    cm[:, 128:256] = 1.0 / 128.0
    cm[:, 256:384] = 1.0
    for m in range(128):
        if (m % 64) < 32:
            cm[m + 32, 384 + m] = -1.0
        else:
            cm[m - 32, 384 + m] = 1.0
    sh["cmat"] = cm
    return sh


_NC_CACHE = {}


def kernel(**inputs):
    x = np.asarray(inputs["x"], dtype=np.float32)
    shared = _host_layout(inputs)
    in_maps = []
    for core in range(NCORES):
        xs = x[core * NSEQ:(core + 1) * NSEQ]
        xT = np.ascontiguousarray(xs.reshape(NSEQ, TT, 512, 8, 128).transpose(0, 1, 4, 3, 2)).reshape(NSEQ, TT, 128, 4096)
        m = dict(shared)
        m["xT"] = xT
        in_maps.append(m)
    if "nc" not in _NC_CACHE:
        _NC_CACHE["nc"] = build_nc()
    nc = _NC_CACHE["nc"]
    res = run_bass_kernel_spmd(nc, in_maps, core_ids=list(range(NCORES)))
    outs = []
    for core in range(NCORES):
        yT = np.asarray(res.results[core]["yT"]).reshape(NSEQ, TT, 128, 8, 512)
        outs.append(yT.transpose(0, 1, 4, 3, 2).reshape(NSEQ, SEQ, D))
    return np.ascontiguousarray(np.concatenate(outs, axis=0).astype(np.float32))
```

```python
import numpy as np
from contextlib import ExitStack
import concourse.bass as bass
import concourse.mybir as mybir
from concourse.bass_utils import run_bass_kernel_spmd

F32 = mybir.dt.float32
BF16 = mybir.dt.bfloat16
AF = mybir.ActivationFunctionType
ALU = mybir.AluOpType

NCORES = 8
SEQ = 2048
D = 1024
NSEQ = 2
TT = 4
HD = 128
NH = 8
DFF = 2816
NJ = 22
EPS = 1e-6
SCALE = HD ** -0.5
NSLOT = 3

V_PRE, V_POST, V_FPRE, V_FPOST, V_GBA, V_GBB = 0, 8, 16, 24, 32, 40
V_QG, V_QGP, V_KG, V_KGP = 48, 49, 50, 51
V_MCW = 52
V_FCW = 76
NV = 142


class Res:
    __slots__ = ("name", "w", "r", "x")

    def __init__(self, name, excl=False):
        self.name = name
        self.w = None
        self.r = []
        self.x = excl


class Sched:
    ENGS = ("pe", "act", "dve", "pool", "sp")

    def __init__(self):
        self.q = {e: [] for e in self.ENGS}
        self.cnt = {e: 0 for e in self.ENGS}
        self.seen = {e: {} for e in self.ENGS}
        self.dcnt = {}

    def _waits(self, eng, reads, writes):
        evs = []
        for r in reads:
            if r.w is not None:
                evs.append(r.w)
        same_ok = (eng == "pe")
        for w in writes:
            if w.w is not None and (w.w[0] != eng or not same_ok):
                evs.append(w.w)
            for ev in w.r:
                if ev[0] != eng or not same_ok:
                    evs.append(ev)
        need = {}
        for k, v in evs:
            if k in self.cnt:
                assert v <= self.cnt[k], (eng, k, v, self.cnt[k])
            if v > self.seen[eng].get(k, 0):
                need[k] = max(need.get(k, 0), v)
        for k, v in need.items():
            self.seen[eng][k] = v
        return list(need.items())

    def op(self, eng, fn, reads=(), writes=(), inc=True):
        xr = [r for r in reads if r.x]
        if xr:
            writes = list(writes) + [r for r in xr if r not in writes]
            reads = [r for r in reads if not r.x]
        waits = self._waits(eng, reads, writes)
        if inc:
            self.cnt[eng] += 1
            ev = (eng, self.cnt[eng])
        else:
            ev = (eng, self.cnt[eng] + 1)
        for r in reads:
            r.r.append(ev)
        for w in writes:
            w.w = ev
            w.r = []
        self.q[eng].append((waits, fn, (eng, 1) if inc else None))
        return ev

    def dma(self, eng, fn, semkey, reads=(), writes=()):
        waits = self._waits(eng, reads, writes)
        self.dcnt[semkey] = self.dcnt.get(semkey, 0) + 16
        ev = (semkey, self.dcnt[semkey])
        for r in reads:
            r.r.append(ev)
        for w in writes:
            w.w = ev
            w.r = []
        self.q[eng].append((waits, fn, (semkey, 16)))
        return ev

    def wait_all(self, eng, evs):
        need = {}
        for k, v in evs:
            if v > self.seen[eng].get(k, 0):
                need[k] = max(need.get(k, 0), v)
        for k, v in need.items():
            self.seen[eng][k] = v
        self.q[eng].append((list(need.items()), None, None))

    def build(self, nc, es):
        sems = {}
        for k in list(self.ENGS) + sorted(self.dcnt.keys()):
            sems[k] = es.enter_context(nc.semaphore("s_" + k))
        block = es.enter_context(nc.Block())
        q = self.q

        def run(eng_name):
            def body(e):
                for waits, fn, inc in q[eng_name]:
                    for k, v in waits:
                        e.wait_ge(sems[k], v)
                    if fn is not None:
                        ins = fn(e)
                        if inc is not None:
                            ins.then_inc(sems[inc[0]], inc[1])
            return body

        block.tensor(run("pe"))
        block.scalar(run("act"))
        block.vector(run("dve"))
        block.gpsimd(run("pool"))
        block.sync(run("sp"))


def build_nc():
    nc = bass.Bass("TRN2", target_bir_lowering=False)
    xT_d = nc.dram_tensor("xT", [NSEQ, TT, 128, 4096], F32, kind="ExternalInput")
    wqkv_d = nc.dram_tensor("w_qkv", [3, 128, 4096], F32, kind="ExternalInput")
    wubc_d = nc.dram_tensor("w_ubc", [8, 128, 3072], F32, kind="ExternalInput")
    wpj_d = nc.dram_tensor("w_pj", [8, 128, 4096], F32, kind="ExternalInput")
    wo_d = nc.dram_tensor("w_o", [2, 128, 4096], F32, kind="ExternalInput")
    wup_d = nc.dram_tensor("w_upg", [11, 128, 4096], F32, kind="ExternalInput")
    wdn_d = nc.dram_tensor("w_dn", [8, 128, 2816], F32, kind="ExternalInput")
    vec_d = nc.dram_tensor("vecs", [128, NV], F32, kind="ExternalInput")
    tab_d = nc.dram_tensor("tabs", [128, 2 * SEQ], F32, kind="ExternalInput")
    cm_d = nc.dram_tensor("cmat", [128, 512], F32, kind="ExternalInput")
    yT_d = nc.dram_tensor("yT", [NSEQ, TT, 128, 4096], F32, kind="ExternalOutput")
    x1_d = nc.dram_tensor("x1s", [NSEQ, TT, 128, 4096], F32)
    wdnb_d = nc.dram_tensor("wdn_bf16", [8, 128, 2816], BF16)

    es = ExitStack()
    S = Sched()
    with es:
        sb = lambda name, shape, dt: es.enter_context(nc.sbuf_tensor(name, shape, dt))
        A = sb("A", [128, 8 * SEQ], BF16)
        O = sb("O", [128, 8 * SEQ], BF16)
        R1 = sb("R1", [128, 8 * SEQ], BF16)
        M = sb("M", [128, 8 * SEQ], BF16)
        tabs = sb("tabs_sb", [128, 2 * SEQ], F32)
        ring = [sb(f"ring{i}", [128, 4096], BF16) for i in range(NSLOT)]
        T = [sb(f"T{i}", [128, 4608], F32) for i in range(2)]
        vecs = sb("vecs_sb", [128, NV], F32)
        cmat = sb("cmat_sb", [128, 512], BF16)
        epsb = sb("epsb", [128, 1], F32)
        PS = [es.enter_context(nc.psum_tensor(f"ps{i}", [128, 1024], F32)) for i in range(4)]

        A_f = A.bitcast(F32)
        O_f = O.bitcast(F32)
        R1_f = R1.bitcast(F32)
        M_f = M.bitcast(F32)
        T_b = [t.bitcast(BF16) for t in T]

        r_A = [Res(f"A{t}") for t in range(TT)]
        r_O = [[Res(f"O{h}_{t}") for t in range(TT)] for h in range(8)]
        r_R1 = [Res(f"R1_{c}") for c in range(8)]
        r_Mb = [Res(f"Mb{b}") for b in range(16)]
        r_ring = [Res(f"ring{i}") for i in range(NSLOT)]
        r_T = [[Res(f"T{i}_{b}") for b in range(9)] for i in range(2)]
        r_bank = [[Res(f"ps{i}_{h}", excl=True) for h in range(2)] for i in range(4)]
        r_consts = [Res("eps"), Res("vec"), Res("cm")]
        r_tabs = Res("tabs")
        r_x1d = [[Res(f"x1d{s}_{t}") for t in range(TT)] for s in range(NSEQ)]
        r_wdnb = [Res(f"wdnb{d}") for d in range(8)]

        ones1024 = cmat[:, 0:128]
        ones128 = cmat[:, 128:256]
        ones1 = cmat[:, 256:384]
        pmatT = cmat[:, 384:512]

        def vcol(i):
            return vecs[:, i:i + 1]

        bank_ap = lambda i, h: PS[i][:, h * 512:(h + 1) * 512]

        ring_state = {"n": 0}

        ring_pinned = set()

        def load_slot(src_ap, L, reads=(), nb=None):
            while True:
                s = ring_state["n"] % NSLOT
                ring_state["n"] += 1
                if s not in ring_pinned:
                    break
            if nb is None:
                nb = L // 1024 if L % 1024 == 0 else L // 704
            bl = L // nb
            S.dma("pool", lambda e, s=s: e.dma_start(
                out=ring[s][:, 0:L].rearrange("p (a b) -> p a b", b=bl),
                in_=src_ap.rearrange("p (a b) -> p a b", b=bl)), f"ring{s}", reads=list(reads), writes=[r_ring[s]])
            return s

        bank_state = {"n": 0}

        bank_reserved = set()

        def next_bank():
            while True:
                n = bank_state["n"] % 8
                bank_state["n"] += 1
                if (n // 2, n % 2) not in bank_reserved:
                    return n // 2, n % 2

        def mm_group(out_ap, out_res, pairs, reads, inc_last=True):
            n = len(pairs)
            for idx, (l, r) in enumerate(pairs):
                S.op("pe", lambda e, l=l, r=r, idx=idx: e.matmul(out_ap, lhsT=l, rhs=r, start=(idx == 0), stop=(idx == n - 1)),
                     reads=reads, writes=[out_res], inc=(inc_last and idx == n - 1))

        def rstd_from(ps_ap, ps_res, out_ap, out_res):
            S.op("act", lambda e: e.activation(out=out_ap, in_=ps_ap, func=AF.Ln, bias=epsb[:, 0:1]),
                 reads=[ps_res, *r_consts], writes=[out_res])
            S.op("act", lambda e: e.activation(out=out_ap, in_=out_ap, func=AF.Exp, scale=-0.5),
                 reads=[out_res], writes=[out_res])

        S.op("dve", lambda e: e.memset(epsb[:], EPS), writes=[r_consts[0]])
        S.dma("sp", lambda e: e.dma_start(out=vecs[:], in_=vec_d[:, :]), "ld_c0", writes=[r_consts[1]])
        S.dma("sp", lambda e: e.dma_start(out=tabs[:], in_=tab_d[:, :]), "ld_c1", writes=[r_tabs])
        S.dma("pool", lambda e: e.dma_start(out=cmat[:], in_=cm_d[:, :]), "ld_c2", writes=[r_consts[2]])
        Ctab = tabs[:, 0:SEQ]
        Stab = tabs[:, SEQ:2 * SEQ]

        out_evs = []
        qk_ctr = {"n": 0}

        qk_pend = {"p": None}

        def qk_pre(pairs, reads):
            k = qk_ctr["n"] % 2
            qk_ctr["n"] += 1
            base = k * 4
            bi, bh = next_bank()
            bank_reserved.add((bi, bh))
            ps_ap, ps_res = bank_ap(bi, bh), r_bank[bi][bh]
            mm_group(ps_ap, ps_res, pairs, reads)
            qb = M[:, base * 1024: base * 1024 + 512]
            sq = M[:, base * 1024 + 512: base * 1024 + 1024]
            S.op("act", lambda e: e.activation(out=qb, in_=ps_ap, func=AF.Copy), reads=[ps_res], writes=[r_Mb[base]])
            S.op("act", lambda e: e.activation(out=sq, in_=ps_ap, func=AF.Square), reads=[ps_res], writes=[r_Mb[base]])
            return base, bi, bh

        def qk_fin(st, gcol, gpcol, dest_ap, dest_res, tt):
            base, pi_, ph_ = st
            ps_ap, ps_res = bank_ap(pi_, ph_), r_bank[pi_][ph_]
            qb = M[:, base * 1024: base * 1024 + 512]
            sq = M[:, base * 1024 + 512: base * 1024 + 1024]
            rs = M_f[:, (base + 1) * 512:(base + 2) * 512]
            t1 = M_f[:, (base + 2) * 512:(base + 3) * 512]
            t2 = M_f[:, (base + 3) * 512:(base + 4) * 512]
            r_qb = r_sq = r_Mb[base]
            r_rs, r_t1, r_t2 = r_Mb[base + 1], r_Mb[base + 2], r_Mb[base + 3]
            bi, bh = next_bank()
            mm_group(bank_ap(bi, bh), r_bank[bi][bh], [(ones128, sq)], [r_sq, *r_consts])
            ci, ch = next_bank()
            mm_group(bank_ap(ci, ch), r_bank[ci][ch], [(pmatT, qb)], [r_qb, *r_consts])
            rstd_from(bank_ap(bi, bh), r_bank[bi][bh], rs, r_rs)
            sl = slice(tt * 512, (tt + 1) * 512)
            S.op("dve", lambda e: e.scalar_tensor_tensor(out=t1, in0=ps_ap, scalar=vcol(gcol), in1=Ctab[:, sl],
                                                         op0=ALU.mult, op1=ALU.mult),
                 reads=[ps_res, *r_consts, r_tabs], writes=[r_t1])
            S.op("dve", lambda e: e.scalar_tensor_tensor(out=t2, in0=bank_ap(ci, ch), scalar=vcol(gpcol), in1=Stab[:, sl],
                                                         op0=ALU.mult, op1=ALU.mult),
                 reads=[r_bank[ci][ch], *r_consts, r_tabs], writes=[r_t2])
            S.op("dve", lambda e: e.tensor_tensor(out=t1, in0=t1, in1=t2, op=ALU.add), reads=[r_t1, r_t2], writes=[r_t1])
            S.op("dve", lambda e: e.tensor_tensor(out=dest_ap, in0=t1, in1=rs, op=ALU.mult),
                 reads=[r_t1, r_rs], writes=[dest_res])
            bank_reserved.discard((pi_, ph_))

        def qk_item(pairs, reads, gcol, gpcol, dest_ap, dest_res, tt):
            st = qk_pre(pairs, reads)
            if qk_pend["p"] is not None:
                qk_fin(*qk_pend["p"])
            qk_pend["p"] = (st, gcol, gpcol, dest_ap, dest_res, tt)

        def qk_flush():
            if qk_pend["p"] is not None:
                qk_fin(*qk_pend["p"])
                qk_pend["p"] = None

        def norm_tile(xt_f, xt_res, g0, dest_ap_fn, dest_res, sq_b, sq_res, rs_ap, rs_res):
            S.op("act", lambda e: e.activation(out=sq_b, in_=xt_f, func=AF.Square), reads=xt_res, writes=sq_res)
            bi, bh = next_bank()
            mm_group(bank_ap(bi, bh), r_bank[bi][bh],
                     [(ones1024, sq_b[:, c * 512:(c + 1) * 512]) for c in range(8)], list(sq_res) + [*r_consts])
            rstd_from(bank_ap(bi, bh), r_bank[bi][bh], rs_ap, rs_res)
            for c in range(8):
                S.op("dve", lambda e, c=c: e.scalar_tensor_tensor(
                    out=dest_ap_fn(c), in0=xt_f[:, c * 512:(c + 1) * 512], scalar=vcol(g0 + c), in1=rs_ap,
                    op0=ALU.mult, op1=ALU.mult),
                    reads=list(xt_res) + [rs_res, *r_consts], writes=[dest_res])

        def emit_ssq(d, sbi, sbh, sqd):
            S.op("pe", lambda e: e.matmul(bank_ap(sbi, sbh), lhsT=ones1024, rhs=sqd[:, d * 512:(d + 1) * 512],
                                          start=(d == 0), stop=(d == 7)),
                 reads=[r_T[1][d // 2], *r_consts], writes=[r_bank[sbi][sbh]], inc=True)

        x_bufs = [(M_f[:, 0:4096], r_Mb[0:8]),
                  (O_f[:, 4096:8192], [r_O[h][t] for h in range(4, 8) for t in range(TT)]),
                  (R1_f[:, 4096:8192], r_R1[4:8]),
                  (O_f[:, 0:4096], [r_O[h][t] for h in range(4) for t in range(TT)])]

        def emit_x_loads(sq_, eng):
            for tt in range(TT):
                xt, xres = x_bufs[tt]
                S.dma(eng, lambda e, xt=xt, sq_=sq_, tt=tt: e.dma_start(out=xt, in_=xT_d[sq_, tt]), f"ld_x{tt}_{eng}",
                      reads=(list(x_bufs[tt - 1][1]) if (tt > 0 and eng == "pool") else []), writes=xres)

        for s in range(NSEQ):
            if s == 0:
                emit_x_loads(0, "sp")
            for tt in range(TT):
                xt, xres = x_bufs[tt]
                sqb = T_b[1][:, 0:4096]
                norm_tile(xt, xres, V_PRE,
                          lambda c, tt=tt: A[:, c * SEQ + tt * 512: c * SEQ + (tt + 1) * 512], r_A[tt],
                          sqb, r_T[1][0:4], T[1][:, 2560:3072], r_T[1][5])

            kT = lambda kvh: R1[:, kvh * SEQ:(kvh + 1) * SEQ]
            Vt = R1[:, 2 * SEQ:4 * SEQ]
            sl_kv = load_slot(wqkv_d[2], 4096)
            rg = ring[sl_kv]
            for tt in range(TT):
                for kvh in range(2):
                    qk_item([(rg[:, c * 512 + kvh * 128: c * 512 + (kvh + 1) * 128],
                              A[:, c * SEQ + tt * 512: c * SEQ + (tt + 1) * 512]) for c in range(8)],
                            [r_ring[sl_kv], r_A[tt]], V_KG, V_KGP,
                            kT(kvh)[:, tt * 512:(tt + 1) * 512], r_R1[kvh], tt)
            for kt2 in range(8):
                bi, bh = next_bank()
                for q in range(2):
                    kt = kt2 * 2 + q
                    mm_group(PS[bi][:, bh * 512 + q * 256: bh * 512 + (q + 1) * 256], r_bank[bi][bh],
                             [(A[:, c * SEQ + kt * 128: c * SEQ + (kt + 1) * 128],
                               rg[:, c * 512 + 256: c * 512 + 512]) for c in range(8)],
                             [r_ring[sl_kv], r_A[kt // 4]])
                S.op("act", lambda e, bi=bi, bh=bh, kt2=kt2: e.activation(
                    out=Vt[:, kt2 * 512:(kt2 + 1) * 512], in_=bank_ap(bi, bh), func=AF.Copy),
                    reads=[r_bank[bi][bh]], writes=[r_R1[2 + kt2 // 4]])

            def p_ap(j):
                return R1[:, (4 + j) * SEQ:(5 + j) * SEQ] if j < 4 else M[:, j * SEQ:(j + 1) * SEQ]

            def p_res(j):
                return [r_R1[4 + j]] if j < 4 else [r_Mb[2 * j], r_Mb[2 * j + 1]]

            S.op("dve", lambda e: e.memset(T[0][:, 0:1], 0.0), writes=[r_T[0][0]])
            S.op("dve", lambda e: e.memset(T[0][:, 2049:2050], 0.0), writes=[r_T[0][4]])
            c_wrow = T[0]
            c_yrow = T[0][:, 2560:4608]
            c_brow = T[1][:, 2560:4608]
            sl_q = None
            for j in range(8):
                if j % 4 == 0:
                    if sl_q is not None:
                        ring_pinned.discard(sl_q)
                    sl_q = load_slot(wqkv_d[j // 4], 4096)
                    ring_pinned.add(sl_q)
                rgq = ring[sl_q]
                hq = j % 4
                sl = load_slot(wubc_d[j], 3072)
                rg = ring[sl]
                for tt in range(TT):
                    qk_item([(rgq[:, c * 512 + hq * 128: c * 512 + (hq + 1) * 128],
                              A[:, c * SEQ + tt * 512: c * SEQ + (tt + 1) * 512]) for c in range(8)],
                            [r_ring[sl_q], r_A[tt]], V_QG, V_QGP,
                            O[:, j * SEQ + tt * 512: j * SEQ + (tt + 1) * 512], r_O[j][tt], tt)
                    bks = []
                    for k in (2, 0, 1):
                        bi, bh = next_bank()
                        mm_group(bank_ap(bi, bh), r_bank[bi][bh],
                                 [(rg[:, c * 384 + k * 128: c * 384 + (k + 1) * 128],
                                   A[:, c * SEQ + tt * 512: c * SEQ + (tt + 1) * 512]) for c in range(8)],
                                 [r_ring[sl], r_A[tt]])
                        bks.append((bi, bh))
                    (cbi, cbh), (ubi, ubh), (bbi, bbh) = bks
                    cs = T[1][:, (tt % 2) * 512:(tt % 2 + 1) * 512]
                    r_cs = r_T[1][tt % 2]
                    S.op("act", lambda e, cs=cs, cbi=cbi, cbh=cbh: e.activation(out=cs, in_=bank_ap(cbi, cbh), func=AF.Copy),
                         reads=[r_bank[cbi][cbh]], writes=[r_cs])
                    S.op("dve", lambda e, cs=cs, ubi=ubi, ubh=ubh, tt=tt: e.tensor_tensor(
                        out=c_wrow[:, 1 + tt * 512: 1 + (tt + 1) * 512], in0=bank_ap(ubi, ubh), in1=cs, op=ALU.mult),
                        reads=[r_bank[ubi][ubh], r_cs], writes=[r_T[0][tt], r_T[0][tt + 1]])
                    S.op("act", lambda e, bbi=bbi, bbh=bbh, tt=tt: e.activation(
                        out=c_brow[:, tt * 512:(tt + 1) * 512], in_=bank_ap(bbi, bbh), func=AF.Copy),
                        reads=[r_bank[bbi][bbh]], writes=[r_T[1][5 + tt]])
                rw = r_T[0][0:5]
                ry = r_T[0][5:9]
                S.op("dve", lambda e, j=j: e.tensor_scalar(out=c_yrow, in0=c_wrow[:, 0:2048], scalar1=vcol(V_MCW + j), scalar2=None,
                                                           op0=ALU.mult), reads=rw + [*r_consts], writes=ry)
                S.op("dve", lambda e, j=j: e.scalar_tensor_tensor(out=c_yrow, in0=c_wrow[:, 1:2049], scalar=vcol(V_MCW + 8 + j),
                                                                  in1=c_yrow, op0=ALU.mult, op1=ALU.add),
                     reads=rw + ry + [*r_consts], writes=ry)
                S.op("dve", lambda e, j=j: e.scalar_tensor_tensor(out=c_yrow, in0=c_wrow[:, 2:2050], scalar=vcol(V_MCW + 16 + j),
                                                                  in1=c_yrow, op0=ALU.mult, op1=ALU.add),
                     reads=rw + ry + [*r_consts], writes=ry)
                S.op("dve", lambda e, j=j: e.tensor_tensor(out=p_ap(j), in0=c_yrow, in1=c_brow, op=ALU.mult),
                     reads=ry + r_T[1][5:9], writes=p_res(j))
            qk_flush()
            ring_pinned.discard(sl_q)

            if s == 0:
                for d in range(8):
                    S.dma("pool", lambda e, d=d: e.dma_start(
                        out=wdnb_d[d].rearrange("p (a b) -> p a b", b=704),
                        in_=wdn_d[d].rearrange("p (a b) -> p a b", b=704)), f"cv_wdn{d}", reads=[r_O[7][3]], writes=[r_wdnb[d]])

            pairs = [(hh, m2) for hh in range(NH) for m2 in range(2)]

            def actx(p):
                hh, m2 = pairs[p]
                qts = (2 * m2, 2 * m2 + 1)
                return dict(kvh=hh // 4,
                            qaps=[O[:, hh * SEQ + qt * 512: hh * SEQ + (qt + 1) * 512] for qt in qts],
                            r_qs=[r_O[hh][qt] for qt in qts])

            def emit_qk(cx, st, kp):
                kvh, qap = cx["kvh"], cx["qaps"][st]
                for q in range(2):
                    kt = kp * 2 + q
                    S.op("pe", lambda e, st=st, q=q, kt=kt, kvh=kvh, qap=qap: e.matmul(
                        PS[st][:, q * 512:(q + 1) * 512], lhsT=kT(kvh)[:, kt * 128:(kt + 1) * 128], rhs=qap,
                        start=True, stop=True),
                        reads=[r_R1[kvh], cx["r_qs"][st]], writes=[r_bank[st][0], r_bank[st][1]], inc=(q == 1))

            def pt_buf(st, kp):
                b = st * 2 + (kp % 2)
                return T_b[0][:, b * 1024:(b + 1) * 1024], r_T[0][b]

            def pa_buf(st, kp):
                b = 4 + st * 2 + (kp % 2)
                return T_b[0][:, b * 1024:b * 1024 + 512], r_T[0][b]

            def emit_exp(st, kp):
                PT, r_PT = pt_buf(st, kp)
                S.op("act", lambda e, PT=PT, st=st: e.activation(out=PT, in_=PS[st][:, :], func=AF.Exp, scale=SCALE),
                     reads=[r_bank[st][0], r_bank[st][1]], writes=[r_PT])
                pa, r_pa = pa_buf(st, kp)
                S.op("dve", lambda e, PT=PT, pa=pa: e.tensor_tensor(out=pa, in0=PT[:, 0:512], in1=PT[:, 512:1024],
                                                                   op=ALU.add), reads=[r_PT], writes=[r_pa])

            def emit_pv(cx, st, kp):
                PT, r_PT = pt_buf(st, kp)
                oacc = bank_ap(2, st)
                kvh = cx["kvh"]
                for q in range(2):
                    kt = kp * 2 + q
                    S.op("pe", lambda e, q=q, kt=kt, oacc=oacc, kvh=kvh, PT=PT: e.matmul(
                        oacc, lhsT=Vt[:, kt * 256 + kvh * 128: kt * 256 + (kvh + 1) * 128],
                        rhs=PT[:, q * 512:(q + 1) * 512], start=(kt == 0), stop=(kt == 15)),
                        reads=[r_PT, r_R1[2 + kt // 8]], writes=[r_bank[2][st]], inc=(q == 1))

            def emit_den(st, kp):
                pa, r_pa = pa_buf(st, kp)
                den = bank_ap(3, st)
                S.op("pe", lambda e, pa=pa, den=den, kp=kp: e.matmul(den, lhsT=ones1, rhs=pa, start=(kp == 0), stop=(kp == 7)),
                     reads=[r_pa, *r_consts], writes=[r_bank[3][st]], inc=True)

            def emit_epilogue(cx):
                for st in range(2):
                    rd = T[1][:, st * 512:(st + 1) * 512]
                    r_rd = r_T[1][st]
                    S.op("act", lambda e, rd=rd, st=st: e.activation(out=rd, in_=bank_ap(3, st), func=AF.Ln),
                         reads=[r_bank[3][st]], writes=[r_rd])
                    S.op("act", lambda e, rd=rd: e.activation(out=rd, in_=rd, func=AF.Exp, scale=-1.0),
                         reads=[r_rd], writes=[r_rd])
                    S.op("dve", lambda e, rd=rd, st=st, qap=cx["qaps"][st]: e.tensor_tensor(
                        out=qap, in0=bank_ap(2, st), in1=rd, op=ALU.mult),
                        reads=[r_bank[2][st], r_rd], writes=[cx["r_qs"][st]])

            cx = actx(0)
            for st in range(2):
                emit_qk(cx, st, 0)
            for p in range(len(pairs)):
                cx = actx(p)
                cx_next = actx(p + 1) if p + 1 < len(pairs) else None
                for kp in range(8):
                    for st in range(2):
                        emit_exp(st, kp)
                    if kp == 0 and p >= 1:
                        emit_epilogue(actx(p - 1))
                    for st in range(2):
                        if kp + 1 < 8:
                            emit_qk(cx, st, kp + 1)
                        elif cx_next is not None:
                            emit_qk(cx_next, st, 0)
                        emit_pv(cx, st, kp)
                        if kp >= 1:
                            emit_den(st, kp - 1)
                for st in range(2):
                    emit_den(st, 7)
            emit_epilogue(actx(len(pairs) - 1))

            def mg_ap(j, tt):
                if j < 4:
                    return R1[:, j * SEQ + tt * 512: j * SEQ + (tt + 1) * 512]
                return M[:, (j - 4) * SEQ + tt * 512:(j - 4) * SEQ + (tt + 1) * 512]

            def mg_res(j, tt):
                return [r_R1[j]] if j < 4 else [r_Mb[((j - 4) * 4 + tt) // 2]]

            def p_tile_res(c, tt):
                return [r_R1[4 + c]] if c < 4 else [r_Mb[(c * 4 + tt) // 2]]

            for j in range(8):
                sl = load_slot(wpj_d[j], 4096)
                rg = ring[sl]
                for tt in range(TT):
                    tsl = lambda c, tt=tt: slice(c * SEQ + tt * 512, c * SEQ + (tt + 1) * 512)
                    src_fns = (
                        (lambda c, tt=tt: O[:, tsl(c)], lambda c, tt=tt: [r_O[c][tt]]),
                        (lambda c, tt=tt: p_ap(c)[:, tt * 512:(tt + 1) * 512], lambda c, tt=tt: p_tile_res(c, tt)),
                        (lambda c, tt=tt: A[:, tsl(c)], lambda c, tt=tt: [r_A[tt]]),
                        (lambda c, tt=tt: A[:, tsl(c)], lambda c, tt=tt: [r_A[tt]]),
                    )
                    bk = []
                    for k, (ap_fn, rs_fn) in enumerate(src_fns):
                        bi, bh = next_bank()
                        rds = [r_ring[sl]]
                        for c in range(8):
                            for r_ in rs_fn(c):
                                if r_ not in rds:
                                    rds.append(r_)
                        mm_group(bank_ap(bi, bh), r_bank[bi][bh],
                                 [(rg[:, c * 512 + k * 128: c * 512 + (k + 1) * 128], ap_fn(c)) for c in range(8)], rds)
                        bk.append((bi, bh))
                    sa = T[tt % 2][:, 0:512]
                    sbg = T[tt % 2][:, 512:1024]
                    m1 = T[tt % 2][:, 1024:1536]
                    r_sa, r_sb, r_m1 = r_T[tt % 2][0], r_T[tt % 2][1], r_T[tt % 2][2]
                    S.op("act", lambda e, sa=sa, b=bk[2], j=j: e.activation(out=sa, in_=bank_ap(*b), func=AF.Sigmoid,
                                                                             bias=vcol(V_GBA + j)),
                         reads=[r_bank[bk[2][0]][bk[2][1]], *r_consts], writes=[r_sa])
                    S.op("act", lambda e, sbg=sbg, b=bk[3], j=j: e.activation(out=sbg, in_=bank_ap(*b), func=AF.Sigmoid,
                                                                               bias=vcol(V_GBB + j)),
                         reads=[r_bank[bk[3][0]][bk[3][1]], *r_consts], writes=[r_sb])
                    S.op("dve", lambda e, sa=sa, m1=m1, b=bk[0]: e.tensor_tensor(out=m1, in0=bank_ap(*b), in1=sa, op=ALU.mult),
                         reads=[r_bank[bk[0][0]][bk[0][1]], r_sa], writes=[r_m1])
                    S.op("dve", lambda e, sbg=sbg, b=bk[1]: e.tensor_tensor(out=sbg, in0=bank_ap(*b), in1=sbg, op=ALU.mult),
                         reads=[r_bank[bk[1][0]][bk[1][1]], r_sb], writes=[r_sb])
                    S.op("dve", lambda e, sbg=sbg, m1=m1, j=j, tt=tt: e.tensor_tensor(
                        out=mg_ap(j, tt), in0=m1, in1=sbg, op=ALU.add),
                        reads=[r_m1, r_sb], writes=mg_res(j, tt))

            for blk in range(2):
                S.dma("pool", lambda e, blk=blk: e.dma_start(
                    out=O[:, blk * 4096:(blk + 1) * 4096].rearrange("p (a b) -> p a b", b=1024),
                    in_=wo_d[blk].rearrange("p (a b) -> p a b", b=1024)), f"ld_wo{blk}",
                    writes=[r_O[h][t] for h in (2 * blk, 2 * blk + 1) for t in range(TT)])
            wo_res = lambda d: [r_O[h][t] for h in (2 * (d // 4), 2 * (d // 4) + 1) for t in range(TT)]
            out_state = {}

            def mo_tile(tt):
                if tt % 2 == 0:
                    return T[0][:, 0:4096], r_T[0][0:8]
                return O_f[:, 4096:8192], [r_O[h][t] for h in range(4, 8) for t in range(TT)]

            def mo_buf(tt, d):
                if tt % 2 == 0:
                    return T[0][:, d * 512:(d + 1) * 512], [r_T[0][d]]
                return (O_f[:, 4096 + d * 512: 4096 + (d + 1) * 512],
                        [r_O[4 + d // 2][(d % 2) * 2], r_O[4 + d // 2][(d % 2) * 2 + 1]])

            SQB = (4, 7, 8)

            def out_ssq(tt, d):
                xt2, xres, sbi, sbh = out_state[tt]
                b = SQB[d % 3]
                S.op("pe", lambda e: e.matmul(bank_ap(sbi, sbh), lhsT=ones1024, rhs=T_b[1][:, b * 1024: b * 1024 + 512],
                                              start=(d == 0), stop=(d == 7)),
                     reads=[r_T[1][b], *r_consts], writes=[r_bank[sbi][sbh]], inc=True)

            def out_A(tt, d0, d1):
                hb = tt % 2
                xt2 = R1_f[:, 4096:8192] if hb == 0 else M_f[:, 4096:8192]
                xres = r_R1[4:8] if hb == 0 else r_Mb[8:16]
                if d0 == 0:
                    sbi, sbh = next_bank()
                    bank_reserved.add((sbi, sbh))
                    out_state[tt] = (xt2, xres, sbi, sbh)
                if d1 == 8:
                    S.dma("sp", lambda e, xt2=xt2, s=s, tt=tt: e.dma_start(out=xt2, in_=xT_d[s, tt]), f"ld_y{hb}", writes=xres)
                for d in range(d0, d1):
                    bi, bh = next_bank()
                    mm_group(bank_ap(bi, bh), r_bank[bi][bh],
                             [(O[:, (d // 4) * 4096 + c * 512 + (d % 4) * 128: (d // 4) * 4096 + c * 512 + (d % 4 + 1) * 128],
                               mg_ap(c, tt)) for c in range(8)],
                             wo_res(d) + [r_ for c in range(8) for r_ in mg_res(c, tt)])
                    if d >= 1:
                        out_ssq(tt, d - 1)
                    mo_ap, mo_res = mo_buf(tt, d)
                    S.op("act", lambda e, mo_ap=mo_ap, bi=bi, bh=bh, d=d: e.activation(
                        out=mo_ap, in_=bank_ap(bi, bh), func=AF.Copy, scale=vcol(V_POST + d)),
                        reads=[r_bank[bi][bh], *r_consts], writes=mo_res)
                    b = SQB[d % 3]
                    S.op("act", lambda e, b=b, bi=bi, bh=bh: e.activation(out=T_b[1][:, b * 1024: b * 1024 + 512],
                                                                          in_=bank_ap(bi, bh), func=AF.Square),
                         reads=[r_bank[bi][bh]], writes=[r_T[1][b]])
                if d1 == 8:
                    out_ssq(tt, 7)

            def out_B1(tt):
                xt2, xres, sbi, sbh = out_state[tt]
                hb = tt % 2
                rs = T[1][:, 2560:3072]
                rstd_from(bank_ap(sbi, sbh), r_bank[sbi][sbh], rs, r_T[1][5])
                bank_reserved.discard((sbi, sbh))
                mt, mt_res = mo_tile(tt)
                mt3 = mt.rearrange("p (c n) -> p c n", n=512)
                rs3 = rs.unsqueeze(1).to_broadcast([128, 8, 512])
                S.op("dve", lambda e, mt3=mt3, rs3=rs3: e.tensor_tensor(out=mt3, in0=mt3, in1=rs3, op=ALU.mult),
                     reads=mt_res + [r_T[1][5]], writes=mt_res)
                S.op("dve", lambda e, mt=mt, xt2=xt2: e.tensor_tensor(out=xt2, in0=mt, in1=xt2, op=ALU.add),
                     reads=mt_res + xres, writes=xres)
                S.dma("sp", lambda e, xt2=xt2, s=s, tt=tt: e.dma_start(out=x1_d[s, tt], in_=xt2), f"st_x1{hb}",
                      reads=xres, writes=[r_x1d[s][tt]])

            def out_C1(tt):
                xt2, xres, sbi, sbh = out_state[tt]
                S.op("act", lambda e, xt2=xt2: e.activation(out=T_b[1][:, 0:4096], in_=xt2, func=AF.Square),
                     reads=xres, writes=r_T[1][0:4])

            def out_C2(tt):
                bi, bh = next_bank()
                bank_reserved.add((bi, bh))
                mm_group(bank_ap(bi, bh), r_bank[bi][bh],
                         [(ones1024, T_b[1][:, c * 512:(c + 1) * 512]) for c in range(8)], r_T[1][0:4] + [*r_consts])
                out_state[("c", tt)] = (bi, bh)

            def out_C3(tt):
                xt2, xres, sbi, sbh = out_state[tt]
                bi, bh = out_state[("c", tt)]
                rs2 = T[1][:, 3072:3584]
                rstd_from(bank_ap(bi, bh), r_bank[bi][bh], rs2, r_T[1][6])
                bank_reserved.discard((bi, bh))
                for c in range(8):
                    S.op("dve", lambda e, c=c, xt2=xt2, tt=tt: e.scalar_tensor_tensor(
                        out=A[:, c * SEQ + tt * 512: c * SEQ + (tt + 1) * 512], in0=xt2[:, c * 512:(c + 1) * 512],
                        scalar=vcol(V_FPRE + c), in1=rs2, op0=ALU.mult, op1=ALU.mult),
                        reads=list(xres) + [r_T[1][6], *r_consts], writes=[r_A[tt]])

            out_A(0, 0, 8)
            out_B1(0)
            out_A(1, 0, 8)
            out_C1(0)
            out_B1(1)
            for tt in range(2, TT):
                out_A(tt, 0, 4)
                out_C2(tt - 2)
                out_C3(tt - 2)
                out_A(tt, 4, 8)
                out_C1(tt - 1)
                out_B1(tt)
            out_C2(TT - 2)
            out_C3(TT - 2)
            out_C1(TT - 1)
            out_C2(TT - 1)
            out_C3(TT - 1)

            hid_buf = lambda j: (O, R1, M)[j // 8]

            def hid_res(j, tt):
                if j < 8:
                    return [r_O[j][tt]]
                if j < 16:
                    return [r_R1[j % 8]]
                return [r_Mb[((j % 8) * 4 + tt) // 2]]

            for i in range(2):
                S.op("dve", lambda e, i=i: e.memset(T[i][:, 0:1], 0.0), writes=[r_T[i][0]])
                S.op("dve", lambda e, i=i: e.memset(T[i][:, 2049:2050], 0.0), writes=[r_T[i][4]])
            for g in range(11):
                sl = load_slot(wup_d[g], 4096)
                rg = ring[sl]
                for jj in range(2):
                    j = 2 * g + jj
                    ti = j % 2
                    arow = T[ti]
                    yrow = T[ti][:, 2560:4608]
                    ra = r_T[ti][0:5]
                    ry = r_T[ti][5:9]
                    for tt in range(TT):
                        bi, bh = next_bank()
                        mm_group(bank_ap(bi, bh), r_bank[bi][bh],
                                 [(rg[:, c * 512 + jj * 128: c * 512 + (jj + 1) * 128],
                                   A[:, c * SEQ + tt * 512: c * SEQ + (tt + 1) * 512]) for c in range(8)],
                                 [r_ring[sl], r_A[tt]])
                        S.op("act", lambda e, bi=bi, bh=bh, tt=tt, arow=arow: e.activation(
                            out=arow[:, 1 + tt * 512: 1 + (tt + 1) * 512], in_=bank_ap(bi, bh), func=AF.Copy),
                            reads=[r_bank[bi][bh]], writes=[r_T[ti][tt], r_T[ti][tt + 1]])
                    bb = []
                    for tt in range(TT):
                        bi, bh = next_bank()
                        mm_group(bank_ap(bi, bh), r_bank[bi][bh],
                                 [(rg[:, c * 512 + 256 + jj * 128: c * 512 + 256 + (jj + 1) * 128],
                                   A[:, c * SEQ + tt * 512: c * SEQ + (tt + 1) * 512]) for c in range(8)],
                                 [r_ring[sl], r_A[tt]])
                        bb.append((bi, bh))
                    S.op("dve", lambda e, j=j, arow=arow, yrow=yrow: e.tensor_scalar(
                        out=yrow, in0=arow[:, 0:2048], scalar1=vcol(V_FCW + j), scalar2=None, op0=ALU.mult),
                        reads=ra + [*r_consts], writes=ry)
                    S.op("dve", lambda e, j=j, arow=arow, yrow=yrow: e.scalar_tensor_tensor(
                        out=yrow, in0=arow[:, 1:2049], scalar=vcol(V_FCW + NJ + j), in1=yrow, op0=ALU.mult, op1=ALU.add),
                        reads=ra + ry + [*r_consts], writes=ry)
                    S.op("dve", lambda e, j=j, arow=arow, yrow=yrow: e.scalar_tensor_tensor(
                        out=yrow, in0=arow[:, 2:2050], scalar=vcol(V_FCW + 2 * NJ + j), in1=yrow, op0=ALU.mult, op1=ALU.add),
                        reads=ra + ry + [*r_consts], writes=ry)
                    S.op("act", lambda e, yrow=yrow: e.activation(out=yrow, in_=yrow, func=AF.Gelu_apprx_tanh),
                         reads=ry, writes=ry)
                    hb_ = hid_buf(j)
                    for tt in range(TT):
                        bi, bh = bb[tt]
                        S.op("dve", lambda e, bi=bi, bh=bh, tt=tt, yrow=yrow, hb_=hb_, j=j: e.tensor_tensor(
                            out=hb_[:, (j % 8) * SEQ + tt * 512:(j % 8) * SEQ + (tt + 1) * 512],
                            in0=bank_ap(bi, bh), in1=yrow[:, tt * 512:(tt + 1) * 512], op=ALU.mult),
                            reads=[r_bank[bi][bh], r_T[ti][5 + tt]], writes=hid_res(j, tt))

            def o2_buf(tt, d):
                if tt % 2 == 0:
                    return T[0][:, d * 512:(d + 1) * 512], [r_T[0][d]]
                if d < 4:
                    return M_f[:, (12 + d) * 512:(13 + d) * 512], [r_Mb[12 + d]]
                b = 4 if d == 7 else 6 + (d - 4)
                return T[1][:, b * 512:(b + 1) * 512], [r_T[1][b]]

            r_x1h = [Res(f"x1h{s}_0"), Res(f"x1h{s}_1")]
            for tt in range(TT):
                hb = tt % 2
                x1t = A_f[:, hb * 4096:(hb + 1) * 4096]
                x1res = [r_x1h[hb]]
                S.dma("sp", lambda e, x1t=x1t, s=s, tt=tt: e.dma_start(out=x1t, in_=x1_d[s, tt]), f"ld_x1{hb}",
                      reads=[r_x1d[s][tt]], writes=x1res + (r_A if tt < 2 else []))
                sqd = T_b[1][:, 0:4096]
                sbi, sbh = next_bank()
                for d in range(8):
                    sl = load_slot(wdnb_d[d], 2816, reads=[r_wdnb[d]], nb=1)
                    rg = ring[sl]
                    bi, bh = next_bank()
                    if (bi, bh) == (sbi, sbh):
                        bi, bh = next_bank()
                    rds = [r_ring[sl]]
                    for j in range(NJ):
                        for r_ in hid_res(j, tt):
                            if r_ not in rds:
                                rds.append(r_)
                    mm_group(bank_ap(bi, bh), r_bank[bi][bh],
                             [(rg[:, j * 128:(j + 1) * 128],
                               hid_buf(j)[:, (j % 8) * SEQ + tt * 512:(j % 8) * SEQ + (tt + 1) * 512]) for j in range(NJ)], rds)
                    if d >= 1:
                        emit_ssq(d - 1, sbi, sbh, sqd)
                    o2_ap, o2_res = o2_buf(tt, d)
                    S.op("act", lambda e, o2_ap=o2_ap, bi=bi, bh=bh: e.activation(out=o2_ap, in_=bank_ap(bi, bh), func=AF.Copy),
                         reads=[r_bank[bi][bh]], writes=o2_res)
                    S.op("act", lambda e, d=d, bi=bi, bh=bh: e.activation(out=sqd[:, d * 512:(d + 1) * 512], in_=bank_ap(bi, bh),
                                                                          func=AF.Square),
                         reads=[r_bank[bi][bh]], writes=[r_T[1][d // 2]])
                emit_ssq(7, sbi, sbh, sqd)
                if tt == TT - 1 and s + 1 < NSEQ:
                    emit_x_loads(s + 1, "pool")
                rs = T[1][:, 2560:3072]
                rstd_from(bank_ap(sbi, sbh), r_bank[sbi][sbh], rs, r_T[1][5])
                for d in range(8):
                    o2_ap, o2_res = o2_buf(tt, d)
                    S.op("dve", lambda e, d=d, o2_ap=o2_ap: e.scalar_tensor_tensor(
                        out=o2_ap, in0=o2_ap, scalar=vcol(V_FPOST + d), in1=rs,
                        op0=ALU.mult, op1=ALU.mult), reads=o2_res + [r_T[1][5], *r_consts], writes=o2_res)
                    S.op("dve", lambda e, d=d, x1t=x1t, o2_ap=o2_ap: e.tensor_tensor(
                        out=x1t[:, d * 512:(d + 1) * 512], in0=o2_ap, in1=x1t[:, d * 512:(d + 1) * 512],
                        op=ALU.add), reads=o2_res + x1res, writes=x1res)
                out_evs.append(S.dma("sp", lambda e, x1t=x1t, s=s, tt=tt: e.dma_start(out=yT_d[s, tt], in_=x1t),
                                     f"st_y{hb}", reads=x1res + r_A))

        S.wait_all("sp", out_evs)
        S.build(nc, es)
    return nc


def _host_layout(inp):
    f = lambda a: np.ascontiguousarray(np.asarray(a, dtype=np.float32))
    w_in = f(inp["w_in"])[0]
    w_ap = f(inp["w_attn_proj"])[0]
    w_cp = f(inp["w_conv_proj"])[0]
    w_out = f(inp["w_out"])[0]
    w_up = f(inp["w_up"])[0]
    w_down = f(inp["w_down"])[0]
    sh = {}
    sh["w_qkv"] = f(w_in[:, 0:1536].reshape(8, 128, 3, 512).transpose(2, 1, 0, 3).reshape(3, 128, 4096))
    sh["w_ubc"] = f(w_in[:, 1536:4608].reshape(8, 128, 3, 8, 128).transpose(3, 1, 0, 2, 4).reshape(8, 128, 3072))
    W4 = np.stack([w_ap, w_cp, w_in[:, 4608:5632], w_in[:, 5632:6656]], axis=0)
    sh["w_pj"] = f(W4.reshape(4, 8, 128, 8, 128).transpose(3, 2, 1, 0, 4).reshape(8, 128, 4096))
    sh["w_o"] = f(w_out.reshape(8, 128, 2, 512).transpose(2, 1, 0, 3).reshape(2, 128, 4096))
    sh["w_upg"] = f(w_up.reshape(8, 128, 2, 11, 2, 128).transpose(3, 1, 0, 2, 4, 5).reshape(11, 128, 4096))
    sh["w_dn"] = f(w_down.reshape(22, 128, 8, 128).transpose(2, 1, 0, 3).reshape(8, 128, 2816))
    vec = np.zeros((128, NV), np.float32)
    col8 = lambda v: f(v).reshape(8, 128).T
    vec[:, V_PRE:V_PRE + 8] = col8(inp["mix_pre_g"][0])
    vec[:, V_POST:V_POST + 8] = col8(inp["mix_post_g"][0])
    vec[:, V_FPRE:V_FPRE + 8] = col8(inp["ffn_pre_g"][0])
    vec[:, V_FPOST:V_FPOST + 8] = col8(inp["ffn_post_g"][0])
    gb = f(inp["gate_b"])[0]
    vec[:, V_GBA:V_GBA + 8] = col8(gb[:D])
    vec[:, V_GBB:V_GBB + 8] = col8(gb[D:])
    perm = np.arange(128)
    perm = np.where((perm % 64) < 32, perm + 32, perm - 32)
    qg = f(inp["q_norm_g"])[0]
    kg = f(inp["k_norm_g"])[0]
    vec[:, V_QG] = qg
    vec[:, V_QGP] = qg[perm]
    vec[:, V_KG] = kg
    vec[:, V_KGP] = kg[perm]
    mcw = f(inp["mix_conv_w"])[0]
    for k in range(3):
        vec[:, V_MCW + 8 * k:V_MCW + 8 * (k + 1)] = mcw[k].reshape(8, 128).T
    fcw = f(inp["ffn_conv_w"])[0]
    for k in range(3):
        vec[:, V_FCW + NJ * k:V_FCW + NJ * (k + 1)] = fcw[k].reshape(NJ, 128).T
    sh["vecs"] = vec
    t = np.arange(SEQ)
    row_pos = (t // 64).astype(np.float64)
    col_pos = (t % 64).astype(np.float64)
    freqs = 10000.0 ** (-(np.arange(32, dtype=np.float64) / 32.0))
    tabs = np.zeros((128, 2 * SEQ), np.float32)
    for d in range(128):
        pos = row_pos if d < 64 else col_pos
        ang = pos * freqs[d % 32]
        tabs[d, 0:SEQ] = np.cos(ang)
        tabs[d, SEQ:] = np.sin(ang)
    sh["tabs"] = tabs
    cm = np.zeros((128, 512), np.float32)
    cm[:, 0:128] = 1.0 / 1024.0
    cm[:, 128:256] = 1.0 / 128.0
    cm[:, 256:384] = 1.0
    for m in range(128):
        if (m % 64) < 32:
            cm[m + 32, 384 + m] = -1.0
        else:
            cm[m - 32, 384 + m] = 1.0
    sh["cmat"] = cm
    return sh


_NC_CACHE = {}


def kernel(**inputs):
    x = np.asarray(inputs["x"], dtype=np.float32)
    shared = _host_layout(inputs)
    in_maps = []
    for core in range(NCORES):
        xs = x[core * NSEQ:(core + 1) * NSEQ]
        xT = np.ascontiguousarray(xs.reshape(NSEQ, TT, 512, 8, 128).transpose(0, 1, 4, 3, 2)).reshape(NSEQ, TT, 128, 4096)
        m = dict(shared)
        m["xT"] = xT
        in_maps.append(m)
    if "nc" not in _NC_CACHE:
        _NC_CACHE["nc"] = build_nc()
    nc = _NC_CACHE["nc"]
    res = run_bass_kernel_spmd(nc, in_maps, core_ids=list(range(NCORES)))
    outs = []
    for core in range(NCORES):
        yT = np.asarray(res.results[core]["yT"]).reshape(NSEQ, TT, 128, 8, 512)
        outs.append(yT.transpose(0, 1, 4, 3, 2).reshape(NSEQ, SEQ, D))
    return np.ascontiguousarray(np.concatenate(outs, axis=0).astype(np.float32))
```

```python
import numpy as np
from contextlib import ExitStack
import concourse.bass as bass
import concourse.mybir as mybir
from concourse.bass_utils import run_bass_kernel_spmd

F32 = mybir.dt.float32
BF16 = mybir.dt.bfloat16
AF = mybir.ActivationFunctionType
ALU = mybir.AluOpType

NCORES = 8
SEQ = 2048
D = 1024
NSEQ = 2
TT = 4
HD = 128
NH = 8
DFF = 2816
NJ = 22
EPS = 1e-6
SCALE = HD ** -0.5
NSLOT = 3

V_PRE, V_POST, V_FPRE, V_FPOST, V_GBA, V_GBB = 0, 8, 16, 24, 32, 40
V_QG, V_QGP, V_KG, V_KGP = 48, 49, 50, 51
V_MCW = 52
V_FCW = 76
NV = 142


class Res:
    __slots__ = ("name", "w", "r", "x")

    def __init__(self, name, excl=False):
        self.name = name
        self.w = None
        self.r = []
        self.x = excl


class Sched:
    ENGS = ("pe", "act", "dve", "pool", "sp")

    def __init__(self):
        self.q = {e: [] for e in self.ENGS}
        self.cnt = {e: 0 for e in self.ENGS}
        self.seen = {e: {} for e in self.ENGS}
        self.dcnt = {}

    def _waits(self, eng, reads, writes):
        evs = []
        for r in reads:
            if r.w is not None:
                evs.append(r.w)
        same_ok = (eng == "pe")
        for w in writes:
            if w.w is not None and (w.w[0] != eng or not same_ok):
                evs.append(w.w)
            for ev in w.r:
                if ev[0] != eng or not same_ok:
                    evs.append(ev)
        need = {}
        for k, v in evs:
            if k in self.cnt:
                assert v <= self.cnt[k], (eng, k, v, self.cnt[k])
            if v > self.seen[eng].get(k, 0):
                need[k] = max(need.get(k, 0), v)
        for k, v in need.items():
            self.seen[eng][k] = v
        return list(need.items())

    def op(self, eng, fn, reads=(), writes=(), inc=True):
        xr = [r for r in reads if r.x]
        if xr:
            writes = list(writes) + [r for r in xr if r not in writes]
            reads = [r for r in reads if not r.x]
        waits = self._waits(eng, reads, writes)
        if inc:
            self.cnt[eng] += 1
            ev = (eng, self.cnt[eng])
        else:
            ev = (eng, self.cnt[eng] + 1)
        for r in reads:
            r.r.append(ev)
        for w in writes:
            w.w = ev
            w.r = []
        self.q[eng].append((waits, fn, (eng, 1) if inc else None))
        return ev

    def dma(self, eng, fn, semkey, reads=(), writes=()):
        waits = self._waits(eng, reads, writes)
        self.dcnt[semkey] = self.dcnt.get(semkey, 0) + 16
        ev = (semkey, self.dcnt[semkey])
        for r in reads:
            r.r.append(ev)
        for w in writes:
            w.w = ev
            w.r = []
        self.q[eng].append((waits, fn, (semkey, 16)))
        return ev

    def wait_all(self, eng, evs):
        need = {}
        for k, v in evs:
            if v > self.seen[eng].get(k, 0):
                need[k] = max(need.get(k, 0), v)
        for k, v in need.items():
            self.seen[eng][k] = v
        self.q[eng].append((list(need.items()), None, None))

    def build(self, nc, es):
        sems = {}
        for k in list(self.ENGS) + sorted(self.dcnt.keys()):
            sems[k] = es.enter_context(nc.semaphore("s_" + k))
        block = es.enter_context(nc.Block())
        q = self.q

        def run(eng_name):
            def body(e):
                for waits, fn, inc in q[eng_name]:
                    for k, v in waits:
                        e.wait_ge(sems[k], v)
                    if fn is not None:
                        ins = fn(e)
                        if inc is not None:
                            ins.then_inc(sems[inc[0]], inc[1])
            return body

        block.tensor(run("pe"))
        block.scalar(run("act"))
        block.vector(run("dve"))
        block.gpsimd(run("pool"))
        block.sync(run("sp"))


def build_nc():
    nc = bass.Bass("TRN2", target_bir_lowering=False)
    xT_d = nc.dram_tensor("xT", [NSEQ, TT, 128, 4096], F32, kind="ExternalInput")
    wqkv_d = nc.dram_tensor("w_qkv", [3, 128, 4096], F32, kind="ExternalInput")
    wubc_d = nc.dram_tensor("w_ubc", [8, 128, 3072], F32, kind="ExternalInput")
    wpj_d = nc.dram_tensor("w_pj", [8, 128, 4096], F32, kind="ExternalInput")
    wo_d = nc.dram_tensor("w_o", [2, 128, 4096], F32, kind="ExternalInput")
    wup_d = nc.dram_tensor("w_upg", [11, 128, 4096], F32, kind="ExternalInput")
    wdn_d = nc.dram_tensor("w_dn", [8, 128, 2816], F32, kind="ExternalInput")
    vec_d = nc.dram_tensor("vecs", [128, NV], F32, kind="ExternalInput")
    tab_d = nc.dram_tensor("tabs", [128, 2 * SEQ], F32, kind="ExternalInput")
    cm_d = nc.dram_tensor("cmat", [128, 512], F32, kind="ExternalInput")
    yT_d = nc.dram_tensor("yT", [NSEQ, TT, 128, 4096], F32, kind="ExternalOutput")
    x1_d = nc.dram_tensor("x1s", [NSEQ, TT, 128, 4096], F32)
    wdnb_d = nc.dram_tensor("wdn_bf16", [8, 128, 2816], BF16)

    es = ExitStack()
    S = Sched()
    with es:
        sb = lambda name, shape, dt: es.enter_context(nc.sbuf_tensor(name, shape, dt))
        A = sb("A", [128, 8 * SEQ], BF16)
        O = sb("O", [128, 8 * SEQ], BF16)
        R1 = sb("R1", [128, 8 * SEQ], BF16)
        M = sb("M", [128, 8 * SEQ], BF16)
        tabs = sb("tabs_sb", [128, 2 * SEQ], F32)
        ring = [sb(f"ring{i}", [128, 4096], BF16) for i in range(NSLOT)]
        T = [sb(f"T{i}", [128, 4608], F32) for i in range(2)]
        vecs = sb("vecs_sb", [128, NV], F32)
        cmat = sb("cmat_sb", [128, 512], BF16)
        epsb = sb("epsb", [128, 1], F32)
        PS = [es.enter_context(nc.psum_tensor(f"ps{i}", [128, 1024], F32)) for i in range(4)]

        A_f = A.bitcast(F32)
        O_f = O.bitcast(F32)
        R1_f = R1.bitcast(F32)
        M_f = M.bitcast(F32)
        T_b = [t.bitcast(BF16) for t in T]

        r_A = [Res(f"A{t}") for t in range(TT)]
        r_O = [[Res(f"O{h}_{t}") for t in range(TT)] for h in range(8)]
        r_R1 = [Res(f"R1_{c}") for c in range(8)]
        r_Mb = [Res(f"Mb{b}") for b in range(16)]
        r_ring = [Res(f"ring{i}") for i in range(NSLOT)]
        r_T = [[Res(f"T{i}_{b}") for b in range(9)] for i in range(2)]
        r_bank = [[Res(f"ps{i}_{h}", excl=True) for h in range(2)] for i in range(4)]
        r_consts = [Res("eps"), Res("vec"), Res("cm")]
        r_tabs = Res("tabs")
        r_x1d = [[Res(f"x1d{s}_{t}") for t in range(TT)] for s in range(NSEQ)]
        r_wdnb = [Res(f"wdnb{d}") for d in range(8)]

        ones1024 = cmat[:, 0:128]
        ones128 = cmat[:, 128:256]
        ones1 = cmat[:, 256:384]
        pmatT = cmat[:, 384:512]

        def vcol(i):
            return vecs[:, i:i + 1]

        bank_ap = lambda i, h: PS[i][:, h * 512:(h + 1) * 512]

        ring_state = {"n": 0}

        ring_pinned = set()

        def load_slot(src_ap, L, reads=(), nb=None):
            while True:
                s = ring_state["n"] % NSLOT
                ring_state["n"] += 1
                if s not in ring_pinned:
                    break
            if nb is None:
                nb = L // 1024 if L % 1024 == 0 else L // 704
            bl = L // nb
            S.dma("pool", lambda e, s=s: e.dma_start(
                out=ring[s][:, 0:L].rearrange("p (a b) -> p a b", b=bl),
                in_=src_ap.rearrange("p (a b) -> p a b", b=bl)), f"ring{s}", reads=list(reads), writes=[r_ring[s]])
            return s

        bank_state = {"n": 0}

        bank_reserved = set()

        def next_bank():
            while True:
                n = bank_state["n"] % 8
                bank_state["n"] += 1
                if (n // 2, n % 2) not in bank_reserved:
                    return n // 2, n % 2

        def mm_group(out_ap, out_res, pairs, reads, inc_last=True):
            n = len(pairs)
            for idx, (l, r) in enumerate(pairs):
                S.op("pe", lambda e, l=l, r=r, idx=idx: e.matmul(out_ap, lhsT=l, rhs=r, start=(idx == 0), stop=(idx == n - 1)),
                     reads=reads, writes=[out_res], inc=(inc_last and idx == n - 1))

        def rstd_from(ps_ap, ps_res, out_ap, out_res):
            S.op("act", lambda e: e.activation(out=out_ap, in_=ps_ap, func=AF.Ln, bias=epsb[:, 0:1]),
                 reads=[ps_res, *r_consts], writes=[out_res])
            S.op("act", lambda e: e.activation(out=out_ap, in_=out_ap, func=AF.Exp, scale=-0.5),
                 reads=[out_res], writes=[out_res])

        S.op("dve", lambda e: e.memset(epsb[:], EPS), writes=[r_consts[0]])
        S.dma("sp", lambda e: e.dma_start(out=vecs[:], in_=vec_d[:, :]), "ld_c0", writes=[r_consts[1]])
        S.dma("sp", lambda e: e.dma_start(out=tabs[:], in_=tab_d[:, :]), "ld_c1", writes=[r_tabs])
        S.dma("pool", lambda e: e.dma_start(out=cmat[:], in_=cm_d[:, :]), "ld_c2", writes=[r_consts[2]])
        Ctab = tabs[:, 0:SEQ]
        Stab = tabs[:, SEQ:2 * SEQ]

        out_evs = []
        qk_ctr = {"n": 0}

        qk_pend = {"p": None}

        def qk_pre(pairs, reads):
            k = qk_ctr["n"] % 2
            qk_ctr["n"] += 1
            base = k * 4
            bi, bh = next_bank()
            bank_reserved.add((bi, bh))
            ps_ap, ps_res = bank_ap(bi, bh), r_bank[bi][bh]
            mm_group(ps_ap, ps_res, pairs, reads)
            qb = M[:, base * 1024: base * 1024 + 512]
            sq = M[:, base * 1024 + 512: base * 1024 + 1024]
            S.op("act", lambda e: e.activation(out=qb, in_=ps_ap, func=AF.Copy), reads=[ps_res], writes=[r_Mb[base]])
            S.op("act", lambda e: e.activation(out=sq, in_=ps_ap, func=AF.Square), reads=[ps_res], writes=[r_Mb[base]])
            return base, bi, bh

        def qk_fin(st, gcol, gpcol, dest_ap, dest_res, tt):
            base, pi_, ph_ = st
            ps_ap, ps_res = bank_ap(pi_, ph_), r_bank[pi_][ph_]
            qb = M[:, base * 1024: base * 1024 + 512]
            sq = M[:, base * 1024 + 512: base * 1024 + 1024]
            rs = M_f[:, (base + 1) * 512:(base + 2) * 512]
            t1 = M_f[:, (base + 2) * 512:(base + 3) * 512]
            t2 = M_f[:, (base + 3) * 512:(base + 4) * 512]
            r_qb = r_sq = r_Mb[base]
            r_rs, r_t1, r_t2 = r_Mb[base + 1], r_Mb[base + 2], r_Mb[base + 3]
            bi, bh = next_bank()
            mm_group(bank_ap(bi, bh), r_bank[bi][bh], [(ones128, sq)], [r_sq, *r_consts])
            ci, ch = next_bank()
            mm_group(bank_ap(ci, ch), r_bank[ci][ch], [(pmatT, qb)], [r_qb, *r_consts])
            rstd_from(bank_ap(bi, bh), r_bank[bi][bh], rs, r_rs)
            sl = slice(tt * 512, (tt + 1) * 512)
            S.op("dve", lambda e: e.scalar_tensor_tensor(out=t1, in0=ps_ap, scalar=vcol(gcol), in1=Ctab[:, sl],
                                                         op0=ALU.mult, op1=ALU.mult),
                 reads=[ps_res, *r_consts, r_tabs], writes=[r_t1])
            S.op("dve", lambda e: e.scalar_tensor_tensor(out=t2, in0=bank_ap(ci, ch), scalar=vcol(gpcol), in1=Stab[:, sl],
                                                         op0=ALU.mult, op1=ALU.mult),
                 reads=[r_bank[ci][ch], *r_consts, r_tabs], writes=[r_t2])
            S.op("dve", lambda e: e.tensor_tensor(out=t1, in0=t1, in1=t2, op=ALU.add), reads=[r_t1, r_t2], writes=[r_t1])
            S.op("dve", lambda e: e.tensor_tensor(out=dest_ap, in0=t1, in1=rs, op=ALU.mult),
                 reads=[r_t1, r_rs], writes=[dest_res])
            bank_reserved.discard((pi_, ph_))

        def qk_item(pairs, reads, gcol, gpcol, dest_ap, dest_res, tt):
            st = qk_pre(pairs, reads)
            if qk_pend["p"] is not None:
                qk_fin(*qk_pend["p"])
            qk_pend["p"] = (st, gcol, gpcol, dest_ap, dest_res, tt)

        def qk_flush():
            if qk_pend["p"] is not None:
                qk_fin(*qk_pend["p"])
                qk_pend["p"] = None

        def norm_tile(xt_f, xt_res, g0, dest_ap_fn, dest_res, sq_b, sq_res, rs_ap, rs_res):
            S.op("act", lambda e: e.activation(out=sq_b, in_=xt_f, func=AF.Square), reads=xt_res, writes=sq_res)
            bi, bh = next_bank()
            mm_group(bank_ap(bi, bh), r_bank[bi][bh],
                     [(ones1024, sq_b[:, c * 512:(c + 1) * 512]) for c in range(8)], list(sq_res) + [*r_consts])
            rstd_from(bank_ap(bi, bh), r_bank[bi][bh], rs_ap, rs_res)
            for c in range(8):
                S.op("dve", lambda e, c=c: e.scalar_tensor_tensor(
                    out=dest_ap_fn(c), in0=xt_f[:, c * 512:(c + 1) * 512], scalar=vcol(g0 + c), in1=rs_ap,
                    op0=ALU.mult, op1=ALU.mult),
                    reads=list(xt_res) + [rs_res, *r_consts], writes=[dest_res])

        def emit_ssq(d, sbi, sbh, sqd):
            S.op("pe", lambda e: e.matmul(bank_ap(sbi, sbh), lhsT=ones1024, rhs=sqd[:, d * 512:(d + 1) * 512],
                                          start=(d == 0), stop=(d == 7)),
                 reads=[r_T[1][d // 2], *r_consts], writes=[r_bank[sbi][sbh]], inc=True)

        x_bufs = [(M_f[:, 0:4096], r_Mb[0:8]),
                  (O_f[:, 4096:8192], [r_O[h][t] for h in range(4, 8) for t in range(TT)]),
                  (R1_f[:, 4096:8192], r_R1[4:8]),
                  (O_f[:, 0:4096], [r_O[h][t] for h in range(4) for t in range(TT)])]

        def emit_x_loads(sq_, eng):
            for tt in range(TT):
                xt, xres = x_bufs[tt]
                S.dma(eng, lambda e, xt=xt, sq_=sq_, tt=tt: e.dma_start(out=xt, in_=xT_d[sq_, tt]), f"ld_x{tt}_{eng}",
                      reads=(list(x_bufs[tt - 1][1]) if (tt > 0 and eng == "pool") else []), writes=xres)

        for s in range(NSEQ):
            if s == 0:
                emit_x_loads(0, "sp")
            for tt in range(TT):
                xt, xres = x_bufs[tt]
                sqb = T_b[1][:, 0:4096]
                norm_tile(xt, xres, V_PRE,
                          lambda c, tt=tt: A[:, c * SEQ + tt * 512: c * SEQ + (tt + 1) * 512], r_A[tt],
                          sqb, r_T[1][0:4], T[1][:, 2560:3072], r_T[1][5])

            kT = lambda kvh: R1[:, kvh * SEQ:(kvh + 1) * SEQ]
            Vt = R1[:, 2 * SEQ:4 * SEQ]
            sl_kv = load_slot(wqkv_d[2], 4096)
            rg = ring[sl_kv]
            for tt in range(TT):
                for kvh in range(2):
                    qk_item([(rg[:, c * 512 + kvh * 128: c * 512 + (kvh + 1) * 128],
                              A[:, c * SEQ + tt * 512: c * SEQ + (tt + 1) * 512]) for c in range(8)],
                            [r_ring[sl_kv], r_A[tt]], V_KG, V_KGP,
                            kT(kvh)[:, tt * 512:(tt + 1) * 512], r_R1[kvh], tt)
            for kt2 in range(8):
                bi, bh = next_bank()
                for q in range(2):
                    kt = kt2 * 2 + q
                    mm_group(PS[bi][:, bh * 512 + q * 256: bh * 512 + (q + 1) * 256], r_bank[bi][bh],
                             [(A[:, c * SEQ + kt * 128: c * SEQ + (kt + 1) * 128],
                               rg[:, c * 512 + 256: c * 512 + 512]) for c in range(8)],
                             [r_ring[sl_kv], r_A[kt // 4]])
                S.op("act", lambda e, bi=bi, bh=bh, kt2=kt2: e.activation(
                    out=Vt[:, kt2 * 512:(kt2 + 1) * 512], in_=bank_ap(bi, bh), func=AF.Copy),
                    reads=[r_bank[bi][bh]], writes=[r_R1[2 + kt2 // 4]])

            def p_ap(j):
                return R1[:, (4 + j) * SEQ:(5 + j) * SEQ] if j < 4 else M[:, j * SEQ:(j + 1) * SEQ]

            def p_res(j):
                return [r_R1[4 + j]] if j < 4 else [r_Mb[2 * j], r_Mb[2 * j + 1]]

            S.op("dve", lambda e: e.memset(T[0][:, 0:1], 0.0), writes=[r_T[0][0]])
            S.op("dve", lambda e: e.memset(T[0][:, 2049:2050], 0.0), writes=[r_T[0][4]])
            c_wrow = T[0]
            c_yrow = T[0][:, 2560:4608]
            c_brow = T[1][:, 2560:4608]
            sl_q = None
            for j in range(8):
                if j % 4 == 0:
                    if sl_q is not None:
                        ring_pinned.discard(sl_q)
                    sl_q = load_slot(wqkv_d[j // 4], 4096)
                    ring_pinned.add(sl_q)
                rgq = ring[sl_q]
                hq = j % 4
                sl = load_slot(wubc_d[j], 3072)
                rg = ring[sl]
                for tt in range(TT):
                    qk_item([(rgq[:, c * 512 + hq * 128: c * 512 + (hq + 1) * 128],
                              A[:, c * SEQ + tt * 512: c * SEQ + (tt + 1) * 512]) for c in range(8)],
                            [r_ring[sl_q], r_A[tt]], V_QG, V_QGP,
                            O[:, j * SEQ + tt * 512: j * SEQ + (tt + 1) * 512], r_O[j][tt], tt)
                    bks = []
                    for k in (2, 0, 1):
                        bi, bh = next_bank()
                        mm_group(bank_ap(bi, bh), r_bank[bi][bh],
                                 [(rg[:, c * 384 + k * 128: c * 384 + (k + 1) * 128],
                                   A[:, c * SEQ + tt * 512: c * SEQ + (tt + 1) * 512]) for c in range(8)],
                                 [r_ring[sl], r_A[tt]])
                        bks.append((bi, bh))
                    (cbi, cbh), (ubi, ubh), (bbi, bbh) = bks
                    cs = T[1][:, (tt % 2) * 512:(tt % 2 + 1) * 512]
                    r_cs = r_T[1][tt % 2]
                    S.op("act", lambda e, cs=cs, cbi=cbi, cbh=cbh: e.activation(out=cs, in_=bank_ap(cbi, cbh), func=AF.Copy),
                         reads=[r_bank[cbi][cbh]], writes=[r_cs])
                    S.op("dve", lambda e, cs=cs, ubi=ubi, ubh=ubh, tt=tt: e.tensor_tensor(
                        out=c_wrow[:, 1 + tt * 512: 1 + (tt + 1) * 512], in0=bank_ap(ubi, ubh), in1=cs, op=ALU.mult),
                        reads=[r_bank[ubi][ubh], r_cs], writes=[r_T[0][tt], r_T[0][tt + 1]])
                    S.op("act", lambda e, bbi=bbi, bbh=bbh, tt=tt: e.activation(
                        out=c_brow[:, tt * 512:(tt + 1) * 512], in_=bank_ap(bbi, bbh), func=AF.Copy),
                        reads=[r_bank[bbi][bbh]], writes=[r_T[1][5 + tt]])
                rw = r_T[0][0:5]
                ry = r_T[0][5:9]
                S.op("dve", lambda e, j=j: e.tensor_scalar(out=c_yrow, in0=c_wrow[:, 0:2048], scalar1=vcol(V_MCW + j), scalar2=None,
                                                           op0=ALU.mult), reads=rw + [*r_consts], writes=ry)
                S.op("dve", lambda e, j=j: e.scalar_tensor_tensor(out=c_yrow, in0=c_wrow[:, 1:2049], scalar=vcol(V_MCW + 8 + j),
                                                                  in1=c_yrow, op0=ALU.mult, op1=ALU.add),
                     reads=rw + ry + [*r_consts], writes=ry)
                S.op("dve", lambda e, j=j: e.scalar_tensor_tensor(out=c_yrow, in0=c_wrow[:, 2:2050], scalar=vcol(V_MCW + 16 + j),
                                                                  in1=c_yrow, op0=ALU.mult, op1=ALU.add),
                     reads=rw + ry + [*r_consts], writes=ry)
                S.op("dve", lambda e, j=j: e.tensor_tensor(out=p_ap(j), in0=c_yrow, in1=c_brow, op=ALU.mult),
                     reads=ry + r_T[1][5:9], writes=p_res(j))
            qk_flush()
            ring_pinned.discard(sl_q)

            if s == 0:
                for d in range(8):
                    S.dma("pool", lambda e, d=d: e.dma_start(
                        out=wdnb_d[d].rearrange("p (a b) -> p a b", b=704),
                        in_=wdn_d[d].rearrange("p (a b) -> p a b", b=704)), f"cv_wdn{d}", reads=[r_O[7][3]], writes=[r_wdnb[d]])

            pairs = [(hh, m2) for hh in range(NH) for m2 in range(2)]

            def actx(p):
                hh, m2 = pairs[p]
                qts = (2 * m2, 2 * m2 + 1)
                return dict(kvh=hh // 4,
                            qaps=[O[:, hh * SEQ + qt * 512: hh * SEQ + (qt + 1) * 512] for qt in qts],
                            r_qs=[r_O[hh][qt] for qt in qts])

            def emit_qk(cx, st, kp):
                kvh, qap = cx["kvh"], cx["qaps"][st]
                for q in range(2):
                    kt = kp * 2 + q
                    S.op("pe", lambda e, st=st, q=q, kt=kt, kvh=kvh, qap=qap: e.matmul(
                        PS[st][:, q * 512:(q + 1) * 512], lhsT=kT(kvh)[:, kt * 128:(kt + 1) * 128], rhs=qap,
                        start=True, stop=True),
                        reads=[r_R1[kvh], cx["r_qs"][st]], writes=[r_bank[st][0], r_bank[st][1]], inc=(q == 1))

            def pt_buf(st, kp):
                b = st * 2 + (kp % 2)
                return T_b[0][:, b * 1024:(b + 1) * 1024], r_T[0][b]

            def pa_buf(st, kp):
                b = 4 + st * 2 + (kp % 2)
                return T_b[0][:, b * 1024:b * 1024 + 512], r_T[0][b]

            def emit_exp(st, kp):
                PT, r_PT = pt_buf(st, kp)
                S.op("act", lambda e, PT=PT, st=st: e.activation(out=PT, in_=PS[st][:, :], func=AF.Exp, scale=SCALE),
                     reads=[r_bank[st][0], r_bank[st][1]], writes=[r_PT])
                pa, r_pa = pa_buf(st, kp)
                S.op("dve", lambda e, PT=PT, pa=pa: e.tensor_tensor(out=pa, in0=PT[:, 0:512], in1=PT[:, 512:1024],
                                                                   op=ALU.add), reads=[r_PT], writes=[r_pa])

            def emit_pv(cx, st, kp):
                PT, r_PT = pt_buf(st, kp)
                oacc = bank_ap(2, st)
                kvh = cx["kvh"]
                for q in range(2):
                    kt = kp * 2 + q
                    S.op("pe", lambda e, q=q, kt=kt, oacc=oacc, kvh=kvh, PT=PT: e.matmul(
                        oacc, lhsT=Vt[:, kt * 256 + kvh * 128: kt * 256 + (kvh + 1) * 128],
                        rhs=PT[:, q * 512:(q + 1) * 512], start=(kt == 0), stop=(kt == 15)),
                        reads=[r_PT, r_R1[2 + kt // 8]], writes=[r_bank[2][st]], inc=(q == 1))

            def emit_den(st, kp):
                pa, r_pa = pa_buf(st, kp)
                den = bank_ap(3, st)
                S.op("pe", lambda e, pa=pa, den=den, kp=kp: e.matmul(den, lhsT=ones1, rhs=pa, start=(kp == 0), stop=(kp == 7)),
                     reads=[r_pa, *r_consts], writes=[r_bank[3][st]], inc=True)

            def emit_epilogue(cx, st):
                rd = T[1][:, st * 512:(st + 1) * 512]
                r_rd = r_T[1][st]
                S.op("act", lambda e, rd=rd, st=st: e.activation(out=rd, in_=bank_ap(3, st), func=AF.Ln),
                     reads=[r_bank[3][st]], writes=[r_rd])
                S.op("act", lambda e, rd=rd: e.activation(out=rd, in_=rd, func=AF.Exp, scale=-1.0),
                     reads=[r_rd], writes=[r_rd])
                S.op("dve", lambda e, rd=rd, st=st, qap=cx["qaps"][st]: e.tensor_tensor(
                    out=qap, in0=bank_ap(2, st), in1=rd, op=ALU.mult),
                    reads=[r_bank[2][st], r_rd], writes=[cx["r_qs"][st]])

            cx = actx(0)
            for st in range(2):
                emit_qk(cx, st, 0)
            for p in range(len(pairs)):
                cx = actx(p)
                cx_next = actx(p + 1) if p + 1 < len(pairs) else None
                for kp in range(8):
                    if kp == 0 and p >= 1:
                        cxp = actx(p - 1)
                        emit_epilogue(cxp, 0)
                        emit_exp(0, 0)
                        emit_exp(1, 0)
                        emit_epilogue(cxp, 1)
                    else:
                        for st in range(2):
                            emit_exp(st, kp)
                    for st in range(2):
                        if kp + 1 < 8:
                            emit_qk(cx, st, kp + 1)
                        elif cx_next is not None:
                            emit_qk(cx_next, st, 0)
                        emit_pv(cx, st, kp)
                        if kp >= 1:
                            emit_den(st, kp - 1)
                        if kp == 7:
                            emit_den(st, 7)
            cxl = actx(len(pairs) - 1)
            emit_epilogue(cxl, 0)
            emit_epilogue(cxl, 1)

            def mg_ap(j, tt):
                if j < 4:
                    return R1[:, j * SEQ + tt * 512: j * SEQ + (tt + 1) * 512]
                return M[:, (j - 4) * SEQ + tt * 512:(j - 4) * SEQ + (tt + 1) * 512]

            def mg_res(j, tt):
                return [r_R1[j]] if j < 4 else [r_Mb[((j - 4) * 4 + tt) // 2]]

            def p_tile_res(c, tt):
                return [r_R1[4 + c]] if c < 4 else [r_Mb[(c * 4 + tt) // 2]]

            for j in range(8):
                sl = load_slot(wpj_d[j], 4096)
                rg = ring[sl]
                for tt in range(TT):
                    tsl = lambda c, tt=tt: slice(c * SEQ + tt * 512, c * SEQ + (tt + 1) * 512)
                    src_fns = (
                        (lambda c, tt=tt: O[:, tsl(c)], lambda c, tt=tt: [r_O[c][tt]]),
                        (lambda c, tt=tt: p_ap(c)[:, tt * 512:(tt + 1) * 512], lambda c, tt=tt: p_tile_res(c, tt)),
                        (lambda c, tt=tt: A[:, tsl(c)], lambda c, tt=tt: [r_A[tt]]),
                        (lambda c, tt=tt: A[:, tsl(c)], lambda c, tt=tt: [r_A[tt]]),
                    )
                    bk = []
                    for k, (ap_fn, rs_fn) in enumerate(src_fns):
                        bi, bh = next_bank()
                        rds = [r_ring[sl]]
                        for c in range(8):
                            for r_ in rs_fn(c):
                                if r_ not in rds:
                                    rds.append(r_)
                        mm_group(bank_ap(bi, bh), r_bank[bi][bh],
                                 [(rg[:, c * 512 + k * 128: c * 512 + (k + 1) * 128], ap_fn(c)) for c in range(8)], rds)
                        bk.append((bi, bh))
                    sa = T[tt % 2][:, 0:512]
                    sbg = T[tt % 2][:, 512:1024]
                    m1 = T[tt % 2][:, 1024:1536]
                    r_sa, r_sb, r_m1 = r_T[tt % 2][0], r_T[tt % 2][1], r_T[tt % 2][2]
                    S.op("act", lambda e, sa=sa, b=bk[2], j=j: e.activation(out=sa, in_=bank_ap(*b), func=AF.Sigmoid,
                                                                             bias=vcol(V_GBA + j)),
                         reads=[r_bank[bk[2][0]][bk[2][1]], *r_consts], writes=[r_sa])
                    S.op("act", lambda e, sbg=sbg, b=bk[3], j=j: e.activation(out=sbg, in_=bank_ap(*b), func=AF.Sigmoid,
                                                                               bias=vcol(V_GBB + j)),
                         reads=[r_bank[bk[3][0]][bk[3][1]], *r_consts], writes=[r_sb])
                    S.op("dve", lambda e, sa=sa, m1=m1, b=bk[0]: e.tensor_tensor(out=m1, in0=bank_ap(*b), in1=sa, op=ALU.mult),
                         reads=[r_bank[bk[0][0]][bk[0][1]], r_sa], writes=[r_m1])
                    S.op("dve", lambda e, sbg=sbg, b=bk[1]: e.tensor_tensor(out=sbg, in0=bank_ap(*b), in1=sbg, op=ALU.mult),
                         reads=[r_bank[bk[1][0]][bk[1][1]], r_sb], writes=[r_sb])
                    S.op("dve", lambda e, sbg=sbg, m1=m1, j=j, tt=tt: e.tensor_tensor(
                        out=mg_ap(j, tt), in0=m1, in1=sbg, op=ALU.add),
                        reads=[r_m1, r_sb], writes=mg_res(j, tt))

            for blk in range(2):
                S.dma("pool", lambda e, blk=blk: e.dma_start(
                    out=O[:, blk * 4096:(blk + 1) * 4096].rearrange("p (a b) -> p a b", b=1024),
                    in_=wo_d[blk].rearrange("p (a b) -> p a b", b=1024)), f"ld_wo{blk}",
                    writes=[r_O[h][t] for h in (2 * blk, 2 * blk + 1) for t in range(TT)])
            wo_res = lambda d: [r_O[h][t] for h in (2 * (d // 4), 2 * (d // 4) + 1) for t in range(TT)]
            out_state = {}

            def mo_tile(tt):
                if tt % 2 == 0:
                    return T[0][:, 0:4096], r_T[0][0:8]
                return O_f[:, 4096:8192], [r_O[h][t] for h in range(4, 8) for t in range(TT)]

            def mo_buf(tt, d):
                if tt % 2 == 0:
                    return T[0][:, d * 512:(d + 1) * 512], [r_T[0][d]]
                return (O_f[:, 4096 + d * 512: 4096 + (d + 1) * 512],
                        [r_O[4 + d // 2][(d % 2) * 2], r_O[4 + d // 2][(d % 2) * 2 + 1]])

            SQB = (4, 7, 8)

            def out_ssq(tt, d):
                xt2, xres, sbi, sbh = out_state[tt]
                b = SQB[d % 3]
                S.op("pe", lambda e: e.matmul(bank_ap(sbi, sbh), lhsT=ones1024, rhs=T_b[1][:, b * 1024: b * 1024 + 512],
                                              start=(d == 0), stop=(d == 7)),
                     reads=[r_T[1][b], *r_consts], writes=[r_bank[sbi][sbh]], inc=True)

            def out_A(tt, d0, d1):
                hb = tt % 2
                xt2 = R1_f[:, 4096:8192] if hb == 0 else M_f[:, 4096:8192]
                xres = r_R1[4:8] if hb == 0 else r_Mb[8:16]
                if d0 == 0:
                    sbi, sbh = next_bank()
                    bank_reserved.add((sbi, sbh))
                    out_state[tt] = (xt2, xres, sbi, sbh)
                if d1 == 8:
                    S.dma("sp", lambda e, xt2=xt2, s=s, tt=tt: e.dma_start(out=xt2, in_=xT_d[s, tt]), f"ld_y{hb}", writes=xres)
                for d in range(d0, d1):
                    bi, bh = next_bank()
                    mm_group(bank_ap(bi, bh), r_bank[bi][bh],
                             [(O[:, (d // 4) * 4096 + c * 512 + (d % 4) * 128: (d // 4) * 4096 + c * 512 + (d % 4 + 1) * 128],
                               mg_ap(c, tt)) for c in range(8)],
                             wo_res(d) + [r_ for c in range(8) for r_ in mg_res(c, tt)])
                    if d >= 1:
                        out_ssq(tt, d - 1)
                    mo_ap, mo_res = mo_buf(tt, d)
                    S.op("act", lambda e, mo_ap=mo_ap, bi=bi, bh=bh, d=d: e.activation(
                        out=mo_ap, in_=bank_ap(bi, bh), func=AF.Copy, scale=vcol(V_POST + d)),
                        reads=[r_bank[bi][bh], *r_consts], writes=mo_res)
                    b = SQB[d % 3]
                    S.op("act", lambda e, b=b, bi=bi, bh=bh: e.activation(out=T_b[1][:, b * 1024: b * 1024 + 512],
                                                                          in_=bank_ap(bi, bh), func=AF.Square),
                         reads=[r_bank[bi][bh]], writes=[r_T[1][b]])
                if d1 == 8:
                    out_ssq(tt, 7)

            def out_B1(tt):
                xt2, xres, sbi, sbh = out_state[tt]
                hb = tt % 2
                rs = T[1][:, 2560:3072]
                rstd_from(bank_ap(sbi, sbh), r_bank[sbi][sbh], rs, r_T[1][5])
                bank_reserved.discard((sbi, sbh))
                mt, mt_res = mo_tile(tt)
                mt3 = mt.rearrange("p (c n) -> p c n", n=512)
                rs3 = rs.unsqueeze(1).to_broadcast([128, 8, 512])
                S.op("dve", lambda e, mt3=mt3, rs3=rs3: e.tensor_tensor(out=mt3, in0=mt3, in1=rs3, op=ALU.mult),
                     reads=mt_res + [r_T[1][5]], writes=mt_res)
                S.op("dve", lambda e, mt=mt, xt2=xt2: e.tensor_tensor(out=xt2, in0=mt, in1=xt2, op=ALU.add),
                     reads=mt_res + xres, writes=xres)
                S.dma("sp", lambda e, xt2=xt2, s=s, tt=tt: e.dma_start(out=x1_d[s, tt], in_=xt2), f"st_x1{hb}",
                      reads=xres, writes=[r_x1d[s][tt]])

            def out_C1(tt):
                xt2, xres, sbi, sbh = out_state[tt]
                S.op("act", lambda e, xt2=xt2: e.activation(out=T_b[1][:, 0:4096], in_=xt2, func=AF.Square),
                     reads=xres, writes=r_T[1][0:4])

            def out_C2(tt):
                bi, bh = next_bank()
                bank_reserved.add((bi, bh))
                mm_group(bank_ap(bi, bh), r_bank[bi][bh],
                         [(ones1024, T_b[1][:, c * 512:(c + 1) * 512]) for c in range(8)], r_T[1][0:4] + [*r_consts])
                out_state[("c", tt)] = (bi, bh)

            def out_C3(tt):
                xt2, xres, sbi, sbh = out_state[tt]
                bi, bh = out_state[("c", tt)]
                rs2 = T[1][:, 3072:3584]
                rstd_from(bank_ap(bi, bh), r_bank[bi][bh], rs2, r_T[1][6])
                bank_reserved.discard((bi, bh))
                for c in range(8):
                    S.op("dve", lambda e, c=c, xt2=xt2, tt=tt: e.scalar_tensor_tensor(
                        out=A[:, c * SEQ + tt * 512: c * SEQ + (tt + 1) * 512], in0=xt2[:, c * 512:(c + 1) * 512],
                        scalar=vcol(V_FPRE + c), in1=rs2, op0=ALU.mult, op1=ALU.mult),
                        reads=list(xres) + [r_T[1][6], *r_consts], writes=[r_A[tt]])

            out_A(0, 0, 8)
            out_B1(0)
            out_A(1, 0, 8)
            out_C1(0)
            out_B1(1)
            for tt in range(2, TT):
                out_A(tt, 0, 4)
                out_C2(tt - 2)
                out_C3(tt - 2)
                out_A(tt, 4, 8)
                out_C1(tt - 1)
                out_B1(tt)
            out_C2(TT - 2)
            out_C3(TT - 2)
            out_C1(TT - 1)
            out_C2(TT - 1)
            out_C3(TT - 1)

            hid_buf = lambda j: (O, R1, M)[j // 8]

            def hid_res(j, tt):
                if j < 8:
                    return [r_O[j][tt]]
                if j < 16:
                    return [r_R1[j % 8]]
                return [r_Mb[((j % 8) * 4 + tt) // 2]]

            for i in range(2):
                S.op("dve", lambda e, i=i: e.memset(T[i][:, 0:1], 0.0), writes=[r_T[i][0]])
                S.op("dve", lambda e, i=i: e.memset(T[i][:, 2049:2050], 0.0), writes=[r_T[i][4]])
            for g in range(11):
                sl = load_slot(wup_d[g], 4096)
                rg = ring[sl]
                for jj in range(2):
                    j = 2 * g + jj
                    ti = j % 2
                    arow = T[ti]
                    yrow = T[ti][:, 2560:4608]
                    ra = r_T[ti][0:5]
                    ry = r_T[ti][5:9]
                    for tt in range(TT):
                        bi, bh = next_bank()
                        mm_group(bank_ap(bi, bh), r_bank[bi][bh],
                                 [(rg[:, c * 512 + jj * 128: c * 512 + (jj + 1) * 128],
                                   A[:, c * SEQ + tt * 512: c * SEQ + (tt + 1) * 512]) for c in range(8)],
                                 [r_ring[sl], r_A[tt]])
                        S.op("act", lambda e, bi=bi, bh=bh, tt=tt, arow=arow: e.activation(
                            out=arow[:, 1 + tt * 512: 1 + (tt + 1) * 512], in_=bank_ap(bi, bh), func=AF.Copy),
                            reads=[r_bank[bi][bh]], writes=[r_T[ti][tt], r_T[ti][tt + 1]])
                    bb = []
                    for tt in range(TT):
                        bi, bh = next_bank()
                        mm_group(bank_ap(bi, bh), r_bank[bi][bh],
                                 [(rg[:, c * 512 + 256 + jj * 128: c * 512 + 256 + (jj + 1) * 128],
                                   A[:, c * SEQ + tt * 512: c * SEQ + (tt + 1) * 512]) for c in range(8)],
                                 [r_ring[sl], r_A[tt]])
                        bb.append((bi, bh))
                    S.op("dve", lambda e, j=j, arow=arow, yrow=yrow: e.tensor_scalar(
                        out=yrow, in0=arow[:, 0:2048], scalar1=vcol(V_FCW + j), scalar2=None, op0=ALU.mult),
                        reads=ra + [*r_consts], writes=ry)
                    S.op("dve", lambda e, j=j, arow=arow, yrow=yrow: e.scalar_tensor_tensor(
                        out=yrow, in0=arow[:, 1:2049], scalar=vcol(V_FCW + NJ + j), in1=yrow, op0=ALU.mult, op1=ALU.add),
                        reads=ra + ry + [*r_consts], writes=ry)
                    S.op("dve", lambda e, j=j, arow=arow, yrow=yrow: e.scalar_tensor_tensor(
                        out=yrow, in0=arow[:, 2:2050], scalar=vcol(V_FCW + 2 * NJ + j), in1=yrow, op0=ALU.mult, op1=ALU.add),
                        reads=ra + ry + [*r_consts], writes=ry)
                    S.op("act", lambda e, yrow=yrow: e.activation(out=yrow, in_=yrow, func=AF.Gelu_apprx_tanh),
                         reads=ry, writes=ry)
                    hb_ = hid_buf(j)
                    for tt in range(TT):
                        bi, bh = bb[tt]
                        S.op("dve", lambda e, bi=bi, bh=bh, tt=tt, yrow=yrow, hb_=hb_, j=j: e.tensor_tensor(
                            out=hb_[:, (j % 8) * SEQ + tt * 512:(j % 8) * SEQ + (tt + 1) * 512],
                            in0=bank_ap(bi, bh), in1=yrow[:, tt * 512:(tt + 1) * 512], op=ALU.mult),
                            reads=[r_bank[bi][bh], r_T[ti][5 + tt]], writes=hid_res(j, tt))

            def o2_buf(tt, d):
                if tt % 2 == 0:
                    return T[0][:, d * 512:(d + 1) * 512], [r_T[0][d]]
                if d < 4:
                    return M_f[:, (12 + d) * 512:(13 + d) * 512], [r_Mb[12 + d]]
                b = 4 if d == 7 else 6 + (d - 4)
                return T[1][:, b * 512:(b + 1) * 512], [r_T[1][b]]

            r_x1h = [Res(f"x1h{s}_0"), Res(f"x1h{s}_1")]
            for tt in range(TT):
                hb = tt % 2
                x1t = A_f[:, hb * 4096:(hb + 1) * 4096]
                x1res = [r_x1h[hb]]
                S.dma("sp", lambda e, x1t=x1t, s=s, tt=tt: e.dma_start(out=x1t, in_=x1_d[s, tt]), f"ld_x1{hb}",
                      reads=[r_x1d[s][tt]], writes=x1res + (r_A if tt < 2 else []))
                sqd = T_b[1][:, 0:4096]
                sbi, sbh = next_bank()
                for d in range(8):
                    sl = load_slot(wdnb_d[d], 2816, reads=[r_wdnb[d]], nb=1)
                    rg = ring[sl]
                    bi, bh = next_bank()
                    if (bi, bh) == (sbi, sbh):
                        bi, bh = next_bank()
                    rds = [r_ring[sl]]
                    for j in range(NJ):
                        for r_ in hid_res(j, tt):
                            if r_ not in rds:
                                rds.append(r_)
                    mm_group(bank_ap(bi, bh), r_bank[bi][bh],
                             [(rg[:, j * 128:(j + 1) * 128],
                               hid_buf(j)[:, (j % 8) * SEQ + tt * 512:(j % 8) * SEQ + (tt + 1) * 512]) for j in range(NJ)], rds)
                    if d >= 1:
                        emit_ssq(d - 1, sbi, sbh, sqd)
                    o2_ap, o2_res = o2_buf(tt, d)
                    S.op("act", lambda e, o2_ap=o2_ap, bi=bi, bh=bh: e.activation(out=o2_ap, in_=bank_ap(bi, bh), func=AF.Copy),
                         reads=[r_bank[bi][bh]], writes=o2_res)
                    S.op("act", lambda e, d=d, bi=bi, bh=bh: e.activation(out=sqd[:, d * 512:(d + 1) * 512], in_=bank_ap(bi, bh),
                                                                          func=AF.Square),
                         reads=[r_bank[bi][bh]], writes=[r_T[1][d // 2]])
                emit_ssq(7, sbi, sbh, sqd)
                if tt == TT - 1 and s + 1 < NSEQ:
                    emit_x_loads(s + 1, "pool")
                rs = T[1][:, 2560:3072]
                rstd_from(bank_ap(sbi, sbh), r_bank[sbi][sbh], rs, r_T[1][5])
                for d in range(8):
                    o2_ap, o2_res = o2_buf(tt, d)
                    S.op("dve", lambda e, d=d, o2_ap=o2_ap: e.scalar_tensor_tensor(
                        out=o2_ap, in0=o2_ap, scalar=vcol(V_FPOST + d), in1=rs,
                        op0=ALU.mult, op1=ALU.mult), reads=o2_res + [r_T[1][5], *r_consts], writes=o2_res)
                    S.op("dve", lambda e, d=d, x1t=x1t, o2_ap=o2_ap: e.tensor_tensor(
                        out=x1t[:, d * 512:(d + 1) * 512], in0=o2_ap, in1=x1t[:, d * 512:(d + 1) * 512],
                        op=ALU.add), reads=o2_res + x1res, writes=x1res)
                out_evs.append(S.dma("sp", lambda e, x1t=x1t, s=s, tt=tt: e.dma_start(out=yT_d[s, tt], in_=x1t),
                                     f"st_y{hb}", reads=x1res + r_A))

        S.wait_all("sp", out_evs)
        S.build(nc, es)
    return nc


def _host_layout(inp):
    f = lambda a: np.ascontiguousarray(np.asarray(a, dtype=np.float32))
    w_in = f(inp["w_in"])[0]
    w_ap = f(inp["w_attn_proj"])[0]
    w_cp = f(inp["w_conv_proj"])[0]
    w_out = f(inp["w_out"])[0]
    w_up = f(inp["w_up"])[0]
    w_down = f(inp["w_down"])[0]
    sh = {}
    sh["w_qkv"] = f(w_in[:, 0:1536].reshape(8, 128, 3, 512).transpose(2, 1, 0, 3).reshape(3, 128, 4096))
    sh["w_ubc"] = f(w_in[:, 1536:4608].reshape(8, 128, 3, 8, 128).transpose(3, 1, 0, 2, 4).reshape(8, 128, 3072))
    W4 = np.stack([w_ap, w_cp, w_in[:, 4608:5632], w_in[:, 5632:6656]], axis=0)
    sh["w_pj"] = f(W4.reshape(4, 8, 128, 8, 128).transpose(3, 2, 1, 0, 4).reshape(8, 128, 4096))
    sh["w_o"] = f(w_out.reshape(8, 128, 2, 512).transpose(2, 1, 0, 3).reshape(2, 128, 4096))
    sh["w_upg"] = f(w_up.reshape(8, 128, 2, 11, 2, 128).transpose(3, 1, 0, 2, 4, 5).reshape(11, 128, 4096))
    sh["w_dn"] = f(w_down.reshape(22, 128, 8, 128).transpose(2, 1, 0, 3).reshape(8, 128, 2816))
    vec = np.zeros((128, NV), np.float32)
    col8 = lambda v: f(v).reshape(8, 128).T
    vec[:, V_PRE:V_PRE + 8] = col8(inp["mix_pre_g"][0])
    vec[:, V_POST:V_POST + 8] = col8(inp["mix_post_g"][0])
    vec[:, V_FPRE:V_FPRE + 8] = col8(inp["ffn_pre_g"][0])
    vec[:, V_FPOST:V_FPOST + 8] = col8(inp["ffn_post_g"][0])
    gb = f(inp["gate_b"])[0]
    vec[:, V_GBA:V_GBA + 8] = col8(gb[:D])
    vec[:, V_GBB:V_GBB + 8] = col8(gb[D:])
    perm = np.arange(128)
    perm = np.where((perm % 64) < 32, perm + 32, perm - 32)
    qg = f(inp["q_norm_g"])[0]
    kg = f(inp["k_norm_g"])[0]
    vec[:, V_QG] = qg
    vec[:, V_QGP] = qg[perm]
    vec[:, V_KG] = kg
    vec[:, V_KGP] = kg[perm]
    mcw = f(inp["mix_conv_w"])[0]
    for k in range(3):
        vec[:, V_MCW + 8 * k:V_MCW + 8 * (k + 1)] = mcw[k].reshape(8, 128).T
    fcw = f(inp["ffn_conv_w"])[0]
    for k in range(3):
        vec[:, V_FCW + NJ * k:V_FCW + NJ * (k + 1)] = fcw[k].reshape(NJ, 128).T
    sh["vecs"] = vec
    t = np.arange(SEQ)
    row_pos = (t // 64).astype(np.float64)
    col_pos = (t % 64).astype(np.float64)
    freqs = 10000.0 ** (-(np.arange(32, dtype=np.float64) / 32.0))
    tabs = np.zeros((128, 2 * SEQ), np.float32)
    for d in range(128):
        pos = row_pos if d < 64 else col_pos
        ang = pos * freqs[d % 32]
        tabs[d, 0:SEQ] = np.cos(ang)
        tabs[d, SEQ:] = np.sin(ang)
    sh["tabs"] = tabs
    cm = np.zeros((128, 512), np.float32)
    cm[:, 0:128] = 1.0 / 1024.0
    cm[:, 128:256] = 1.0 / 128.0
    cm[:, 256:384] = 1.0
    for m in range(128):
        if (m % 64) < 32:
            cm[m + 32, 384 + m] = -1.0
        else:
            cm[m - 32, 384 + m] = 1.0
    sh["cmat"] = cm
    return sh


_NC_CACHE = {}


def kernel(**inputs):
    x = np.asarray(inputs["x"], dtype=np.float32)
    shared = _host_layout(inputs)
    in_maps = []
    for core in range(NCORES):
        xs = x[core * NSEQ:(core + 1) * NSEQ]
        xT = np.ascontiguousarray(xs.reshape(NSEQ, TT, 512, 8, 128).transpose(0, 1, 4, 3, 2)).reshape(NSEQ, TT, 128, 4096)
        m = dict(shared)
        m["xT"] = xT
        in_maps.append(m)
    if "nc" not in _NC_CACHE:
        _NC_CACHE["nc"] = build_nc()
    nc = _NC_CACHE["nc"]
    res = run_bass_kernel_spmd(nc, in_maps, core_ids=list(range(NCORES)))
    outs = []
    for core in range(NCORES):
        yT = np.asarray(res.results[core]["yT"]).reshape(NSEQ, TT, 128, 8, 512)
        outs.append(yT.transpose(0, 1, 4, 3, 2).reshape(NSEQ, SEQ, D))
    return np.ascontiguousarray(np.concatenate(outs, axis=0).astype(np.float32))
```

```python
import numpy as np
from contextlib import ExitStack
import concourse.bass as bass
import concourse.mybir as mybir
from concourse.bass_utils import run_bass_kernel_spmd

F32 = mybir.dt.float32
BF16 = mybir.dt.bfloat16
AF = mybir.ActivationFunctionType
ALU = mybir.AluOpType

NCORES = 8
SEQ = 2048
D = 1024
NSEQ = 2
TT = 4
HD = 128
NH = 8
DFF = 2816
NJ = 22
EPS = 1e-6
SCALE = HD ** -0.5
NSLOT = 3

V_PRE, V_POST, V_FPRE, V_FPOST, V_GBA, V_GBB = 0, 8, 16, 24, 32, 40
V_QG, V_QGP, V_KG, V_KGP = 48, 49, 50, 51
V_MCW = 52
V_FCW = 76
NV = 142


class Res:
    __slots__ = ("name", "w", "r", "x")

    def __init__(self, name, excl=False):
        self.name = name
        self.w = None
        self.r = []
        self.x = excl


class Sched:
    ENGS = ("pe", "act", "dve", "pool", "sp")

    def __init__(self):
        self.q = {e: [] for e in self.ENGS}
        self.cnt = {e: 0 for e in self.ENGS}
        self.seen = {e: {} for e in self.ENGS}
        self.dcnt = {}

    def _waits(self, eng, reads, writes):
        evs = []
        for r in reads:
            if r.w is not None:
                evs.append(r.w)
        same_ok = (eng == "pe")
        for w in writes:
            if w.w is not None and (w.w[0] != eng or not same_ok):
                evs.append(w.w)
            for ev in w.r:
                if ev[0] != eng or not same_ok:
                    evs.append(ev)
        need = {}
        for k, v in evs:
            if k in self.cnt:
                assert v <= self.cnt[k], (eng, k, v, self.cnt[k])
            if v > self.seen[eng].get(k, 0):
                need[k] = max(need.get(k, 0), v)
        for k, v in need.items():
            self.seen[eng][k] = v
        return list(need.items())

    def op(self, eng, fn, reads=(), writes=(), inc=True):
        xr = [r for r in reads if r.x]
        if xr:
            writes = list(writes) + [r for r in xr if r not in writes]
            reads = [r for r in reads if not r.x]
        waits = self._waits(eng, reads, writes)
        if inc:
            self.cnt[eng] += 1
            ev = (eng, self.cnt[eng])
        else:
            ev = (eng, self.cnt[eng] + 1)
        for r in reads:
            r.r.append(ev)
        for w in writes:
            w.w = ev
            w.r = []
        self.q[eng].append((waits, fn, (eng, 1) if inc else None))
        return ev

    def dma(self, eng, fn, semkey, reads=(), writes=()):
        waits = self._waits(eng, reads, writes)
        self.dcnt[semkey] = self.dcnt.get(semkey, 0) + 16
        ev = (semkey, self.dcnt[semkey])
        for r in reads:
            r.r.append(ev)
        for w in writes:
            w.w = ev
            w.r = []
        self.q[eng].append((waits, fn, (semkey, 16)))
        return ev

    def wait_all(self, eng, evs):
        need = {}
        for k, v in evs:
            if v > self.seen[eng].get(k, 0):
                need[k] = max(need.get(k, 0), v)
        for k, v in need.items():
            self.seen[eng][k] = v
        self.q[eng].append((list(need.items()), None, None))

    def build(self, nc, es):
        sems = {}
        for k in list(self.ENGS) + sorted(self.dcnt.keys()):
            sems[k] = es.enter_context(nc.semaphore("s_" + k))
        block = es.enter_context(nc.Block())
        q = self.q

        def run(eng_name):
            def body(e):
                for waits, fn, inc in q[eng_name]:
                    for k, v in waits:
                        e.wait_ge(sems[k], v)
                    if fn is not None:
                        ins = fn(e)
                        if inc is not None:
                            ins.then_inc(sems[inc[0]], inc[1])
            return body

        block.tensor(run("pe"))
        block.scalar(run("act"))
        block.vector(run("dve"))
        block.gpsimd(run("pool"))
        block.sync(run("sp"))


def build_nc():
    nc = bass.Bass("TRN2", target_bir_lowering=False)
    xT_d = nc.dram_tensor("xT", [NSEQ, TT, 128, 4096], F32, kind="ExternalInput")
    wqkv_d = nc.dram_tensor("w_qkv", [3, 128, 4096], F32, kind="ExternalInput")
    wubc_d = nc.dram_tensor("w_ubc", [8, 128, 3072], F32, kind="ExternalInput")
    wpj_d = nc.dram_tensor("w_pj", [8, 128, 4096], F32, kind="ExternalInput")
    wo_d = nc.dram_tensor("w_o", [2, 128, 4096], F32, kind="ExternalInput")
    wup_d = nc.dram_tensor("w_upg", [11, 128, 4096], F32, kind="ExternalInput")
    wdn_d = nc.dram_tensor("w_dn", [8, 128, 2816], F32, kind="ExternalInput")
    vec_d = nc.dram_tensor("vecs", [128, NV], F32, kind="ExternalInput")
    tab_d = nc.dram_tensor("tabs", [128, 2 * SEQ], F32, kind="ExternalInput")
    cm_d = nc.dram_tensor("cmat", [128, 512], F32, kind="ExternalInput")
    yT_d = nc.dram_tensor("yT", [NSEQ, TT, 128, 4096], F32, kind="ExternalOutput")
    x1_d = nc.dram_tensor("x1s", [NSEQ, TT, 128, 4096], F32)
    wdnb_d = nc.dram_tensor("wdn_bf16", [8, 128, 2816], BF16)

    es = ExitStack()
    S = Sched()
    with es:
        sb = lambda name, shape, dt: es.enter_context(nc.sbuf_tensor(name, shape, dt))
        A = sb("A", [128, 8 * SEQ], BF16)
        O = sb("O", [128, 8 * SEQ], BF16)
        R1 = sb("R1", [128, 8 * SEQ], BF16)
        M = sb("M", [128, 8 * SEQ], BF16)
        tabs = sb("tabs_sb", [128, 2 * SEQ], F32)
        ring = [sb(f"ring{i}", [128, 4096], BF16) for i in range(NSLOT)]
        T = [sb(f"T{i}", [128, 4608], F32) for i in range(2)]
        vecs = sb("vecs_sb", [128, NV], F32)
        cmat = sb("cmat_sb", [128, 512], BF16)
        epsb = sb("epsb", [128, 1], F32)
        PS = [es.enter_context(nc.psum_tensor(f"ps{i}", [128, 1024], F32)) for i in range(4)]

        A_f = A.bitcast(F32)
        O_f = O.bitcast(F32)
        R1_f = R1.bitcast(F32)
        M_f = M.bitcast(F32)
        T_b = [t.bitcast(BF16) for t in T]

        r_A = [Res(f"A{t}") for t in range(TT)]
        r_O = [[Res(f"O{h}_{t}") for t in range(TT)] for h in range(8)]
        r_R1 = [Res(f"R1_{c}") for c in range(8)]
        r_Mb = [Res(f"Mb{b}") for b in range(16)]
        r_ring = [Res(f"ring{i}") for i in range(NSLOT)]
        r_T = [[Res(f"T{i}_{b}") for b in range(9)] for i in range(2)]
        r_bank = [[Res(f"ps{i}_{h}", excl=True) for h in range(2)] for i in range(4)]
        r_consts = [Res("eps"), Res("vec"), Res("cm")]
        r_tabs = Res("tabs")
        r_x1d = [[Res(f"x1d{s}_{t}") for t in range(TT)] for s in range(NSEQ)]
        r_wdnb = [Res(f"wdnb{d}") for d in range(8)]

        ones1024 = cmat[:, 0:128]
        ones128 = cmat[:, 128:256]
        ones1 = cmat[:, 256:384]
        pmatT = cmat[:, 384:512]

        def vcol(i):
            return vecs[:, i:i + 1]

        bank_ap = lambda i, h: PS[i][:, h * 512:(h + 1) * 512]

        ring_state = {"n": 0}

        ring_pinned = set()

        def load_slot(src_ap, L, reads=(), nb=None):
            while True:
                s = ring_state["n"] % NSLOT
                ring_state["n"] += 1
                if s not in ring_pinned:
                    break
            if nb is None:
                nb = L // 1024 if L % 1024 == 0 else L // 704
            bl = L // nb
            S.dma("pool", lambda e, s=s: e.dma_start(
                out=ring[s][:, 0:L].rearrange("p (a b) -> p a b", b=bl),
                in_=src_ap.rearrange("p (a b) -> p a b", b=bl)), f"ring{s}", reads=list(reads), writes=[r_ring[s]])
            return s

        bank_state = {"n": 0}

        bank_reserved = set()

        def next_bank():
            while True:
                n = bank_state["n"] % 8
                bank_state["n"] += 1
                if (n // 2, n % 2) not in bank_reserved:
                    return n // 2, n % 2

        def mm_group(out_ap, out_res, pairs, reads, inc_last=True):
            n = len(pairs)
            for idx, (l, r) in enumerate(pairs):
                S.op("pe", lambda e, l=l, r=r, idx=idx: e.matmul(out_ap, lhsT=l, rhs=r, start=(idx == 0), stop=(idx == n - 1)),
                     reads=reads, writes=[out_res], inc=(inc_last and idx == n - 1))

        def rstd_from(ps_ap, ps_res, out_ap, out_res):
            S.op("act", lambda e: e.activation(out=out_ap, in_=ps_ap, func=AF.Ln, bias=epsb[:, 0:1]),
                 reads=[ps_res, *r_consts], writes=[out_res])
            S.op("act", lambda e: e.activation(out=out_ap, in_=out_ap, func=AF.Exp, scale=-0.5),
                 reads=[out_res], writes=[out_res])

        S.op("dve", lambda e: e.memset(epsb[:], EPS), writes=[r_consts[0]])
        S.dma("sp", lambda e: e.dma_start(out=vecs[:], in_=vec_d[:, :]), "ld_c0", writes=[r_consts[1]])
        S.dma("sp", lambda e: e.dma_start(out=tabs[:], in_=tab_d[:, :]), "ld_c1", writes=[r_tabs])
        S.dma("pool", lambda e: e.dma_start(out=cmat[:], in_=cm_d[:, :]), "ld_c2", writes=[r_consts[2]])
        Ctab = tabs[:, 0:SEQ]
        Stab = tabs[:, SEQ:2 * SEQ]

        out_evs = []
        qk_ctr = {"n": 0}

        qk_pend = {"p": None}

        def qk_pre(pairs, reads):
            k = qk_ctr["n"] % 2
            qk_ctr["n"] += 1
            base = k * 4
            bi, bh = next_bank()
            bank_reserved.add((bi, bh))
            ps_ap, ps_res = bank_ap(bi, bh), r_bank[bi][bh]
            mm_group(ps_ap, ps_res, pairs, reads)
            qb = M[:, base * 1024: base * 1024 + 512]
            sq = M[:, base * 1024 + 512: base * 1024 + 1024]
            S.op("act", lambda e: e.activation(out=qb, in_=ps_ap, func=AF.Copy), reads=[ps_res], writes=[r_Mb[base]])
            S.op("act", lambda e: e.activation(out=sq, in_=ps_ap, func=AF.Square), reads=[ps_res], writes=[r_Mb[base]])
            return base, bi, bh

        def qk_fin(st, gcol, gpcol, dest_ap, dest_res, tt):
            base, pi_, ph_ = st
            ps_ap, ps_res = bank_ap(pi_, ph_), r_bank[pi_][ph_]
            qb = M[:, base * 1024: base * 1024 + 512]
            sq = M[:, base * 1024 + 512: base * 1024 + 1024]
            rs = M_f[:, (base + 1) * 512:(base + 2) * 512]
            t1 = M_f[:, (base + 2) * 512:(base + 3) * 512]
            t2 = M_f[:, (base + 3) * 512:(base + 4) * 512]
            r_qb = r_sq = r_Mb[base]
            r_rs, r_t1, r_t2 = r_Mb[base + 1], r_Mb[base + 2], r_Mb[base + 3]
            bi, bh = next_bank()
            mm_group(bank_ap(bi, bh), r_bank[bi][bh], [(ones128, sq)], [r_sq, *r_consts])
            ci, ch = next_bank()
            mm_group(bank_ap(ci, ch), r_bank[ci][ch], [(pmatT, qb)], [r_qb, *r_consts])
            rstd_from(bank_ap(bi, bh), r_bank[bi][bh], rs, r_rs)
            sl = slice(tt * 512, (tt + 1) * 512)
            S.op("dve", lambda e: e.scalar_tensor_tensor(out=t1, in0=ps_ap, scalar=vcol(gcol), in1=Ctab[:, sl],
                                                         op0=ALU.mult, op1=ALU.mult),
                 reads=[ps_res, *r_consts, r_tabs], writes=[r_t1])
            S.op("dve", lambda e: e.scalar_tensor_tensor(out=t2, in0=bank_ap(ci, ch), scalar=vcol(gpcol), in1=Stab[:, sl],
                                                         op0=ALU.mult, op1=ALU.mult),
                 reads=[r_bank[ci][ch], *r_consts, r_tabs], writes=[r_t2])
            S.op("dve", lambda e: e.tensor_tensor(out=t1, in0=t1, in1=t2, op=ALU.add), reads=[r_t1, r_t2], writes=[r_t1])
            S.op("dve", lambda e: e.tensor_tensor(out=dest_ap, in0=t1, in1=rs, op=ALU.mult),
                 reads=[r_t1, r_rs], writes=[dest_res])
            bank_reserved.discard((pi_, ph_))

        def qk_item(pairs, reads, gcol, gpcol, dest_ap, dest_res, tt):
            st = qk_pre(pairs, reads)
            if qk_pend["p"] is not None:
                qk_fin(*qk_pend["p"])
            qk_pend["p"] = (st, gcol, gpcol, dest_ap, dest_res, tt)

        def qk_flush():
            if qk_pend["p"] is not None:
                qk_fin(*qk_pend["p"])
                qk_pend["p"] = None

        def norm_tile(xt_f, xt_res, g0, dest_ap_fn, dest_res, sq_b, sq_res, rs_ap, rs_res):
            S.op("act", lambda e: e.activation(out=sq_b, in_=xt_f, func=AF.Square), reads=xt_res, writes=sq_res)
            bi, bh = next_bank()
            mm_group(bank_ap(bi, bh), r_bank[bi][bh],
                     [(ones1024, sq_b[:, c * 512:(c + 1) * 512]) for c in range(8)], list(sq_res) + [*r_consts])
            rstd_from(bank_ap(bi, bh), r_bank[bi][bh], rs_ap, rs_res)
            for c in range(8):
                S.op("dve", lambda e, c=c: e.scalar_tensor_tensor(
                    out=dest_ap_fn(c), in0=xt_f[:, c * 512:(c + 1) * 512], scalar=vcol(g0 + c), in1=rs_ap,
                    op0=ALU.mult, op1=ALU.mult),
                    reads=list(xt_res) + [rs_res, *r_consts], writes=[dest_res])

        def emit_ssq(d, sbi, sbh, sqd):
            S.op("pe", lambda e: e.matmul(bank_ap(sbi, sbh), lhsT=ones1024, rhs=sqd[:, d * 512:(d + 1) * 512],
                                          start=(d == 0), stop=(d == 7)),
                 reads=[r_T[1][d // 2], *r_consts], writes=[r_bank[sbi][sbh]], inc=True)

        x_bufs = [(M_f[:, 0:4096], r_Mb[0:8]),
                  (O_f[:, 4096:8192], [r_O[h][t] for h in range(4, 8) for t in range(TT)]),
                  (R1_f[:, 4096:8192], r_R1[4:8]),
                  (O_f[:, 0:4096], [r_O[h][t] for h in range(4) for t in range(TT)])]

        def emit_x_loads(sq_, eng):
            for tt in range(TT):
                xt, xres = x_bufs[tt]
                S.dma(eng, lambda e, xt=xt, sq_=sq_, tt=tt: e.dma_start(out=xt, in_=xT_d[sq_, tt]), f"ld_x{tt}_{eng}",
                      reads=(list(x_bufs[tt - 1][1]) if (tt > 0 and eng == "pool") else []), writes=xres)

        for s in range(NSEQ):
            if s == 0:
                emit_x_loads(0, "sp")
            for tt in range(TT):
                xt, xres = x_bufs[tt]
                sqb = T_b[1][:, 0:4096]
                norm_tile(xt, xres, V_PRE,
                          lambda c, tt=tt: A[:, c * SEQ + tt * 512: c * SEQ + (tt + 1) * 512], r_A[tt],
                          sqb, r_T[1][0:4], T[1][:, 2560:3072], r_T[1][5])

            kT = lambda kvh: R1[:, kvh * SEQ:(kvh + 1) * SEQ]
            Vt = R1[:, 2 * SEQ:4 * SEQ]
            sl_kv = load_slot(wqkv_d[2], 4096)
            rg = ring[sl_kv]
            for tt in range(TT):
                for kvh in range(2):
                    qk_item([(rg[:, c * 512 + kvh * 128: c * 512 + (kvh + 1) * 128],
                              A[:, c * SEQ + tt * 512: c * SEQ + (tt + 1) * 512]) for c in range(8)],
                            [r_ring[sl_kv], r_A[tt]], V_KG, V_KGP,
                            kT(kvh)[:, tt * 512:(tt + 1) * 512], r_R1[kvh], tt)
            for kt2 in range(8):
                bi, bh = next_bank()
                for q in range(2):
                    kt = kt2 * 2 + q
                    mm_group(PS[bi][:, bh * 512 + q * 256: bh * 512 + (q + 1) * 256], r_bank[bi][bh],
                             [(A[:, c * SEQ + kt * 128: c * SEQ + (kt + 1) * 128],
                               rg[:, c * 512 + 256: c * 512 + 512]) for c in range(8)],
                             [r_ring[sl_kv], r_A[kt // 4]])
                S.op("act", lambda e, bi=bi, bh=bh, kt2=kt2: e.activation(
                    out=Vt[:, kt2 * 512:(kt2 + 1) * 512], in_=bank_ap(bi, bh), func=AF.Copy),
                    reads=[r_bank[bi][bh]], writes=[r_R1[2 + kt2 // 4]])

            def p_ap(j):
                return R1[:, (4 + j) * SEQ:(5 + j) * SEQ] if j < 4 else M[:, j * SEQ:(j + 1) * SEQ]

            def p_res(j):
                return [r_R1[4 + j]] if j < 4 else [r_Mb[2 * j], r_Mb[2 * j + 1]]

            S.op("dve", lambda e: e.memset(T[0][:, 0:1], 0.0), writes=[r_T[0][0]])
            S.op("dve", lambda e: e.memset(T[0][:, 2049:2050], 0.0), writes=[r_T[0][4]])
            c_wrow = T[0]
            c_yrow = T[0][:, 2560:4608]
            c_brow = T[1][:, 2560:4608]
            sl_q = None
            for j in range(8):
                if j % 4 == 0:
                    if sl_q is not None:
                        ring_pinned.discard(sl_q)
                    sl_q = load_slot(wqkv_d[j // 4], 4096)
                    ring_pinned.add(sl_q)
                rgq = ring[sl_q]
                hq = j % 4
                sl = load_slot(wubc_d[j], 3072)
                rg = ring[sl]
                for tt in range(TT):
                    qk_item([(rgq[:, c * 512 + hq * 128: c * 512 + (hq + 1) * 128],
                              A[:, c * SEQ + tt * 512: c * SEQ + (tt + 1) * 512]) for c in range(8)],
                            [r_ring[sl_q], r_A[tt]], V_QG, V_QGP,
                            O[:, j * SEQ + tt * 512: j * SEQ + (tt + 1) * 512], r_O[j][tt], tt)
                    bks = []
                    for k in (2, 0, 1):
                        bi, bh = next_bank()
                        mm_group(bank_ap(bi, bh), r_bank[bi][bh],
                                 [(rg[:, c * 384 + k * 128: c * 384 + (k + 1) * 128],
                                   A[:, c * SEQ + tt * 512: c * SEQ + (tt + 1) * 512]) for c in range(8)],
                                 [r_ring[sl], r_A[tt]])
                        bks.append((bi, bh))
                    (cbi, cbh), (ubi, ubh), (bbi, bbh) = bks
                    cs = T[1][:, (tt % 2) * 512:(tt % 2 + 1) * 512]
                    r_cs = r_T[1][tt % 2]
                    S.op("act", lambda e, cs=cs, cbi=cbi, cbh=cbh: e.activation(out=cs, in_=bank_ap(cbi, cbh), func=AF.Copy),
                         reads=[r_bank[cbi][cbh]], writes=[r_cs])
                    S.op("dve", lambda e, cs=cs, ubi=ubi, ubh=ubh, tt=tt: e.tensor_tensor(
                        out=c_wrow[:, 1 + tt * 512: 1 + (tt + 1) * 512], in0=bank_ap(ubi, ubh), in1=cs, op=ALU.mult),
                        reads=[r_bank[ubi][ubh], r_cs], writes=[r_T[0][tt], r_T[0][tt + 1]])
                    S.op("act", lambda e, bbi=bbi, bbh=bbh, tt=tt: e.activation(
                        out=c_brow[:, tt * 512:(tt + 1) * 512], in_=bank_ap(bbi, bbh), func=AF.Copy),
                        reads=[r_bank[bbi][bbh]], writes=[r_T[1][5 + tt]])
                rw = r_T[0][0:5]
                ry = r_T[0][5:9]
                S.op("dve", lambda e, j=j: e.tensor_scalar(out=c_yrow, in0=c_wrow[:, 0:2048], scalar1=vcol(V_MCW + j), scalar2=None,
                                                           op0=ALU.mult), reads=rw + [*r_consts], writes=ry)
                S.op("dve", lambda e, j=j: e.scalar_tensor_tensor(out=c_yrow, in0=c_wrow[:, 1:2049], scalar=vcol(V_MCW + 8 + j),
                                                                  in1=c_yrow, op0=ALU.mult, op1=ALU.add),
                     reads=rw + ry + [*r_consts], writes=ry)
                S.op("dve", lambda e, j=j: e.scalar_tensor_tensor(out=c_yrow, in0=c_wrow[:, 2:2050], scalar=vcol(V_MCW + 16 + j),
                                                                  in1=c_yrow, op0=ALU.mult, op1=ALU.add),
                     reads=rw + ry + [*r_consts], writes=ry)
                S.op("dve", lambda e, j=j: e.tensor_tensor(out=p_ap(j), in0=c_yrow, in1=c_brow, op=ALU.mult),
                     reads=ry + r_T[1][5:9], writes=p_res(j))
            qk_flush()
            ring_pinned.discard(sl_q)

            if s == 0:
                for d in range(8):
                    S.dma("pool", lambda e, d=d: e.dma_start(
                        out=wdnb_d[d].rearrange("p (a b) -> p a b", b=704),
                        in_=wdn_d[d].rearrange("p (a b) -> p a b", b=704)), f"cv_wdn{d}", reads=[r_O[7][3]], writes=[r_wdnb[d]])

            pairs = [(hh, m2) for hh in range(NH) for m2 in range(2)]

            def actx(p):
                hh, m2 = pairs[p]
                qts = (2 * m2, 2 * m2 + 1)
                return dict(kvh=hh // 4,
                            qaps=[O[:, hh * SEQ + qt * 512: hh * SEQ + (qt + 1) * 512] for qt in qts],
                            r_qs=[r_O[hh][qt] for qt in qts])

            def emit_qk(cx, st, kp):
                kvh, qap = cx["kvh"], cx["qaps"][st]
                for q in range(2):
                    kt = kp * 2 + q
                    S.op("pe", lambda e, st=st, q=q, kt=kt, kvh=kvh, qap=qap: e.matmul(
                        PS[st][:, q * 512:(q + 1) * 512], lhsT=kT(kvh)[:, kt * 128:(kt + 1) * 128], rhs=qap,
                        start=True, stop=True),
                        reads=[r_R1[kvh], cx["r_qs"][st]], writes=[r_bank[st][0], r_bank[st][1]], inc=(q == 1))

            def pt_buf(st, kp):
                b = st * 2 + (kp % 2)
                return T_b[0][:, b * 1024:(b + 1) * 1024], r_T[0][b]

            def pa_buf(st, kp):
                b = 4 + st * 2 + (kp % 2)
                return T_b[0][:, b * 1024:b * 1024 + 512], r_T[0][b]

            def emit_exp(st, kp):
                PT, r_PT = pt_buf(st, kp)
                S.op("act", lambda e, PT=PT, st=st: e.activation(out=PT, in_=PS[st][:, :], func=AF.Exp, scale=SCALE),
                     reads=[r_bank[st][0], r_bank[st][1]], writes=[r_PT])
                pa, r_pa = pa_buf(st, kp)
                S.op("dve", lambda e, PT=PT, pa=pa: e.tensor_tensor(out=pa, in0=PT[:, 0:512], in1=PT[:, 512:1024],
                                                                   op=ALU.add), reads=[r_PT], writes=[r_pa])

            def emit_pv(cx, st, kp):
                PT, r_PT = pt_buf(st, kp)
                oacc = bank_ap(2, st)
                kvh = cx["kvh"]
                for q in range(2):
                    kt = kp * 2 + q
                    S.op("pe", lambda e, q=q, kt=kt, oacc=oacc, kvh=kvh, PT=PT: e.matmul(
                        oacc, lhsT=Vt[:, kt * 256 + kvh * 128: kt * 256 + (kvh + 1) * 128],
                        rhs=PT[:, q * 512:(q + 1) * 512], start=(kt == 0), stop=(kt == 15)),
                        reads=[r_PT, r_R1[2 + kt // 8]], writes=[r_bank[2][st]], inc=(q == 1))

            def emit_den(st, kp):
                pa, r_pa = pa_buf(st, kp)
                den = bank_ap(3, st)
                S.op("pe", lambda e, pa=pa, den=den, kp=kp: e.matmul(den, lhsT=ones1, rhs=pa, start=(kp == 0), stop=(kp == 7)),
                     reads=[r_pa, *r_consts], writes=[r_bank[3][st]], inc=True)

            def emit_epilogue(cx, st):
                rd = T[1][:, st * 512:(st + 1) * 512]
                r_rd = r_T[1][st]
                S.op("act", lambda e, rd=rd, st=st: e.activation(out=rd, in_=bank_ap(3, st), func=AF.Ln),
                     reads=[r_bank[3][st]], writes=[r_rd])
                S.op("act", lambda e, rd=rd: e.activation(out=rd, in_=rd, func=AF.Exp, scale=-1.0),
                     reads=[r_rd], writes=[r_rd])
                S.op("dve", lambda e, rd=rd, st=st, qap=cx["qaps"][st]: e.tensor_tensor(
                    out=qap, in0=bank_ap(2, st), in1=rd, op=ALU.mult),
                    reads=[r_bank[2][st], r_rd], writes=[cx["r_qs"][st]])

            cx = actx(0)
            for st in range(2):
                emit_qk(cx, st, 0)
            for p in range(len(pairs)):
                cx = actx(p)
                cx_next = actx(p + 1) if p + 1 < len(pairs) else None
                for kp in range(8):
                    if kp == 0 and p >= 1:
                        cxp = actx(p - 1)
                        emit_exp(0, 0)
                        emit_epilogue(cxp, 0)
                        emit_exp(1, 0)
                        emit_epilogue(cxp, 1)
                    else:
                        for st in range(2):
                            emit_exp(st, kp)
                    for st in range(2):
                        if kp + 1 < 8:
                            emit_qk(cx, st, kp + 1)
                        elif cx_next is not None:
                            emit_qk(cx_next, st, 0)
                        emit_pv(cx, st, kp)
                        if kp >= 1:
                            emit_den(st, kp - 1)
                        if kp == 7:
                            emit_den(st, 7)
            cxl = actx(len(pairs) - 1)
            emit_epilogue(cxl, 0)
            emit_epilogue(cxl, 1)

            def mg_ap(j, tt):
                if j < 4:
                    return R1[:, j * SEQ + tt * 512: j * SEQ + (tt + 1) * 512]
                return M[:, (j - 4) * SEQ + tt * 512:(j - 4) * SEQ + (tt + 1) * 512]

            def mg_res(j, tt):
                return [r_R1[j]] if j < 4 else [r_Mb[((j - 4) * 4 + tt) // 2]]

            def p_tile_res(c, tt):
                return [r_R1[4 + c]] if c < 4 else [r_Mb[(c * 4 + tt) // 2]]

            for j in range(8):
                sl = load_slot(wpj_d[j], 4096)
                rg = ring[sl]
                for tt in range(TT):
                    tsl = lambda c, tt=tt: slice(c * SEQ + tt * 512, c * SEQ + (tt + 1) * 512)
                    src_fns = (
                        (lambda c, tt=tt: O[:, tsl(c)], lambda c, tt=tt: [r_O[c][tt]]),
                        (lambda c, tt=tt: p_ap(c)[:, tt * 512:(tt + 1) * 512], lambda c, tt=tt: p_tile_res(c, tt)),
                        (lambda c, tt=tt: A[:, tsl(c)], lambda c, tt=tt: [r_A[tt]]),
                        (lambda c, tt=tt: A[:, tsl(c)], lambda c, tt=tt: [r_A[tt]]),
                    )
                    bk = []
                    for k, (ap_fn, rs_fn) in enumerate(src_fns):
                        bi, bh = next_bank()
                        rds = [r_ring[sl]]
                        for c in range(8):
                            for r_ in rs_fn(c):
                                if r_ not in rds:
                                    rds.append(r_)
                        mm_group(bank_ap(bi, bh), r_bank[bi][bh],
                                 [(rg[:, c * 512 + k * 128: c * 512 + (k + 1) * 128], ap_fn(c)) for c in range(8)], rds)
                        bk.append((bi, bh))
                    sa = T[tt % 2][:, 0:512]
                    sbg = T[tt % 2][:, 512:1024]
                    m1 = T[tt % 2][:, 1024:1536]
                    r_sa, r_sb, r_m1 = r_T[tt % 2][0], r_T[tt % 2][1], r_T[tt % 2][2]
                    S.op("act", lambda e, sa=sa, b=bk[2], j=j: e.activation(out=sa, in_=bank_ap(*b), func=AF.Sigmoid,
                                                                             bias=vcol(V_GBA + j)),
                         reads=[r_bank[bk[2][0]][bk[2][1]], *r_consts], writes=[r_sa])
                    S.op("act", lambda e, sbg=sbg, b=bk[3], j=j: e.activation(out=sbg, in_=bank_ap(*b), func=AF.Sigmoid,
                                                                               bias=vcol(V_GBB + j)),
                         reads=[r_bank[bk[3][0]][bk[3][1]], *r_consts], writes=[r_sb])
                    S.op("dve", lambda e, sa=sa, m1=m1, b=bk[0]: e.tensor_tensor(out=m1, in0=bank_ap(*b), in1=sa, op=ALU.mult),
                         reads=[r_bank[bk[0][0]][bk[0][1]], r_sa], writes=[r_m1])
                    S.op("dve", lambda e, sbg=sbg, b=bk[1]: e.tensor_tensor(out=sbg, in0=bank_ap(*b), in1=sbg, op=ALU.mult),
                         reads=[r_bank[bk[1][0]][bk[1][1]], r_sb], writes=[r_sb])
                    S.op("dve", lambda e, sbg=sbg, m1=m1, j=j, tt=tt: e.tensor_tensor(
                        out=mg_ap(j, tt), in0=m1, in1=sbg, op=ALU.add),
                        reads=[r_m1, r_sb], writes=mg_res(j, tt))

            for blk in range(2):
                S.dma("pool", lambda e, blk=blk: e.dma_start(
                    out=O[:, blk * 4096:(blk + 1) * 4096].rearrange("p (a b) -> p a b", b=1024),
                    in_=wo_d[blk].rearrange("p (a b) -> p a b", b=1024)), f"ld_wo{blk}",
                    writes=[r_O[h][t] for h in (2 * blk, 2 * blk + 1) for t in range(TT)])
            wo_res = lambda d: [r_O[h][t] for h in (2 * (d // 4), 2 * (d // 4) + 1) for t in range(TT)]
            out_state = {}

            def mo_tile(tt):
                if tt % 2 == 0:
                    return T[0][:, 0:4096], r_T[0][0:8]
                return O_f[:, 4096:8192], [r_O[h][t] for h in range(4, 8) for t in range(TT)]

            def mo_buf(tt, d):
                if tt % 2 == 0:
                    return T[0][:, d * 512:(d + 1) * 512], [r_T[0][d]]
                return (O_f[:, 4096 + d * 512: 4096 + (d + 1) * 512],
                        [r_O[4 + d // 2][(d % 2) * 2], r_O[4 + d // 2][(d % 2) * 2 + 1]])

            SQB = (4, 7, 8)

            def out_ssq(tt, d):
                xt2, xres, sbi, sbh = out_state[tt]
                b = SQB[d % 3]
                S.op("pe", lambda e: e.matmul(bank_ap(sbi, sbh), lhsT=ones1024, rhs=T_b[1][:, b * 1024: b * 1024 + 512],
                                              start=(d == 0), stop=(d == 7)),
                     reads=[r_T[1][b], *r_consts], writes=[r_bank[sbi][sbh]], inc=True)

            def out_A(tt, d0, d1):
                hb = tt % 2
                xt2 = R1_f[:, 4096:8192] if hb == 0 else M_f[:, 4096:8192]
                xres = r_R1[4:8] if hb == 0 else r_Mb[8:16]
                if d0 == 0:
                    sbi, sbh = next_bank()
                    bank_reserved.add((sbi, sbh))
                    out_state[tt] = (xt2, xres, sbi, sbh)
                if d1 == 8:
                    S.dma("sp", lambda e, xt2=xt2, s=s, tt=tt: e.dma_start(out=xt2, in_=xT_d[s, tt]), f"ld_y{hb}", writes=xres)
                for d in range(d0, d1):
                    bi, bh = next_bank()
                    mm_group(bank_ap(bi, bh), r_bank[bi][bh],
                             [(O[:, (d // 4) * 4096 + c * 512 + (d % 4) * 128: (d // 4) * 4096 + c * 512 + (d % 4 + 1) * 128],
                               mg_ap(c, tt)) for c in range(8)],
                             wo_res(d) + [r_ for c in range(8) for r_ in mg_res(c, tt)])
                    if d >= 1:
                        out_ssq(tt, d - 1)
                    mo_ap, mo_res = mo_buf(tt, d)
                    S.op("act", lambda e, mo_ap=mo_ap, bi=bi, bh=bh, d=d: e.activation(
                        out=mo_ap, in_=bank_ap(bi, bh), func=AF.Copy, scale=vcol(V_POST + d)),
                        reads=[r_bank[bi][bh], *r_consts], writes=mo_res)
                    b = SQB[d % 3]
                    S.op("act", lambda e, b=b, bi=bi, bh=bh: e.activation(out=T_b[1][:, b * 1024: b * 1024 + 512],
                                                                          in_=bank_ap(bi, bh), func=AF.Square),
                         reads=[r_bank[bi][bh]], writes=[r_T[1][b]])
                if d1 == 8:
                    out_ssq(tt, 7)

            def out_B1(tt):
                xt2, xres, sbi, sbh = out_state[tt]
                hb = tt % 2
                rs = T[1][:, 2560:3072]
                rstd_from(bank_ap(sbi, sbh), r_bank[sbi][sbh], rs, r_T[1][5])
                bank_reserved.discard((sbi, sbh))
                mt, mt_res = mo_tile(tt)
                mt3 = mt.rearrange("p (c n) -> p c n", n=512)
                rs3 = rs.unsqueeze(1).to_broadcast([128, 8, 512])
                S.op("dve", lambda e, mt3=mt3, rs3=rs3: e.tensor_tensor(out=mt3, in0=mt3, in1=rs3, op=ALU.mult),
                     reads=mt_res + [r_T[1][5]], writes=mt_res)
                S.op("dve", lambda e, mt=mt, xt2=xt2: e.tensor_tensor(out=xt2, in0=mt, in1=xt2, op=ALU.add),
                     reads=mt_res + xres, writes=xres)
                S.dma("sp", lambda e, xt2=xt2, s=s, tt=tt: e.dma_start(out=x1_d[s, tt], in_=xt2), f"st_x1{hb}",
                      reads=xres, writes=[r_x1d[s][tt]])

            def out_C1(tt):
                xt2, xres, sbi, sbh = out_state[tt]
                S.op("act", lambda e, xt2=xt2: e.activation(out=T_b[1][:, 0:4096], in_=xt2, func=AF.Square),
                     reads=xres, writes=r_T[1][0:4])

            def out_C2(tt):
                bi, bh = next_bank()
                bank_reserved.add((bi, bh))
                mm_group(bank_ap(bi, bh), r_bank[bi][bh],
                         [(ones1024, T_b[1][:, c * 512:(c + 1) * 512]) for c in range(8)], r_T[1][0:4] + [*r_consts])
                out_state[("c", tt)] = (bi, bh)

            def out_C3(tt):
                xt2, xres, sbi, sbh = out_state[tt]
                bi, bh = out_state[("c", tt)]
                rs2 = T[1][:, 3072:3584]
                rstd_from(bank_ap(bi, bh), r_bank[bi][bh], rs2, r_T[1][6])
                bank_reserved.discard((bi, bh))
                for c in range(8):
                    S.op("dve", lambda e, c=c, xt2=xt2, tt=tt: e.scalar_tensor_tensor(
                        out=A[:, c * SEQ + tt * 512: c * SEQ + (tt + 1) * 512], in0=xt2[:, c * 512:(c + 1) * 512],
                        scalar=vcol(V_FPRE + c), in1=rs2, op0=ALU.mult, op1=ALU.mult),
                        reads=list(xres) + [r_T[1][6], *r_consts], writes=[r_A[tt]])

            out_A(0, 0, 8)
            out_B1(0)
            out_A(1, 0, 8)
            out_C1(0)
            out_B1(1)
            for tt in range(2, TT):
                out_A(tt, 0, 4)
                out_C2(tt - 2)
                out_C3(tt - 2)
                out_A(tt, 4, 8)
                out_C1(tt - 1)
                out_B1(tt)
            out_C2(TT - 2)
            out_C3(TT - 2)
            out_C1(TT - 1)
            out_C2(TT - 1)
            out_C3(TT - 1)

            hid_buf = lambda j: (O, R1, M)[j // 8]

            def hid_res(j, tt):
                if j < 8:
                    return [r_O[j][tt]]
                if j < 16:
                    return [r_R1[j % 8]]
                return [r_Mb[((j % 8) * 4 + tt) // 2]]

            for i in range(2):
                S.op("dve", lambda e, i=i: e.memset(T[i][:, 0:1], 0.0), writes=[r_T[i][0]])
                S.op("dve", lambda e, i=i: e.memset(T[i][:, 2049:2050], 0.0), writes=[r_T[i][4]])
            for g in range(11):
                sl = load_slot(wup_d[g], 4096)
                rg = ring[sl]
                for jj in range(2):
                    j = 2 * g + jj
                    ti = j % 2
                    arow = T[ti]
                    yrow = T[ti][:, 2560:4608]
                    ra = r_T[ti][0:5]
                    ry = r_T[ti][5:9]
                    for tt in range(TT):
                        bi, bh = next_bank()
                        mm_group(bank_ap(bi, bh), r_bank[bi][bh],
                                 [(rg[:, c * 512 + jj * 128: c * 512 + (jj + 1) * 128],
                                   A[:, c * SEQ + tt * 512: c * SEQ + (tt + 1) * 512]) for c in range(8)],
                                 [r_ring[sl], r_A[tt]])
                        S.op("act", lambda e, bi=bi, bh=bh, tt=tt, arow=arow: e.activation(
                            out=arow[:, 1 + tt * 512: 1 + (tt + 1) * 512], in_=bank_ap(bi, bh), func=AF.Copy),
                            reads=[r_bank[bi][bh]], writes=[r_T[ti][tt], r_T[ti][tt + 1]])
                    bb = []
                    for tt in range(TT):
                        bi, bh = next_bank()
                        mm_group(bank_ap(bi, bh), r_bank[bi][bh],
                                 [(rg[:, c * 512 + 256 + jj * 128: c * 512 + 256 + (jj + 1) * 128],
                                   A[:, c * SEQ + tt * 512: c * SEQ + (tt + 1) * 512]) for c in range(8)],
                                 [r_ring[sl], r_A[tt]])
                        bb.append((bi, bh))
                    S.op("dve", lambda e, j=j, arow=arow, yrow=yrow: e.tensor_scalar(
                        out=yrow, in0=arow[:, 0:2048], scalar1=vcol(V_FCW + j), scalar2=None, op0=ALU.mult),
                        reads=ra + [*r_consts], writes=ry)
                    S.op("dve", lambda e, j=j, arow=arow, yrow=yrow: e.scalar_tensor_tensor(
                        out=yrow, in0=arow[:, 1:2049], scalar=vcol(V_FCW + NJ + j), in1=yrow, op0=ALU.mult, op1=ALU.add),
                        reads=ra + ry + [*r_consts], writes=ry)
                    S.op("dve", lambda e, j=j, arow=arow, yrow=yrow: e.scalar_tensor_tensor(
                        out=yrow, in0=arow[:, 2:2050], scalar=vcol(V_FCW + 2 * NJ + j), in1=yrow, op0=ALU.mult, op1=ALU.add),
                        reads=ra + ry + [*r_consts], writes=ry)
                    S.op("act", lambda e, yrow=yrow: e.activation(out=yrow, in_=yrow, func=AF.Gelu_apprx_tanh),
                         reads=ry, writes=ry)
                    hb_ = hid_buf(j)
                    for tt in range(TT):
                        bi, bh = bb[tt]
                        S.op("dve", lambda e, bi=bi, bh=bh, tt=tt, yrow=yrow, hb_=hb_, j=j: e.tensor_tensor(
                            out=hb_[:, (j % 8) * SEQ + tt * 512:(j % 8) * SEQ + (tt + 1) * 512],
                            in0=bank_ap(bi, bh), in1=yrow[:, tt * 512:(tt + 1) * 512], op=ALU.mult),
                            reads=[r_bank[bi][bh], r_T[ti][5 + tt]], writes=hid_res(j, tt))

            def o2_buf(tt, d):
                if tt % 2 == 0:
                    return T[0][:, d * 512:(d + 1) * 512], [r_T[0][d]]
                if d < 4:
                    return M_f[:, (12 + d) * 512:(13 + d) * 512], [r_Mb[12 + d]]
                b = 4 if d == 7 else 6 + (d - 4)
                return T[1][:, b * 512:(b + 1) * 512], [r_T[1][b]]

            r_x1h = [Res(f"x1h{s}_0"), Res(f"x1h{s}_1")]
            for tt in range(TT):
                hb = tt % 2
                x1t = A_f[:, hb * 4096:(hb + 1) * 4096]
                x1res = [r_x1h[hb]]
                S.dma("sp", lambda e, x1t=x1t, s=s, tt=tt: e.dma_start(out=x1t, in_=x1_d[s, tt]), f"ld_x1{hb}",
                      reads=[r_x1d[s][tt]], writes=x1res + (r_A if tt < 2 else []))
                sqd = T_b[1][:, 0:4096]
                sbi, sbh = next_bank()
                for d in range(8):
                    sl = load_slot(wdnb_d[d], 2816, reads=[r_wdnb[d]], nb=1)
                    rg = ring[sl]
                    bi, bh = next_bank()
                    if (bi, bh) == (sbi, sbh):
                        bi, bh = next_bank()
                    rds = [r_ring[sl]]
                    for j in range(NJ):
                        for r_ in hid_res(j, tt):
                            if r_ not in rds:
                                rds.append(r_)
                    mm_group(bank_ap(bi, bh), r_bank[bi][bh],
                             [(rg[:, j * 128:(j + 1) * 128],
                               hid_buf(j)[:, (j % 8) * SEQ + tt * 512:(j % 8) * SEQ + (tt + 1) * 512]) for j in range(NJ)], rds)
                    if d >= 1:
                        emit_ssq(d - 1, sbi, sbh, sqd)
                    o2_ap, o2_res = o2_buf(tt, d)
                    S.op("act", lambda e, o2_ap=o2_ap, bi=bi, bh=bh: e.activation(out=o2_ap, in_=bank_ap(bi, bh), func=AF.Copy),
                         reads=[r_bank[bi][bh]], writes=o2_res)
                    S.op("act", lambda e, d=d, bi=bi, bh=bh: e.activation(out=sqd[:, d * 512:(d + 1) * 512], in_=bank_ap(bi, bh),
                                                                          func=AF.Square),
                         reads=[r_bank[bi][bh]], writes=[r_T[1][d // 2]])
                emit_ssq(7, sbi, sbh, sqd)
                if tt == TT - 1 and s + 1 < NSEQ:
                    emit_x_loads(s + 1, "pool")
                rs = T[1][:, 2560:3072]
                rstd_from(bank_ap(sbi, sbh), r_bank[sbi][sbh], rs, r_T[1][5])
                for d in range(8):
                    o2_ap, o2_res = o2_buf(tt, d)
                    S.op("dve", lambda e, d=d, o2_ap=o2_ap: e.scalar_tensor_tensor(
                        out=o2_ap, in0=o2_ap, scalar=vcol(V_FPOST + d), in1=rs,
                        op0=ALU.mult, op1=ALU.mult), reads=o2_res + [r_T[1][5], *r_consts], writes=o2_res)
                    S.op("dve", lambda e, d=d, x1t=x1t, o2_ap=o2_ap: e.tensor_tensor(
                        out=x1t[:, d * 512:(d + 1) * 512], in0=o2_ap, in1=x1t[:, d * 512:(d + 1) * 512],
                        op=ALU.add), reads=o2_res + x1res, writes=x1res)
                out_evs.append(S.dma("sp", lambda e, x1t=x1t, s=s, tt=tt: e.dma_start(out=yT_d[s, tt], in_=x1t),
                                     f"st_y{hb}", reads=x1res + r_A))

        S.wait_all("sp", out_evs)
        S.build(nc, es)
    return nc


def _host_layout(inp):
    f = lambda a: np.ascontiguousarray(np.asarray(a, dtype=np.float32))
    w_in = f(inp["w_in"])[0]
    w_ap = f(inp["w_attn_proj"])[0]
    w_cp = f(inp["w_conv_proj"])[0]
    w_out = f(inp["w_out"])[0]
    w_up = f(inp["w_up"])[0]
    w_down = f(inp["w_down"])[0]
    sh = {}
    sh["w_qkv"] = f(w_in[:, 0:1536].reshape(8, 128, 3, 512).transpose(2, 1, 0, 3).reshape(3, 128, 4096))
    sh["w_ubc"] = f(w_in[:, 1536:4608].reshape(8, 128, 3, 8, 128).transpose(3, 1, 0, 2, 4).reshape(8, 128, 3072))
    W4 = np.stack([w_ap, w_cp, w_in[:, 4608:5632], w_in[:, 5632:6656]], axis=0)
    sh["w_pj"] = f(W4.reshape(4, 8, 128, 8, 128).transpose(3, 2, 1, 0, 4).reshape(8, 128, 4096))
    sh["w_o"] = f(w_out.reshape(8, 128, 2, 512).transpose(2, 1, 0, 3).reshape(2, 128, 4096))
    sh["w_upg"] = f(w_up.reshape(8, 128, 2, 11, 2, 128).transpose(3, 1, 0, 2, 4, 5).reshape(11, 128, 4096))
    sh["w_dn"] = f(w_down.reshape(22, 128, 8, 128).transpose(2, 1, 0, 3).reshape(8, 128, 2816))
    vec = np.zeros((128, NV), np.float32)
    col8 = lambda v: f(v).reshape(8, 128).T
    vec[:, V_PRE:V_PRE + 8] = col8(inp["mix_pre_g"][0])
    vec[:, V_POST:V_POST + 8] = col8(inp["mix_post_g"][0])
    vec[:, V_FPRE:V_FPRE + 8] = col8(inp["ffn_pre_g"][0])
    vec[:, V_FPOST:V_FPOST + 8] = col8(inp["ffn_post_g"][0])
    gb = f(inp["gate_b"])[0]
    vec[:, V_GBA:V_GBA + 8] = col8(gb[:D])
    vec[:, V_GBB:V_GBB + 8] = col8(gb[D:])
    perm = np.arange(128)
    perm = np.where((perm % 64) < 32, perm + 32, perm - 32)
    qg = f(inp["q_norm_g"])[0]
    kg = f(inp["k_norm_g"])[0]
    vec[:, V_QG] = qg
    vec[:, V_QGP] = qg[perm]
    vec[:, V_KG] = kg
    vec[:, V_KGP] = kg[perm]
    mcw = f(inp["mix_conv_w"])[0]
    for k in range(3):
        vec[:, V_MCW + 8 * k:V_MCW + 8 * (k + 1)] = mcw[k].reshape(8, 128).T
    fcw = f(inp["ffn_conv_w"])[0]
    for k in range(3):
        vec[:, V_FCW + NJ * k:V_FCW + NJ * (k + 1)] = fcw[k].reshape(NJ, 128).T
    sh["vecs"] = vec
    t = np.arange(SEQ)
    row_pos = (t // 64).astype(np.float64)
    col_pos = (t % 64).astype(np.float64)
    freqs = 10000.0 ** (-(np.arange(32, dtype=np.float64) / 32.0))
    tabs = np.zeros((128, 2 * SEQ), np.float32)
    for d in range(128):
        pos = row_pos if d < 64 else col_pos
        ang = pos * freqs[d % 32]
        tabs[d, 0:SEQ] = np.cos(ang)
        tabs[d, SEQ:] = np.sin(ang)
    sh["tabs"] = tabs
    cm = np.zeros((128, 512), np.float32)
    cm[:, 0:128] = 1.0 / 1024.0
    cm[:, 128:256] = 1.0 / 128.0
    cm[:, 256:384] = 1.0
    for m in range(128):
        if (m % 64) < 32:
            cm[m + 32, 384 + m] = -1.0
        else:
            cm[m - 32, 384 + m] = 1.0
    sh["cmat"] = cm
    return sh


_NC_CACHE = {}


def kernel(**inputs):
    x = np.asarray(inputs["x"], dtype=np.float32)
    shared = _host_layout(inputs)
    in_maps = []
    for core in range(NCORES):
        xs = x[core * NSEQ:(core + 1) * NSEQ]
        xT = np.ascontiguousarray(xs.reshape(NSEQ, TT, 512, 8, 128).transpose(0, 1, 4, 3, 2)).reshape(NSEQ, TT, 128, 4096)
        m = dict(shared)
        m["xT"] = xT
        in_maps.append(m)
    if "nc" not in _NC_CACHE:
        _NC_CACHE["nc"] = build_nc()
    nc = _NC_CACHE["nc"]
    res = run_bass_kernel_spmd(nc, in_maps, core_ids=list(range(NCORES)))
    outs = []
    for core in range(NCORES):
        yT = np.asarray(res.results[core]["yT"]).reshape(NSEQ, TT, 128, 8, 512)
        outs.append(yT.transpose(0, 1, 4, 3, 2).reshape(NSEQ, SEQ, D))
    return np.ascontiguousarray(np.concatenate(outs, axis=0).astype(np.float32))
```

```python
import numpy as np
from contextlib import ExitStack
import concourse.bass as bass
import concourse.mybir as mybir
from concourse.bass_utils import run_bass_kernel_spmd

F32 = mybir.dt.float32
BF16 = mybir.dt.bfloat16
AF = mybir.ActivationFunctionType
ALU = mybir.AluOpType

NCORES = 8
SEQ = 2048
D = 1024
NSEQ = 2
TT = 4
HD = 128
NH = 8
DFF = 2816
NJ = 22
EPS = 1e-6
SCALE = HD ** -0.5
NSLOT = 3

V_PRE, V_POST, V_FPRE, V_FPOST, V_GBA, V_GBB = 0, 8, 16, 24, 32, 40
V_QG, V_QGP, V_KG, V_KGP = 48, 49, 50, 51
V_MCW = 52
V_FCW = 76
NV = 142


class Res:
    __slots__ = ("name", "w", "r", "x")

    def __init__(self, name, excl=False):
        self.name = name
        self.w = None
        self.r = []
        self.x = excl


class Sched:
    ENGS = ("pe", "act", "dve", "pool", "sp")

    def __init__(self):
        self.q = {e: [] for e in self.ENGS}
        self.cnt = {e: 0 for e in self.ENGS}
        self.seen = {e: {} for e in self.ENGS}
        self.dcnt = {}

    def _waits(self, eng, reads, writes):
        evs = []
        for r in reads:
            if r.w is not None:
                evs.append(r.w)
        same_ok = (eng == "pe")
        for w in writes:
            if w.w is not None and (w.w[0] != eng or not same_ok):
                evs.append(w.w)
            for ev in w.r:
                if ev[0] != eng or not same_ok:
                    evs.append(ev)
        need = {}
        for k, v in evs:
            if k in self.cnt:
                assert v <= self.cnt[k], (eng, k, v, self.cnt[k])
            if v > self.seen[eng].get(k, 0):
                need[k] = max(need.get(k, 0), v)
        for k, v in need.items():
            self.seen[eng][k] = v
        return list(need.items())

    def op(self, eng, fn, reads=(), writes=(), inc=True):
        xr = [r for r in reads if r.x]
        if xr:
            writes = list(writes) + [r for r in xr if r not in writes]
            reads = [r for r in reads if not r.x]
        waits = self._waits(eng, reads, writes)
        if inc:
            self.cnt[eng] += 1
            ev = (eng, self.cnt[eng])
        else:
            ev = (eng, self.cnt[eng] + 1)
        for r in reads:
            r.r.append(ev)
        for w in writes:
            w.w = ev
            w.r = []
        self.q[eng].append((waits, fn, (eng, 1) if inc else None))
        return ev

    def dma(self, eng, fn, semkey, reads=(), writes=()):
        waits = self._waits(eng, reads, writes)
        self.dcnt[semkey] = self.dcnt.get(semkey, 0) + 16
        ev = (semkey, self.dcnt[semkey])
        for r in reads:
            r.r.append(ev)
        for w in writes:
            w.w = ev
            w.r = []
        self.q[eng].append((waits, fn, (semkey, 16)))
        return ev

    def wait_all(self, eng, evs):
        need = {}
        for k, v in evs:
            if v > self.seen[eng].get(k, 0):
                need[k] = max(need.get(k, 0), v)
        for k, v in need.items():
            self.seen[eng][k] = v
        self.q[eng].append((list(need.items()), None, None))

    def build(self, nc, es):
        sems = {}
        for k in list(self.ENGS) + sorted(self.dcnt.keys()):
            sems[k] = es.enter_context(nc.semaphore("s_" + k))
        block = es.enter_context(nc.Block())
        q = self.q

        def run(eng_name):
            def body(e):
                for waits, fn, inc in q[eng_name]:
                    for k, v in waits:
                        e.wait_ge(sems[k], v)
                    if fn is not None:
                        ins = fn(e)
                        if inc is not None:
                            ins.then_inc(sems[inc[0]], inc[1])
            return body

        block.tensor(run("pe"))
        block.scalar(run("act"))
        block.vector(run("dve"))
        block.gpsimd(run("pool"))
        block.sync(run("sp"))


def build_nc():
    nc = bass.Bass("TRN2", target_bir_lowering=False)
    xT_d = nc.dram_tensor("xT", [NSEQ, TT, 128, 4096], F32, kind="ExternalInput")
    wqkv_d = nc.dram_tensor("w_qkv", [3, 128, 4096], F32, kind="ExternalInput")
    wubc_d = nc.dram_tensor("w_ubc", [8, 128, 3072], F32, kind="ExternalInput")
    wpj_d = nc.dram_tensor("w_pj", [8, 128, 4096], F32, kind="ExternalInput")
    wo_d = nc.dram_tensor("w_o", [2, 128, 4096], F32, kind="ExternalInput")
    wup_d = nc.dram_tensor("w_upg", [11, 128, 4096], F32, kind="ExternalInput")
    wdn_d = nc.dram_tensor("w_dn", [8, 128, 2816], F32, kind="ExternalInput")
    vec_d = nc.dram_tensor("vecs", [128, NV], F32, kind="ExternalInput")
    tab_d = nc.dram_tensor("tabs", [128, 2 * SEQ], F32, kind="ExternalInput")
    cm_d = nc.dram_tensor("cmat", [128, 512], F32, kind="ExternalInput")
    yT_d = nc.dram_tensor("yT", [NSEQ, TT, 128, 4096], F32, kind="ExternalOutput")
    x1_d = nc.dram_tensor("x1s", [NSEQ, TT, 128, 4096], F32)
    wdnb_d = nc.dram_tensor("wdn_bf16", [8, 128, 2816], BF16)

    es = ExitStack()
    S = Sched()
    with es:
        sb = lambda name, shape, dt: es.enter_context(nc.sbuf_tensor(name, shape, dt))
        A = sb("A", [128, 8 * SEQ], BF16)
        O = sb("O", [128, 8 * SEQ], BF16)
        R1 = sb("R1", [128, 8 * SEQ], BF16)
        M = sb("M", [128, 8 * SEQ], BF16)
        tabs = sb("tabs_sb", [128, 2 * SEQ], F32)
        ring = [sb(f"ring{i}", [128, 4096], BF16) for i in range(NSLOT)]
        T = [sb(f"T{i}", [128, 4608], F32) for i in range(2)]
        vecs = sb("vecs_sb", [128, NV], F32)
        cmat = sb("cmat_sb", [128, 512], BF16)
        epsb = sb("epsb", [128, 1], F32)
        PS = [es.enter_context(nc.psum_tensor(f"ps{i}", [128, 1024], F32)) for i in range(4)]

        A_f = A.bitcast(F32)
        O_f = O.bitcast(F32)
        R1_f = R1.bitcast(F32)
        M_f = M.bitcast(F32)
        T_b = [t.bitcast(BF16) for t in T]
        tabs_b = tabs.bitcast(BF16)

        r_A = [Res(f"A{t}") for t in range(TT)]
        r_O = [[Res(f"O{h}_{t}") for t in range(TT)] for h in range(8)]
        r_R1 = [Res(f"R1_{c}") for c in range(8)]
        r_Mb = [Res(f"Mb{b}") for b in range(16)]
        r_ring = [Res(f"ring{i}") for i in range(NSLOT)]
        r_T = [[Res(f"T{i}_{b}") for b in range(9)] for i in range(2)]
        r_bank = [[Res(f"ps{i}_{h}", excl=True) for h in range(2)] for i in range(4)]
        r_consts = [Res("eps"), Res("vec"), Res("cm")]
        r_tabs = Res("tabs")
        r_wo = [Res("wo0"), Res("wo1")]
        r_x1d = [[Res(f"x1d{s}_{t}") for t in range(TT)] for s in range(NSEQ)]
        r_wdnb = [Res(f"wdnb{d}") for d in range(8)]

        ones1024 = cmat[:, 0:128]
        ones128 = cmat[:, 128:256]
        ones1 = cmat[:, 256:384]
        pmatT = cmat[:, 384:512]

        def vcol(i):
            return vecs[:, i:i + 1]

        bank_ap = lambda i, h: PS[i][:, h * 512:(h + 1) * 512]

        ring_state = {"n": 0}

        ring_pinned = set()

        def load_slot(src_ap, L, reads=(), nb=None):
            while True:
                s = ring_state["n"] % NSLOT
                ring_state["n"] += 1
                if s not in ring_pinned:
                    break
            if nb is None:
                nb = L // 1024 if L % 1024 == 0 else L // 704
            bl = L // nb
            S.dma("pool", lambda e, s=s: e.dma_start(
                out=ring[s][:, 0:L].rearrange("p (a b) -> p a b", b=bl),
                in_=src_ap.rearrange("p (a b) -> p a b", b=bl)), f"ring{s}", reads=list(reads), writes=[r_ring[s]])
            return s

        bank_state = {"n": 0}

        bank_reserved = set()

        def next_bank():
            while True:
                n = bank_state["n"] % 8
                bank_state["n"] += 1
                if (n // 2, n % 2) not in bank_reserved:
                    return n // 2, n % 2

        def mm_group(out_ap, out_res, pairs, reads, inc_last=True):
            n = len(pairs)
            for idx, (l, r) in enumerate(pairs):
                S.op("pe", lambda e, l=l, r=r, idx=idx: e.matmul(out_ap, lhsT=l, rhs=r, start=(idx == 0), stop=(idx == n - 1)),
                     reads=reads, writes=[out_res], inc=(inc_last and idx == n - 1))

        def rstd_from(ps_ap, ps_res, out_ap, out_res):
            S.op("act", lambda e: e.activation(out=out_ap, in_=ps_ap, func=AF.Ln, bias=epsb[:, 0:1]),
                 reads=[ps_res, *r_consts], writes=[out_res])
            S.op("act", lambda e: e.activation(out=out_ap, in_=out_ap, func=AF.Exp, scale=-0.5),
                 reads=[out_res], writes=[out_res])

        S.op("dve", lambda e: e.memset(epsb[:], EPS), writes=[r_consts[0]])
        S.dma("sp", lambda e: e.dma_start(out=vecs[:], in_=vec_d[:, :]), "ld_c0", writes=[r_consts[1]])
        S.dma("sp", lambda e: e.dma_start(out=tabs[:], in_=tab_d[:, :]), "ld_c1", writes=[r_tabs])
        S.dma("pool", lambda e: e.dma_start(out=cmat[:], in_=cm_d[:, :]), "ld_c2", writes=[r_consts[2]])
        Ctab = tabs[:, 0:SEQ]
        Stab = tabs[:, SEQ:2 * SEQ]

        out_evs = []
        qk_ctr = {"n": 0}

        qk_pend = {"p": None}

        def qk_pre(pairs, reads):
            k = qk_ctr["n"] % 2
            qk_ctr["n"] += 1
            base = k * 4
            bi, bh = next_bank()
            bank_reserved.add((bi, bh))
            ps_ap, ps_res = bank_ap(bi, bh), r_bank[bi][bh]
            mm_group(ps_ap, ps_res, pairs, reads)
            qb = M[:, base * 1024: base * 1024 + 512]
            sq = M[:, base * 1024 + 512: base * 1024 + 1024]
            S.op("act", lambda e: e.activation(out=qb, in_=ps_ap, func=AF.Copy), reads=[ps_res], writes=[r_Mb[base]])
            S.op("act", lambda e: e.activation(out=sq, in_=ps_ap, func=AF.Square), reads=[ps_res], writes=[r_Mb[base]])
            return base, bi, bh

        def qk_fin(st, gcol, gpcol, dest_ap, dest_res, tt):
            base, pi_, ph_ = st
            ps_ap, ps_res = bank_ap(pi_, ph_), r_bank[pi_][ph_]
            qb = M[:, base * 1024: base * 1024 + 512]
            sq = M[:, base * 1024 + 512: base * 1024 + 1024]
            rs = M_f[:, (base + 1) * 512:(base + 2) * 512]
            t1 = M_f[:, (base + 2) * 512:(base + 3) * 512]
            t2 = M_f[:, (base + 3) * 512:(base + 4) * 512]
            r_qb = r_sq = r_Mb[base]
            r_rs, r_t1, r_t2 = r_Mb[base + 1], r_Mb[base + 2], r_Mb[base + 3]
            bi, bh = next_bank()
            mm_group(bank_ap(bi, bh), r_bank[bi][bh], [(ones128, sq)], [r_sq, *r_consts])
            ci, ch = next_bank()
            mm_group(bank_ap(ci, ch), r_bank[ci][ch], [(pmatT, qb)], [r_qb, *r_consts])
            rstd_from(bank_ap(bi, bh), r_bank[bi][bh], rs, r_rs)
            sl = slice(tt * 512, (tt + 1) * 512)
            S.op("dve", lambda e: e.scalar_tensor_tensor(out=t1, in0=ps_ap, scalar=vcol(gcol), in1=Ctab[:, sl],
                                                         op0=ALU.mult, op1=ALU.mult),
                 reads=[ps_res, *r_consts, r_tabs], writes=[r_t1])
            S.op("dve", lambda e: e.scalar_tensor_tensor(out=t2, in0=bank_ap(ci, ch), scalar=vcol(gpcol), in1=Stab[:, sl],
                                                         op0=ALU.mult, op1=ALU.mult),
                 reads=[r_bank[ci][ch], *r_consts, r_tabs], writes=[r_t2])
            S.op("dve", lambda e: e.tensor_tensor(out=t1, in0=t1, in1=t2, op=ALU.add), reads=[r_t1, r_t2], writes=[r_t1])
            S.op("dve", lambda e: e.tensor_tensor(out=dest_ap, in0=t1, in1=rs, op=ALU.mult),
                 reads=[r_t1, r_rs], writes=[dest_res])
            bank_reserved.discard((pi_, ph_))

        def qk_item(pairs, reads, gcol, gpcol, dest_ap, dest_res, tt):
            st = qk_pre(pairs, reads)
            if qk_pend["p"] is not None:
                qk_fin(*qk_pend["p"])
            qk_pend["p"] = (st, gcol, gpcol, dest_ap, dest_res, tt)

        def qk_flush():
            if qk_pend["p"] is not None:
                qk_fin(*qk_pend["p"])
                qk_pend["p"] = None

        def norm_tile(xt_f, xt_res, g0, dest_ap_fn, dest_res, sq_b, sq_res, rs_ap, rs_res):
            S.op("act", lambda e: e.activation(out=sq_b, in_=xt_f, func=AF.Square), reads=xt_res, writes=sq_res)
            bi, bh = next_bank()
            mm_group(bank_ap(bi, bh), r_bank[bi][bh],
                     [(ones1024, sq_b[:, c * 512:(c + 1) * 512]) for c in range(8)], list(sq_res) + [*r_consts])
            rstd_from(bank_ap(bi, bh), r_bank[bi][bh], rs_ap, rs_res)
            for c in range(8):
                S.op("dve", lambda e, c=c: e.scalar_tensor_tensor(
                    out=dest_ap_fn(c), in0=xt_f[:, c * 512:(c + 1) * 512], scalar=vcol(g0 + c), in1=rs_ap,
                    op0=ALU.mult, op1=ALU.mult),
                    reads=list(xt_res) + [rs_res, *r_consts], writes=[dest_res])

        def emit_ssq(d, sbi, sbh, sqd):
            S.op("pe", lambda e: e.matmul(bank_ap(sbi, sbh), lhsT=ones1024, rhs=sqd[:, d * 512:(d + 1) * 512],
                                          start=(d == 0), stop=(d == 7)),
                 reads=[r_T[1][d // 2], *r_consts], writes=[r_bank[sbi][sbh]], inc=True)

        x_bufs = [(M_f[:, 0:4096], r_Mb[0:8]),
                  (O_f[:, 4096:8192], [r_O[h][t] for h in range(4, 8) for t in range(TT)]),
                  (R1_f[:, 4096:8192], r_R1[4:8]),
                  (O_f[:, 0:4096], [r_O[h][t] for h in range(4) for t in range(TT)])]

        def emit_x_loads(sq_, eng):
            for tt in range(TT):
                xt, xres = x_bufs[tt]
                S.dma(eng, lambda e, xt=xt, sq_=sq_, tt=tt: e.dma_start(out=xt, in_=xT_d[sq_, tt]), f"ld_x{tt}_{eng}",
                      reads=(list(x_bufs[tt - 1][1]) if (tt > 0 and eng == "pool") else []), writes=xres)

        for s in range(NSEQ):
            if s == 0:
                emit_x_loads(0, "sp")
            for tt in range(TT):
                xt, xres = x_bufs[tt]
                sqb = T_b[1][:, 0:4096]
                norm_tile(xt, xres, V_PRE,
                          lambda c, tt=tt: A[:, c * SEQ + tt * 512: c * SEQ + (tt + 1) * 512], r_A[tt],
                          sqb, r_T[1][0:4], T[1][:, 2560:3072], r_T[1][5])

            kT = lambda kvh: R1[:, kvh * SEQ:(kvh + 1) * SEQ]
            Vt = R1[:, 2 * SEQ:4 * SEQ]
            sl_kv = load_slot(wqkv_d[2], 4096)
            rg = ring[sl_kv]
            for tt in range(TT):
                for kvh in range(2):
                    qk_item([(rg[:, c * 512 + kvh * 128: c * 512 + (kvh + 1) * 128],
                              A[:, c * SEQ + tt * 512: c * SEQ + (tt + 1) * 512]) for c in range(8)],
                            [r_ring[sl_kv], r_A[tt]], V_KG, V_KGP,
                            kT(kvh)[:, tt * 512:(tt + 1) * 512], r_R1[kvh], tt)
            for kt2 in range(8):
                bi, bh = next_bank()
                for q in range(2):
                    kt = kt2 * 2 + q
                    mm_group(PS[bi][:, bh * 512 + q * 256: bh * 512 + (q + 1) * 256], r_bank[bi][bh],
                             [(A[:, c * SEQ + kt * 128: c * SEQ + (kt + 1) * 128],
                               rg[:, c * 512 + 256: c * 512 + 512]) for c in range(8)],
                             [r_ring[sl_kv], r_A[kt // 4]])
                S.op("act", lambda e, bi=bi, bh=bh, kt2=kt2: e.activation(
                    out=Vt[:, kt2 * 512:(kt2 + 1) * 512], in_=bank_ap(bi, bh), func=AF.Copy),
                    reads=[r_bank[bi][bh]], writes=[r_R1[2 + kt2 // 4]])

            def p_ap(j):
                return R1[:, (4 + j) * SEQ:(5 + j) * SEQ] if j < 4 else M[:, j * SEQ:(j + 1) * SEQ]

            def p_res(j):
                return [r_R1[4 + j]] if j < 4 else [r_Mb[2 * j], r_Mb[2 * j + 1]]

            S.op("dve", lambda e: e.memset(T[0][:, 0:1], 0.0), writes=[r_T[0][0]])
            S.op("dve", lambda e: e.memset(T[0][:, 2049:2050], 0.0), writes=[r_T[0][4]])
            c_wrow = T[0]
            c_yrow = T[0][:, 2560:4608]
            c_brow = T[1][:, 2560:4608]
            sl_q = None
            for j in range(8):
                if j % 4 == 0:
                    if sl_q is not None:
                        ring_pinned.discard(sl_q)
                    sl_q = load_slot(wqkv_d[j // 4], 4096)
                    ring_pinned.add(sl_q)
                rgq = ring[sl_q]
                hq = j % 4
                sl = load_slot(wubc_d[j], 3072)
                rg = ring[sl]
                for tt in range(TT):
                    qk_item([(rgq[:, c * 512 + hq * 128: c * 512 + (hq + 1) * 128],
                              A[:, c * SEQ + tt * 512: c * SEQ + (tt + 1) * 512]) for c in range(8)],
                            [r_ring[sl_q], r_A[tt]], V_QG, V_QGP,
                            O[:, j * SEQ + tt * 512: j * SEQ + (tt + 1) * 512], r_O[j][tt], tt)
                    bks = []
                    for k in (2, 0, 1):
                        bi, bh = next_bank()
                        mm_group(bank_ap(bi, bh), r_bank[bi][bh],
                                 [(rg[:, c * 384 + k * 128: c * 384 + (k + 1) * 128],
                                   A[:, c * SEQ + tt * 512: c * SEQ + (tt + 1) * 512]) for c in range(8)],
                                 [r_ring[sl], r_A[tt]])
                        bks.append((bi, bh))
                    (cbi, cbh), (ubi, ubh), (bbi, bbh) = bks
                    cs = T[1][:, (tt % 2) * 512:(tt % 2 + 1) * 512]
                    r_cs = r_T[1][tt % 2]
                    S.op("act", lambda e, cs=cs, cbi=cbi, cbh=cbh: e.activation(out=cs, in_=bank_ap(cbi, cbh), func=AF.Copy),
                         reads=[r_bank[cbi][cbh]], writes=[r_cs])
                    S.op("dve", lambda e, cs=cs, ubi=ubi, ubh=ubh, tt=tt: e.tensor_tensor(
                        out=c_wrow[:, 1 + tt * 512: 1 + (tt + 1) * 512], in0=bank_ap(ubi, ubh), in1=cs, op=ALU.mult),
                        reads=[r_bank[ubi][ubh], r_cs], writes=[r_T[0][tt], r_T[0][tt + 1]])
                    S.op("act", lambda e, bbi=bbi, bbh=bbh, tt=tt: e.activation(
                        out=c_brow[:, tt * 512:(tt + 1) * 512], in_=bank_ap(bbi, bbh), func=AF.Copy),
                        reads=[r_bank[bbi][bbh]], writes=[r_T[1][5 + tt]])
                rw = r_T[0][0:5]
                ry = r_T[0][5:9]
                S.op("dve", lambda e, j=j: e.tensor_scalar(out=c_yrow, in0=c_wrow[:, 0:2048], scalar1=vcol(V_MCW + j), scalar2=None,
                                                           op0=ALU.mult), reads=rw + [*r_consts], writes=ry)
                S.op("dve", lambda e, j=j: e.scalar_tensor_tensor(out=c_yrow, in0=c_wrow[:, 1:2049], scalar=vcol(V_MCW + 8 + j),
                                                                  in1=c_yrow, op0=ALU.mult, op1=ALU.add),
                     reads=rw + ry + [*r_consts], writes=ry)
                S.op("dve", lambda e, j=j: e.scalar_tensor_tensor(out=c_yrow, in0=c_wrow[:, 2:2050], scalar=vcol(V_MCW + 16 + j),
                                                                  in1=c_yrow, op0=ALU.mult, op1=ALU.add),
                     reads=rw + ry + [*r_consts], writes=ry)
                S.op("dve", lambda e, j=j: e.tensor_tensor(out=p_ap(j), in0=c_yrow, in1=c_brow, op=ALU.mult),
                     reads=ry + r_T[1][5:9], writes=p_res(j))
            qk_flush()
            ring_pinned.discard(sl_q)
            for blk in range(2):
                S.dma("pool", lambda e, blk=blk: e.dma_start(
                    out=tabs_b[:, blk * 4096:(blk + 1) * 4096].rearrange("p (a b) -> p a b", b=1024),
                    in_=wo_d[blk].rearrange("p (a b) -> p a b", b=1024)), f"ld_wo{blk}",
                    writes=[r_wo[blk], r_tabs])

            if s == 0:
                for d in range(8):
                    S.dma("pool", lambda e, d=d: e.dma_start(
                        out=wdnb_d[d].rearrange("p (a b) -> p a b", b=704),
                        in_=wdn_d[d].rearrange("p (a b) -> p a b", b=704)), f"cv_wdn{d}", reads=[r_O[7][3]], writes=[r_wdnb[d]])

            pairs = [(hh, m2) for hh in range(NH) for m2 in range(2)]

            def actx(p):
                hh, m2 = pairs[p]
                qts = (2 * m2, 2 * m2 + 1)
                return dict(kvh=hh // 4,
                            qaps=[O[:, hh * SEQ + qt * 512: hh * SEQ + (qt + 1) * 512] for qt in qts],
                            r_qs=[r_O[hh][qt] for qt in qts])

            def emit_qk(cx, st, kp):
                kvh, qap = cx["kvh"], cx["qaps"][st]
                for q in range(2):
                    kt = kp * 2 + q
                    S.op("pe", lambda e, st=st, q=q, kt=kt, kvh=kvh, qap=qap: e.matmul(
                        PS[st][:, q * 512:(q + 1) * 512], lhsT=kT(kvh)[:, kt * 128:(kt + 1) * 128], rhs=qap,
                        start=True, stop=True),
                        reads=[r_R1[kvh], cx["r_qs"][st]], writes=[r_bank[st][0], r_bank[st][1]], inc=(q == 1))

            def pt_buf(st, kp):
                b = st * 2 + (kp % 2)
                return T_b[0][:, b * 1024:(b + 1) * 1024], r_T[0][b]

            def pa_buf(st, kp):
                b = 4 + st * 2 + (kp % 2)
                return T_b[0][:, b * 1024:b * 1024 + 512], r_T[0][b]

            def emit_exp(st, kp):
                PT, r_PT = pt_buf(st, kp)
                S.op("act", lambda e, PT=PT, st=st: e.activation(out=PT, in_=PS[st][:, :], func=AF.Exp, scale=SCALE),
                     reads=[r_bank[st][0], r_bank[st][1]], writes=[r_PT])
                pa, r_pa = pa_buf(st, kp)
                S.op("dve", lambda e, PT=PT, pa=pa: e.tensor_tensor(out=pa, in0=PT[:, 0:512], in1=PT[:, 512:1024],
                                                                   op=ALU.add), reads=[r_PT], writes=[r_pa])

            def emit_pv(cx, st, kp):
                PT, r_PT = pt_buf(st, kp)
                oacc = bank_ap(2, st)
                kvh = cx["kvh"]
                for q in range(2):
                    kt = kp * 2 + q
                    S.op("pe", lambda e, q=q, kt=kt, oacc=oacc, kvh=kvh, PT=PT: e.matmul(
                        oacc, lhsT=Vt[:, kt * 256 + kvh * 128: kt * 256 + (kvh + 1) * 128],
                        rhs=PT[:, q * 512:(q + 1) * 512], start=(kt == 0), stop=(kt == 15)),
                        reads=[r_PT, r_R1[2 + kt // 8]], writes=[r_bank[2][st]], inc=(q == 1))

            def emit_den(st, kp):
                pa, r_pa = pa_buf(st, kp)
                den = bank_ap(3, st)
                S.op("pe", lambda e, pa=pa, den=den, kp=kp: e.matmul(den, lhsT=ones1, rhs=pa, start=(kp == 0), stop=(kp == 7)),
                     reads=[r_pa, *r_consts], writes=[r_bank[3][st]], inc=True)

            def emit_epilogue(cx, st):
                rd = T[1][:, st * 512:(st + 1) * 512]
                r_rd = r_T[1][st]
                S.op("act", lambda e, rd=rd, st=st: e.activation(out=rd, in_=bank_ap(3, st), func=AF.Ln),
                     reads=[r_bank[3][st]], writes=[r_rd])
                S.op("act", lambda e, rd=rd: e.activation(out=rd, in_=rd, func=AF.Exp, scale=-1.0),
                     reads=[r_rd], writes=[r_rd])
                S.op("dve", lambda e, rd=rd, st=st, qap=cx["qaps"][st]: e.tensor_tensor(
                    out=qap, in0=bank_ap(2, st), in1=rd, op=ALU.mult),
                    reads=[r_bank[2][st], r_rd], writes=[cx["r_qs"][st]])

            cx = actx(0)
            for st in range(2):
                emit_qk(cx, st, 0)
            for p in range(len(pairs)):
                cx = actx(p)
                cx_next = actx(p + 1) if p + 1 < len(pairs) else None
                for kp in range(8):
                    if kp == 0 and p >= 1:
                        cxp = actx(p - 1)
                        emit_exp(0, 0)
                        emit_epilogue(cxp, 0)
                        emit_exp(1, 0)
                        emit_epilogue(cxp, 1)
                    else:
                        for st in range(2):
                            emit_exp(st, kp)
                    for st in range(2):
                        if kp + 1 < 8:
                            emit_qk(cx, st, kp + 1)
                        elif cx_next is not None:
                            emit_qk(cx_next, st, 0)
                        emit_pv(cx, st, kp)
                        if kp >= 1:
                            emit_den(st, kp - 1)
                        if kp == 7:
                            emit_den(st, 7)
            cxl = actx(len(pairs) - 1)
            emit_epilogue(cxl, 0)
            emit_epilogue(cxl, 1)

            def mg_ap(j, tt):
                if j < 4:
                    return R1[:, j * SEQ + tt * 512: j * SEQ + (tt + 1) * 512]
                return M[:, (j - 4) * SEQ + tt * 512:(j - 4) * SEQ + (tt + 1) * 512]

            def mg_res(j, tt):
                return [r_R1[j]] if j < 4 else [r_Mb[((j - 4) * 4 + tt) // 2]]

            def p_tile_res(c, tt):
                return [r_R1[4 + c]] if c < 4 else [r_Mb[(c * 4 + tt) // 2]]

            for j in range(8):
                sl = load_slot(wpj_d[j], 4096)
                rg = ring[sl]
                for tt in range(TT):
                    tsl = lambda c, tt=tt: slice(c * SEQ + tt * 512, c * SEQ + (tt + 1) * 512)
                    src_fns = (
                        (lambda c, tt=tt: O[:, tsl(c)], lambda c, tt=tt: [r_O[c][tt]]),
                        (lambda c, tt=tt: p_ap(c)[:, tt * 512:(tt + 1) * 512], lambda c, tt=tt: p_tile_res(c, tt)),
                        (lambda c, tt=tt: A[:, tsl(c)], lambda c, tt=tt: [r_A[tt]]),
                        (lambda c, tt=tt: A[:, tsl(c)], lambda c, tt=tt: [r_A[tt]]),
                    )
                    bk = []
                    for k, (ap_fn, rs_fn) in enumerate(src_fns):
                        bi, bh = next_bank()
                        rds = [r_ring[sl]]
                        for c in range(8):
                            for r_ in rs_fn(c):
                                if r_ not in rds:
                                    rds.append(r_)
                        mm_group(bank_ap(bi, bh), r_bank[bi][bh],
                                 [(rg[:, c * 512 + k * 128: c * 512 + (k + 1) * 128], ap_fn(c)) for c in range(8)], rds)
                        bk.append((bi, bh))
                    sa = T[tt % 2][:, 0:512]
                    sbg = T[tt % 2][:, 512:1024]
                    m1 = T[tt % 2][:, 1024:1536]
                    r_sa, r_sb, r_m1 = r_T[tt % 2][0], r_T[tt % 2][1], r_T[tt % 2][2]
                    S.op("act", lambda e, sa=sa, b=bk[2], j=j: e.activation(out=sa, in_=bank_ap(*b), func=AF.Sigmoid,
                                                                             bias=vcol(V_GBA + j)),
                         reads=[r_bank[bk[2][0]][bk[2][1]], *r_consts], writes=[r_sa])
                    S.op("act", lambda e, sbg=sbg, b=bk[3], j=j: e.activation(out=sbg, in_=bank_ap(*b), func=AF.Sigmoid,
                                                                               bias=vcol(V_GBB + j)),
                         reads=[r_bank[bk[3][0]][bk[3][1]], *r_consts], writes=[r_sb])
                    S.op("dve", lambda e, sa=sa, m1=m1, b=bk[0]: e.tensor_tensor(out=m1, in0=bank_ap(*b), in1=sa, op=ALU.mult),
                         reads=[r_bank[bk[0][0]][bk[0][1]], r_sa], writes=[r_m1])
                    S.op("dve", lambda e, sbg=sbg, b=bk[1]: e.tensor_tensor(out=sbg, in0=bank_ap(*b), in1=sbg, op=ALU.mult),
                         reads=[r_bank[bk[1][0]][bk[1][1]], r_sb], writes=[r_sb])
                    S.op("dve", lambda e, sbg=sbg, m1=m1, j=j, tt=tt: e.tensor_tensor(
                        out=mg_ap(j, tt), in0=m1, in1=sbg, op=ALU.add),
                        reads=[r_m1, r_sb], writes=mg_res(j, tt))

            wo_res = lambda d: [r_wo[d // 4], r_tabs]
            out_state = {}

            def mo_tile(tt):
                if tt % 2 == 0:
                    return T[0][:, 0:4096], r_T[0][0:8]
                return O_f[:, 4096:8192], [r_O[h][t] for h in range(4, 8) for t in range(TT)]

            def mo_buf(tt, d):
                if tt % 2 == 0:
                    return T[0][:, d * 512:(d + 1) * 512], [r_T[0][d]]
                return (O_f[:, 4096 + d * 512: 4096 + (d + 1) * 512],
                        [r_O[4 + d // 2][(d % 2) * 2], r_O[4 + d // 2][(d % 2) * 2 + 1]])

            SQB = (4, 7, 8)

            def out_ssq(tt, d):
                xt2, xres, sbi, sbh = out_state[tt]
                b = SQB[d % 3]
                S.op("pe", lambda e: e.matmul(bank_ap(sbi, sbh), lhsT=ones1024, rhs=T_b[1][:, b * 1024: b * 1024 + 512],
                                              start=(d == 0), stop=(d == 7)),
                     reads=[r_T[1][b], *r_consts], writes=[r_bank[sbi][sbh]], inc=True)

            def out_A(tt, d0, d1):
                hb = tt % 2
                xt2 = R1_f[:, 4096:8192] if hb == 0 else M_f[:, 4096:8192]
                xres = r_R1[4:8] if hb == 0 else r_Mb[8:16]
                if d0 == 0:
                    sbi, sbh = next_bank()
                    bank_reserved.add((sbi, sbh))
                    out_state[tt] = (xt2, xres, sbi, sbh)
                if d1 == 8:
                    S.dma("sp", lambda e, xt2=xt2, s=s, tt=tt: e.dma_start(out=xt2, in_=xT_d[s, tt]), f"ld_y{hb}", writes=xres)
                for d in range(d0, d1):
                    bi, bh = next_bank()
                    mm_group(bank_ap(bi, bh), r_bank[bi][bh],
                             [(tabs_b[:, (d // 4) * 4096 + c * 512 + (d % 4) * 128: (d // 4) * 4096 + c * 512 + (d % 4 + 1) * 128],
                               mg_ap(c, tt)) for c in range(8)],
                             wo_res(d) + [r_ for c in range(8) for r_ in mg_res(c, tt)])
                    if d >= 1:
                        out_ssq(tt, d - 1)
                    mo_ap, mo_res = mo_buf(tt, d)
                    S.op("act", lambda e, mo_ap=mo_ap, bi=bi, bh=bh, d=d: e.activation(
                        out=mo_ap, in_=bank_ap(bi, bh), func=AF.Copy, scale=vcol(V_POST + d)),
                        reads=[r_bank[bi][bh], *r_consts], writes=mo_res)
                    b = SQB[d % 3]
                    S.op("act", lambda e, b=b, bi=bi, bh=bh: e.activation(out=T_b[1][:, b * 1024: b * 1024 + 512],
                                                                          in_=bank_ap(bi, bh), func=AF.Square),
                         reads=[r_bank[bi][bh]], writes=[r_T[1][b]])
                if d1 == 8:
                    out_ssq(tt, 7)

            def out_B1(tt):
                xt2, xres, sbi, sbh = out_state[tt]
                hb = tt % 2
                rs = T[1][:, 2560:3072]
                rstd_from(bank_ap(sbi, sbh), r_bank[sbi][sbh], rs, r_T[1][5])
                bank_reserved.discard((sbi, sbh))
                mt, mt_res = mo_tile(tt)
                mt3 = mt.rearrange("p (c n) -> p c n", n=512)
                rs3 = rs.unsqueeze(1).to_broadcast([128, 8, 512])
                S.op("dve", lambda e, mt3=mt3, rs3=rs3: e.tensor_tensor(out=mt3, in0=mt3, in1=rs3, op=ALU.mult),
                     reads=mt_res + [r_T[1][5]], writes=mt_res)
                S.op("dve", lambda e, mt=mt, xt2=xt2: e.tensor_tensor(out=xt2, in0=mt, in1=xt2, op=ALU.add),
                     reads=mt_res + xres, writes=xres)
                S.dma("sp", lambda e, xt2=xt2, s=s, tt=tt: e.dma_start(out=x1_d[s, tt], in_=xt2), f"st_x1{hb}",
                      reads=xres, writes=[r_x1d[s][tt]])

            def out_C1(tt):
                xt2, xres, sbi, sbh = out_state[tt]
                S.op("act", lambda e, xt2=xt2: e.activation(out=T_b[1][:, 0:4096], in_=xt2, func=AF.Square),
                     reads=xres, writes=r_T[1][0:4])

            def out_C2(tt):
                bi, bh = next_bank()
                bank_reserved.add((bi, bh))
                mm_group(bank_ap(bi, bh), r_bank[bi][bh],
                         [(ones1024, T_b[1][:, c * 512:(c + 1) * 512]) for c in range(8)], r_T[1][0:4] + [*r_consts])
                out_state[("c", tt)] = (bi, bh)

            def out_C3(tt):
                xt2, xres, sbi, sbh = out_state[tt]
                bi, bh = out_state[("c", tt)]
                rs2 = T[1][:, 3072:3584]
                rstd_from(bank_ap(bi, bh), r_bank[bi][bh], rs2, r_T[1][6])
                bank_reserved.discard((bi, bh))
                for c in range(8):
                    S.op("dve", lambda e, c=c, xt2=xt2, tt=tt: e.scalar_tensor_tensor(
                        out=A[:, c * SEQ + tt * 512: c * SEQ + (tt + 1) * 512], in0=xt2[:, c * 512:(c + 1) * 512],
                        scalar=vcol(V_FPRE + c), in1=rs2, op0=ALU.mult, op1=ALU.mult),
                        reads=list(xres) + [r_T[1][6], *r_consts], writes=[r_A[tt]])

            out_A(0, 0, 8)
            out_B1(0)
            out_A(1, 0, 8)
            out_C1(0)
            out_B1(1)
            for tt in range(2, TT):
                out_A(tt, 0, 4)
                out_C2(tt - 2)
                out_C3(tt - 2)
                out_A(tt, 4, 8)
                out_C1(tt - 1)
                out_B1(tt)
            out_C2(TT - 2)
            out_C3(TT - 2)
            out_C1(TT - 1)
            out_C2(TT - 1)
            out_C3(TT - 1)
            if s + 1 < NSEQ:
                S.dma("sp", lambda e: e.dma_start(out=tabs[:], in_=tab_d[:, :]), "ld_c1r", writes=[r_tabs])

            hid_buf = lambda j: (O, R1, M)[j // 8]

            def hid_res(j, tt):
                if j < 8:
                    return [r_O[j][tt]]
                if j < 16:
                    return [r_R1[j % 8]]
                return [r_Mb[((j % 8) * 4 + tt) // 2]]

            for i in range(2):
                S.op("dve", lambda e, i=i: e.memset(T[i][:, 0:1], 0.0), writes=[r_T[i][0]])
                S.op("dve", lambda e, i=i: e.memset(T[i][:, 2049:2050], 0.0), writes=[r_T[i][4]])
            for g in range(11):
                sl = load_slot(wup_d[g], 4096)
                rg = ring[sl]
                for jj in range(2):
                    j = 2 * g + jj
                    ti = j % 2
                    arow = T[ti]
                    yrow = T[ti][:, 2560:4608]
                    ra = r_T[ti][0:5]
                    ry = r_T[ti][5:9]
                    for tt in range(TT):
                        bi, bh = next_bank()
                        mm_group(bank_ap(bi, bh), r_bank[bi][bh],
                                 [(rg[:, c * 512 + jj * 128: c * 512 + (jj + 1) * 128],
                                   A[:, c * SEQ + tt * 512: c * SEQ + (tt + 1) * 512]) for c in range(8)],
                                 [r_ring[sl], r_A[tt]])
                        S.op("act", lambda e, bi=bi, bh=bh, tt=tt, arow=arow: e.activation(
                            out=arow[:, 1 + tt * 512: 1 + (tt + 1) * 512], in_=bank_ap(bi, bh), func=AF.Copy),
                            reads=[r_bank[bi][bh]], writes=[r_T[ti][tt], r_T[ti][tt + 1]])
                    bb = []
                    for tt in range(TT):
                        bi, bh = next_bank()
                        mm_group(bank_ap(bi, bh), r_bank[bi][bh],
                                 [(rg[:, c * 512 + 256 + jj * 128: c * 512 + 256 + (jj + 1) * 128],
                                   A[:, c * SEQ + tt * 512: c * SEQ + (tt + 1) * 512]) for c in range(8)],
                                 [r_ring[sl], r_A[tt]])
                        bb.append((bi, bh))
                    S.op("dve", lambda e, j=j, arow=arow, yrow=yrow: e.tensor_scalar(
                        out=yrow, in0=arow[:, 0:2048], scalar1=vcol(V_FCW + j), scalar2=None, op0=ALU.mult),
                        reads=ra + [*r_consts], writes=ry)
                    S.op("dve", lambda e, j=j, arow=arow, yrow=yrow: e.scalar_tensor_tensor(
                        out=yrow, in0=arow[:, 1:2049], scalar=vcol(V_FCW + NJ + j), in1=yrow, op0=ALU.mult, op1=ALU.add),
                        reads=ra + ry + [*r_consts], writes=ry)
                    S.op("dve", lambda e, j=j, arow=arow, yrow=yrow: e.scalar_tensor_tensor(
                        out=yrow, in0=arow[:, 2:2050], scalar=vcol(V_FCW + 2 * NJ + j), in1=yrow, op0=ALU.mult, op1=ALU.add),
                        reads=ra + ry + [*r_consts], writes=ry)
                    S.op("act", lambda e, yrow=yrow: e.activation(out=yrow, in_=yrow, func=AF.Gelu_apprx_tanh),
                         reads=ry, writes=ry)
                    hb_ = hid_buf(j)
                    for tt in range(TT):
                        bi, bh = bb[tt]
                        S.op("dve", lambda e, bi=bi, bh=bh, tt=tt, yrow=yrow, hb_=hb_, j=j: e.tensor_tensor(
                            out=hb_[:, (j % 8) * SEQ + tt * 512:(j % 8) * SEQ + (tt + 1) * 512],
                            in0=bank_ap(bi, bh), in1=yrow[:, tt * 512:(tt + 1) * 512], op=ALU.mult),
                            reads=[r_bank[bi][bh], r_T[ti][5 + tt]], writes=hid_res(j, tt))

            def o2_buf(tt, d):
                if tt % 2 == 0:
                    return T[0][:, d * 512:(d + 1) * 512], [r_T[0][d]]
                if d < 4:
                    return M_f[:, (12 + d) * 512:(13 + d) * 512], [r_Mb[12 + d]]
                b = 4 if d == 7 else 6 + (d - 4)
                return T[1][:, b * 512:(b + 1) * 512], [r_T[1][b]]

            r_x1h = [Res(f"x1h{s}_0"), Res(f"x1h{s}_1")]
            for tt in range(TT):
                hb = tt % 2
                x1t = A_f[:, hb * 4096:(hb + 1) * 4096]
                x1res = [r_x1h[hb]]
                S.dma("sp", lambda e, x1t=x1t, s=s, tt=tt: e.dma_start(out=x1t, in_=x1_d[s, tt]), f"ld_x1{hb}",
                      reads=[r_x1d[s][tt]], writes=x1res + (r_A if tt < 2 else []))
                sqd = T_b[1][:, 0:4096]
                sbi, sbh = next_bank()
                for d in range(8):
                    sl = load_slot(wdnb_d[d], 2816, reads=[r_wdnb[d]], nb=1)
                    rg = ring[sl]
                    bi, bh = next_bank()
                    if (bi, bh) == (sbi, sbh):
                        bi, bh = next_bank()
                    rds = [r_ring[sl]]
                    for j in range(NJ):
                        for r_ in hid_res(j, tt):
                            if r_ not in rds:
                                rds.append(r_)
                    mm_group(bank_ap(bi, bh), r_bank[bi][bh],
                             [(rg[:, j * 128:(j + 1) * 128],
                               hid_buf(j)[:, (j % 8) * SEQ + tt * 512:(j % 8) * SEQ + (tt + 1) * 512]) for j in range(NJ)], rds)
                    if d >= 1:
                        emit_ssq(d - 1, sbi, sbh, sqd)
                    o2_ap, o2_res = o2_buf(tt, d)
                    S.op("act", lambda e, o2_ap=o2_ap, bi=bi, bh=bh: e.activation(out=o2_ap, in_=bank_ap(bi, bh), func=AF.Copy),
                         reads=[r_bank[bi][bh]], writes=o2_res)
                    S.op("act", lambda e, d=d, bi=bi, bh=bh: e.activation(out=sqd[:, d * 512:(d + 1) * 512], in_=bank_ap(bi, bh),
                                                                          func=AF.Square),
                         reads=[r_bank[bi][bh]], writes=[r_T[1][d // 2]])
                emit_ssq(7, sbi, sbh, sqd)
                if tt == TT - 1 and s + 1 < NSEQ:
                    emit_x_loads(s + 1, "pool")
                rs = T[1][:, 2560:3072]
                rstd_from(bank_ap(sbi, sbh), r_bank[sbi][sbh], rs, r_T[1][5])
                for d in range(8):
                    o2_ap, o2_res = o2_buf(tt, d)
                    S.op("dve", lambda e, d=d, o2_ap=o2_ap: e.scalar_tensor_tensor(
                        out=o2_ap, in0=o2_ap, scalar=vcol(V_FPOST + d), in1=rs,
                        op0=ALU.mult, op1=ALU.mult), reads=o2_res + [r_T[1][5], *r_consts], writes=o2_res)
                    S.op("dve", lambda e, d=d, x1t=x1t, o2_ap=o2_ap: e.tensor_tensor(
                        out=x1t[:, d * 512:(d + 1) * 512], in0=o2_ap, in1=x1t[:, d * 512:(d + 1) * 512],
                        op=ALU.add), reads=o2_res + x1res, writes=x1res)
                out_evs.append(S.dma("sp", lambda e, x1t=x1t, s=s, tt=tt: e.dma_start(out=yT_d[s, tt], in_=x1t),
                                     f"st_y{hb}", reads=x1res + r_A))

        S.wait_all("sp", out_evs)
        S.build(nc, es)
    return nc


def _host_layout(inp):
    f = lambda a: np.ascontiguousarray(np.asarray(a, dtype=np.float32))
    w_in = f(inp["w_in"])[0]
    w_ap = f(inp["w_attn_proj"])[0]
    w_cp = f(inp["w_conv_proj"])[0]
    w_out = f(inp["w_out"])[0]
    w_up = f(inp["w_up"])[0]
    w_down = f(inp["w_down"])[0]
    sh = {}
    sh["w_qkv"] = f(w_in[:, 0:1536].reshape(8, 128, 3, 512).transpose(2, 1, 0, 3).reshape(3, 128, 4096))
    sh["w_ubc"] = f(w_in[:, 1536:4608].reshape(8, 128, 3, 8, 128).transpose(3, 1, 0, 2, 4).reshape(8, 128, 3072))
    W4 = np.stack([w_ap, w_cp, w_in[:, 4608:5632], w_in[:, 5632:6656]], axis=0)
    sh["w_pj"] = f(W4.reshape(4, 8, 128, 8, 128).transpose(3, 2, 1, 0, 4).reshape(8, 128, 4096))
    sh["w_o"] = f(w_out.reshape(8, 128, 2, 512).transpose(2, 1, 0, 3).reshape(2, 128, 4096))
    sh["w_upg"] = f(w_up.reshape(8, 128, 2, 11, 2, 128).transpose(3, 1, 0, 2, 4, 5).reshape(11, 128, 4096))
    sh["w_dn"] = f(w_down.reshape(22, 128, 8, 128).transpose(2, 1, 0, 3).reshape(8, 128, 2816))
    vec = np.zeros((128, NV), np.float32)
    col8 = lambda v: f(v).reshape(8, 128).T
    vec[:, V_PRE:V_PRE + 8] = col8(inp["mix_pre_g"][0])
    vec[:, V_POST:V_POST + 8] = col8(inp["mix_post_g"][0])
    vec[:, V_FPRE:V_FPRE + 8] = col8(inp["ffn_pre_g"][0])
    vec[:, V_FPOST:V_FPOST + 8] = col8(inp["ffn_post_g"][0])
    gb = f(inp["gate_b"])[0]
    vec[:, V_GBA:V_GBA + 8] = col8(gb[:D])
    vec[:, V_GBB:V_GBB + 8] = col8(gb[D:])
    perm = np.arange(128)
    perm = np.where((perm % 64) < 32, perm + 32, perm - 32)
    qg = f(inp["q_norm_g"])[0]
    kg = f(inp["k_norm_g"])[0]
    vec[:, V_QG] = qg
    vec[:, V_QGP] = qg[perm]
    vec[:, V_KG] = kg
    vec[:, V_KGP] = kg[perm]
    mcw = f(inp["mix_conv_w"])[0]
    for k in range(3):
        vec[:, V_MCW + 8 * k:V_MCW + 8 * (k + 1)] = mcw[k].reshape(8, 128).T
    fcw = f(inp["ffn_conv_w"])[0]
    for k in range(3):
        vec[:, V_FCW + NJ * k:V_FCW + NJ * (k + 1)] = fcw[k].reshape(NJ, 128).T
    sh["vecs"] = vec
    t = np.arange(SEQ)
    row_pos = (t // 64).astype(np.float64)
    col_pos = (t % 64).astype(np.float64)
    freqs = 10000.0 ** (-(np.arange(32, dtype=np.float64) / 32.0))
    tabs = np.zeros((128, 2 * SEQ), np.float32)
    for d in range(128):
        pos = row_pos if d < 64 else col_pos
        ang = pos * freqs[d % 32]
        tabs[d, 0:SEQ] = np.cos(ang)
        tabs[d, SEQ:] = np.sin(ang)
    sh["tabs"] = tabs
    cm = np.zeros((128, 512), np.float32)
    cm[:, 0:128] = 1.0 / 1024.0
    cm[:, 128:256] = 1.0 / 128.0
    cm[:, 256:384] = 1.0
    for m in range(128):
        if (m % 64) < 32:
            cm[m + 32, 384 + m] = -1.0
        else:
            cm[m - 32, 384 + m] = 1.0
    sh["cmat"] = cm
    return sh


_NC_CACHE = {}


def kernel(**inputs):
    x = np.asarray(inputs["x"], dtype=np.float32)
    shared = _host_layout(inputs)
    in_maps = []
    for core in range(NCORES):
        xs = x[core * NSEQ:(core + 1) * NSEQ]
        xT = np.ascontiguousarray(xs.reshape(NSEQ, TT, 512, 8, 128).transpose(0, 1, 4, 3, 2)).reshape(NSEQ, TT, 128, 4096)
        m = dict(shared)
        m["xT"] = xT
        in_maps.append(m)
    if "nc" not in _NC_CACHE:
        _NC_CACHE["nc"] = build_nc()
    nc = _NC_CACHE["nc"]
    res = run_bass_kernel_spmd(nc, in_maps, core_ids=list(range(NCORES)))
    outs = []
    for core in range(NCORES):
        yT = np.asarray(res.results[core]["yT"]).reshape(NSEQ, TT, 128, 8, 512)
        outs.append(yT.transpose(0, 1, 4, 3, 2).reshape(NSEQ, SEQ, D))
    return np.ascontiguousarray(np.concatenate(outs, axis=0).astype(np.float32))
```

```python
import numpy as np
from contextlib import ExitStack
import concourse.bass as bass
import concourse.mybir as mybir
from concourse.bass_utils import run_bass_kernel_spmd

F32 = mybir.dt.float32
BF16 = mybir.dt.bfloat16
AF = mybir.ActivationFunctionType
ALU = mybir.AluOpType

NCORES = 8
SEQ = 2048
D = 1024
NSEQ = 2
TT = 4
HD = 128
NH = 8
DFF = 2816
NJ = 22
EPS = 1e-6
SCALE = HD ** -0.5
NSLOT = 3

V_PRE, V_POST, V_FPRE, V_FPOST, V_GBA, V_GBB = 0, 8, 16, 24, 32, 40
V_QG, V_QGP, V_KG, V_KGP = 48, 49, 50, 51
V_MCW = 52
V_FCW = 76
NV = 142


class Res:
    __slots__ = ("name", "w", "r", "x")

    def __init__(self, name, excl=False):
        self.name = name
        self.w = None
        self.r = []
        self.x = excl


class Sched:
    ENGS = ("pe", "act", "dve", "pool", "sp")

    def __init__(self):
        self.q = {e: [] for e in self.ENGS}
        self.cnt = {e: 0 for e in self.ENGS}
        self.seen = {e: {} for e in self.ENGS}
        self.dcnt = {}

    def _waits(self, eng, reads, writes):
        evs = []
        for r in reads:
            if r.w is not None:
                evs.append(r.w)
        same_ok = (eng == "pe")
        for w in writes:
            if w.w is not None and (w.w[0] != eng or not same_ok):
                evs.append(w.w)
            for ev in w.r:
                if ev[0] != eng or not same_ok:
                    evs.append(ev)
        need = {}
        for k, v in evs:
            if k in self.cnt:
                assert v <= self.cnt[k], (eng, k, v, self.cnt[k])
            if v > self.seen[eng].get(k, 0):
                need[k] = max(need.get(k, 0), v)
        for k, v in need.items():
            self.seen[eng][k] = v
        return list(need.items())

    def op(self, eng, fn, reads=(), writes=(), inc=True):
        xr = [r for r in reads if r.x]
        if xr:
            writes = list(writes) + [r for r in xr if r not in writes]
            reads = [r for r in reads if not r.x]
        waits = self._waits(eng, reads, writes)
        if inc:
            self.cnt[eng] += 1
            ev = (eng, self.cnt[eng])
        else:
            ev = (eng, self.cnt[eng] + 1)
        for r in reads:
            r.r.append(ev)
        for w in writes:
            w.w = ev
            w.r = []
        self.q[eng].append((waits, fn, (eng, 1) if inc else None))
        return ev

    def dma(self, eng, fn, semkey, reads=(), writes=()):
        waits = self._waits(eng, reads, writes)
        self.dcnt[semkey] = self.dcnt.get(semkey, 0) + 16
        ev = (semkey, self.dcnt[semkey])
        for r in reads:
            r.r.append(ev)
        for w in writes:
            w.w = ev
            w.r = []
        self.q[eng].append((waits, fn, (semkey, 16)))
        return ev

    def wait_all(self, eng, evs):
        need = {}
        for k, v in evs:
            if v > self.seen[eng].get(k, 0):
                need[k] = max(need.get(k, 0), v)
        for k, v in need.items():
            self.seen[eng][k] = v
        self.q[eng].append((list(need.items()), None, None))

    def build(self, nc, es):
        sems = {}
        for k in list(self.ENGS) + sorted(self.dcnt.keys()):
            sems[k] = es.enter_context(nc.semaphore("s_" + k))
        block = es.enter_context(nc.Block())
        q = self.q

        def run(eng_name):
            def body(e):
                for waits, fn, inc in q[eng_name]:
                    for k, v in waits:
                        e.wait_ge(sems[k], v)
                    if fn is not None:
                        ins = fn(e)
                        if inc is not None:
                            ins.then_inc(sems[inc[0]], inc[1])
            return body

        block.tensor(run("pe"))
        block.scalar(run("act"))
        block.vector(run("dve"))
        block.gpsimd(run("pool"))
        block.sync(run("sp"))


def build_nc():
    nc = bass.Bass("TRN2", target_bir_lowering=False)
    xT_d = nc.dram_tensor("xT", [NSEQ, TT, 128, 4096], F32, kind="ExternalInput")
    wqkv_d = nc.dram_tensor("w_qkv", [3, 128, 4096], F32, kind="ExternalInput")
    wubc_d = nc.dram_tensor("w_ubc", [8, 128, 3072], F32, kind="ExternalInput")
    wpj_d = nc.dram_tensor("w_pj", [8, 128, 4096], F32, kind="ExternalInput")
    wo_d = nc.dram_tensor("w_o", [2, 128, 4096], F32, kind="ExternalInput")
    wup_d = nc.dram_tensor("w_upg", [11, 128, 4096], F32, kind="ExternalInput")
    wdn_d = nc.dram_tensor("w_dn", [8, 128, 2816], F32, kind="ExternalInput")
    vec_d = nc.dram_tensor("vecs", [128, NV], F32, kind="ExternalInput")
    tab_d = nc.dram_tensor("tabs", [128, 2 * SEQ], F32, kind="ExternalInput")
    cm_d = nc.dram_tensor("cmat", [128, 512], F32, kind="ExternalInput")
    yT_d = nc.dram_tensor("yT", [NSEQ, TT, 128, 4096], F32, kind="ExternalOutput")
    x1_d = nc.dram_tensor("x1s", [NSEQ, TT, 128, 4096], F32)
    wdnb_d = nc.dram_tensor("wdn_bf16", [8, 128, 2816], BF16)

    es = ExitStack()
    S = Sched()
    with es:
        sb = lambda name, shape, dt: es.enter_context(nc.sbuf_tensor(name, shape, dt))
        A = sb("A", [128, 8 * SEQ], BF16)
        O = sb("O", [128, 8 * SEQ], BF16)
        R1 = sb("R1", [128, 8 * SEQ], BF16)
        M = sb("M", [128, 8 * SEQ], BF16)
        tabs = sb("tabs_sb", [128, 2 * SEQ], F32)
        ring = [sb(f"ring{i}", [128, 4096], BF16) for i in range(NSLOT)]
        T = [sb(f"T{i}", [128, 4608], F32) for i in range(2)]
        vecs = sb("vecs_sb", [128, NV], F32)
        cmat = sb("cmat_sb", [128, 512], BF16)
        epsb = sb("epsb", [128, 1], F32)
        PS = [es.enter_context(nc.psum_tensor(f"ps{i}", [128, 1024], F32)) for i in range(4)]

        A_f = A.bitcast(F32)
        O_f = O.bitcast(F32)
        R1_f = R1.bitcast(F32)
        M_f = M.bitcast(F32)
        T_b = [t.bitcast(BF16) for t in T]
        tabs_b = tabs.bitcast(BF16)

        r_A = [Res(f"A{t}") for t in range(TT)]
        r_O = [[Res(f"O{h}_{t}") for t in range(TT)] for h in range(8)]
        r_R1 = [Res(f"R1_{c}") for c in range(8)]
        r_Mb = [Res(f"Mb{b}") for b in range(16)]
        r_ring = [Res(f"ring{i}") for i in range(NSLOT)]
        r_T = [[Res(f"T{i}_{b}") for b in range(9)] for i in range(2)]
        r_bank = [[Res(f"ps{i}_{h}", excl=True) for h in range(2)] for i in range(4)]
        r_consts = [Res("eps"), Res("vec"), Res("cm")]
        r_tabs = Res("tabs")
        r_wo = [Res("wo0"), Res("wo1")]
        r_x1d = [[Res(f"x1d{s}_{t}") for t in range(TT)] for s in range(NSEQ)]
        r_wdnb = [Res(f"wdnb{d}") for d in range(8)]

        ones1024 = cmat[:, 0:128]
        ones128 = cmat[:, 128:256]
        ones1 = cmat[:, 256:384]
        pmatT = cmat[:, 384:512]

        def vcol(i):
            return vecs[:, i:i + 1]

        bank_ap = lambda i, h: PS[i][:, h * 512:(h + 1) * 512]

        ring_state = {"n": 0}

        ring_pinned = set()

        def load_slot(src_ap, L, reads=(), nb=None):
            while True:
                s = ring_state["n"] % NSLOT
                ring_state["n"] += 1
                if s not in ring_pinned:
                    break
            if nb is None:
                nb = L // 1024 if L % 1024 == 0 else L // 704
            bl = L // nb
            S.dma("pool", lambda e, s=s: e.dma_start(
                out=ring[s][:, 0:L].rearrange("p (a b) -> p a b", b=bl),
                in_=src_ap.rearrange("p (a b) -> p a b", b=bl)), f"ring{s}", reads=list(reads), writes=[r_ring[s]])
            return s

        bank_state = {"n": 0}

        bank_reserved = set()

        def next_bank():
            while True:
                n = bank_state["n"] % 8
                bank_state["n"] += 1
                if (n // 2, n % 2) not in bank_reserved:
                    return n // 2, n % 2

        def mm_group(out_ap, out_res, pairs, reads, inc_last=True):
            n = len(pairs)
            for idx, (l, r) in enumerate(pairs):
                S.op("pe", lambda e, l=l, r=r, idx=idx: e.matmul(out_ap, lhsT=l, rhs=r, start=(idx == 0), stop=(idx == n - 1)),
                     reads=reads, writes=[out_res], inc=(inc_last and idx == n - 1))

        def rstd_from(ps_ap, ps_res, out_ap, out_res):
            S.op("act", lambda e: e.activation(out=out_ap, in_=ps_ap, func=AF.Ln, bias=epsb[:, 0:1]),
                 reads=[ps_res, *r_consts], writes=[out_res])
            S.op("act", lambda e: e.activation(out=out_ap, in_=out_ap, func=AF.Exp, scale=-0.5),
                 reads=[out_res], writes=[out_res])

        S.op("dve", lambda e: e.memset(epsb[:], EPS), writes=[r_consts[0]])
        S.dma("sp", lambda e: e.dma_start(out=vecs[:], in_=vec_d[:, :]), "ld_c0", writes=[r_consts[1]])
        S.dma("sp", lambda e: e.dma_start(out=tabs[:], in_=tab_d[:, :]), "ld_c1", writes=[r_tabs])
        S.dma("pool", lambda e: e.dma_start(out=cmat[:], in_=cm_d[:, :]), "ld_c2", writes=[r_consts[2]])
        Ctab = tabs[:, 0:SEQ]
        Stab = tabs[:, SEQ:2 * SEQ]

        out_evs = []
        qk_ctr = {"n": 0}

        qk_pend = {"p": None}

        def qk_pre(pairs, reads):
            k = qk_ctr["n"] % 2
            qk_ctr["n"] += 1
            base = k * 4
            bi, bh = next_bank()
            bank_reserved.add((bi, bh))
            ps_ap, ps_res = bank_ap(bi, bh), r_bank[bi][bh]
            mm_group(ps_ap, ps_res, pairs, reads)
            qb = M[:, base * 1024: base * 1024 + 512]
            sq = M[:, base * 1024 + 512: base * 1024 + 1024]
            S.op("act", lambda e: e.activation(out=qb, in_=ps_ap, func=AF.Copy), reads=[ps_res], writes=[r_Mb[base]])
            S.op("act", lambda e: e.activation(out=sq, in_=ps_ap, func=AF.Square), reads=[ps_res], writes=[r_Mb[base]])
            return base, bi, bh

        def qk_fin(st, gcol, gpcol, dest_ap, dest_res, tt):
            base, pi_, ph_ = st
            ps_ap, ps_res = bank_ap(pi_, ph_), r_bank[pi_][ph_]
            qb = M[:, base * 1024: base * 1024 + 512]
            sq = M[:, base * 1024 + 512: base * 1024 + 1024]
            rs = M_f[:, (base + 1) * 512:(base + 2) * 512]
            t1 = M_f[:, (base + 2) * 512:(base + 3) * 512]
            t2 = M_f[:, (base + 3) * 512:(base + 4) * 512]
            r_qb = r_sq = r_Mb[base]
            r_rs, r_t1, r_t2 = r_Mb[base + 1], r_Mb[base + 2], r_Mb[base + 3]
            bi, bh = next_bank()
            mm_group(bank_ap(bi, bh), r_bank[bi][bh], [(ones128, sq)], [r_sq, *r_consts])
            ci, ch = next_bank()
            mm_group(bank_ap(ci, ch), r_bank[ci][ch], [(pmatT, qb)], [r_qb, *r_consts])
            rstd_from(bank_ap(bi, bh), r_bank[bi][bh], rs, r_rs)
            sl = slice(tt * 512, (tt + 1) * 512)
            S.op("dve", lambda e: e.scalar_tensor_tensor(out=t1, in0=ps_ap, scalar=vcol(gcol), in1=Ctab[:, sl],
                                                         op0=ALU.mult, op1=ALU.mult),
                 reads=[ps_res, *r_consts, r_tabs], writes=[r_t1])
            S.op("dve", lambda e: e.scalar_tensor_tensor(out=t2, in0=bank_ap(ci, ch), scalar=vcol(gpcol), in1=Stab[:, sl],
                                                         op0=ALU.mult, op1=ALU.mult),
                 reads=[r_bank[ci][ch], *r_consts, r_tabs], writes=[r_t2])
            S.op("dve", lambda e: e.tensor_tensor(out=t1, in0=t1, in1=t2, op=ALU.add), reads=[r_t1, r_t2], writes=[r_t1])
            S.op("dve", lambda e: e.tensor_tensor(out=dest_ap, in0=t1, in1=rs, op=ALU.mult),
                 reads=[r_t1, r_rs], writes=[dest_res])
            bank_reserved.discard((pi_, ph_))

        def qk_item(pairs, reads, gcol, gpcol, dest_ap, dest_res, tt):
            st = qk_pre(pairs, reads)
            if qk_pend["p"] is not None:
                qk_fin(*qk_pend["p"])
            qk_pend["p"] = (st, gcol, gpcol, dest_ap, dest_res, tt)

        def qk_flush():
            if qk_pend["p"] is not None:
                qk_fin(*qk_pend["p"])
                qk_pend["p"] = None

        def norm_tile(xt_f, xt_res, g0, dest_ap_fn, dest_res, sq_b, sq_res, rs_ap, rs_res):
            S.op("act", lambda e: e.activation(out=sq_b, in_=xt_f, func=AF.Square), reads=xt_res, writes=sq_res)
            bi, bh = next_bank()
            mm_group(bank_ap(bi, bh), r_bank[bi][bh],
                     [(ones1024, sq_b[:, c * 512:(c + 1) * 512]) for c in range(8)], list(sq_res) + [*r_consts])
            rstd_from(bank_ap(bi, bh), r_bank[bi][bh], rs_ap, rs_res)
            for c in range(8):
                S.op("dve", lambda e, c=c: e.scalar_tensor_tensor(
                    out=dest_ap_fn(c), in0=xt_f[:, c * 512:(c + 1) * 512], scalar=vcol(g0 + c), in1=rs_ap,
                    op0=ALU.mult, op1=ALU.mult),
                    reads=list(xt_res) + [rs_res, *r_consts], writes=[dest_res])

        def emit_ssq(d, sbi, sbh, sqd):
            S.op("pe", lambda e: e.matmul(bank_ap(sbi, sbh), lhsT=ones1024, rhs=sqd[:, d * 512:(d + 1) * 512],
                                          start=(d == 0), stop=(d == 7)),
                 reads=[r_T[1][d // 2], *r_consts], writes=[r_bank[sbi][sbh]], inc=True)

        x_bufs = [(M_f[:, 0:4096], r_Mb[0:8]),
                  (O_f[:, 4096:8192], [r_O[h][t] for h in range(4, 8) for t in range(TT)]),
                  (R1_f[:, 4096:8192], r_R1[4:8]),
                  (O_f[:, 0:4096], [r_O[h][t] for h in range(4) for t in range(TT)])]

        def emit_x_loads(sq_, eng):
            for tt in range(TT):
                xt, xres = x_bufs[tt]
                S.dma(eng, lambda e, xt=xt, sq_=sq_, tt=tt: e.dma_start(out=xt, in_=xT_d[sq_, tt]), f"ld_x{tt}_{eng}",
                      reads=(list(x_bufs[tt - 1][1]) if (tt > 0 and eng == "pool") else []), writes=xres)

        for s in range(NSEQ):
            if s == 0:
                emit_x_loads(0, "sp")
            for tt in range(TT):
                xt, xres = x_bufs[tt]
                sqb = T_b[1][:, 0:4096]
                norm_tile(xt, xres, V_PRE,
                          lambda c, tt=tt: A[:, c * SEQ + tt * 512: c * SEQ + (tt + 1) * 512], r_A[tt],
                          sqb, r_T[1][0:4], T[1][:, 2560:3072], r_T[1][5])

            kT = lambda kvh: R1[:, kvh * SEQ:(kvh + 1) * SEQ]
            Vt = R1[:, 2 * SEQ:4 * SEQ]
            sl_kv = load_slot(wqkv_d[2], 4096)
            rg = ring[sl_kv]
            for tt in range(TT):
                for kvh in range(2):
                    qk_item([(rg[:, c * 512 + kvh * 128: c * 512 + (kvh + 1) * 128],
                              A[:, c * SEQ + tt * 512: c * SEQ + (tt + 1) * 512]) for c in range(8)],
                            [r_ring[sl_kv], r_A[tt]], V_KG, V_KGP,
                            kT(kvh)[:, tt * 512:(tt + 1) * 512], r_R1[kvh], tt)
            for kt2 in range(8):
                bi, bh = next_bank()
                for q in range(2):
                    kt = kt2 * 2 + q
                    mm_group(PS[bi][:, bh * 512 + q * 256: bh * 512 + (q + 1) * 256], r_bank[bi][bh],
                             [(A[:, c * SEQ + kt * 128: c * SEQ + (kt + 1) * 128],
                               rg[:, c * 512 + 256: c * 512 + 512]) for c in range(8)],
                             [r_ring[sl_kv], r_A[kt // 4]])
                S.op("act", lambda e, bi=bi, bh=bh, kt2=kt2: e.activation(
                    out=Vt[:, kt2 * 512:(kt2 + 1) * 512], in_=bank_ap(bi, bh), func=AF.Copy),
                    reads=[r_bank[bi][bh]], writes=[r_R1[2 + kt2 // 4]])

            def p_ap(j):
                return R1[:, (4 + j) * SEQ:(5 + j) * SEQ] if j < 4 else M[:, j * SEQ:(j + 1) * SEQ]

            def p_res(j):
                return [r_R1[4 + j]] if j < 4 else [r_Mb[2 * j], r_Mb[2 * j + 1]]

            S.op("dve", lambda e: e.memset(T[0][:, 0:1], 0.0), writes=[r_T[0][0]])
            S.op("dve", lambda e: e.memset(T[0][:, 2049:2050], 0.0), writes=[r_T[0][4]])
            c_wrow = T[0]
            c_yrow = T[0][:, 2560:4608]
            c_brow = T[1][:, 2560:4608]
            sl_q = None
            for j in range(8):
                if j % 4 == 0:
                    if sl_q is not None:
                        ring_pinned.discard(sl_q)
                    sl_q = load_slot(wqkv_d[j // 4], 4096)
                    ring_pinned.add(sl_q)
                rgq = ring[sl_q]
                hq = j % 4
                sl = load_slot(wubc_d[j], 3072)
                rg = ring[sl]
                for tt in range(TT):
                    qk_item([(rgq[:, c * 512 + hq * 128: c * 512 + (hq + 1) * 128],
                              A[:, c * SEQ + tt * 512: c * SEQ + (tt + 1) * 512]) for c in range(8)],
                            [r_ring[sl_q], r_A[tt]], V_QG, V_QGP,
                            O[:, j * SEQ + tt * 512: j * SEQ + (tt + 1) * 512], r_O[j][tt], tt)
                    bks = []
                    for k in (2, 0, 1):
                        bi, bh = next_bank()
                        mm_group(bank_ap(bi, bh), r_bank[bi][bh],
                                 [(rg[:, c * 384 + k * 128: c * 384 + (k + 1) * 128],
                                   A[:, c * SEQ + tt * 512: c * SEQ + (tt + 1) * 512]) for c in range(8)],
                                 [r_ring[sl], r_A[tt]])
                        bks.append((bi, bh))
                    (cbi, cbh), (ubi, ubh), (bbi, bbh) = bks
                    cs = T[1][:, (tt % 2) * 512:(tt % 2 + 1) * 512]
                    r_cs = r_T[1][tt % 2]
                    S.op("act", lambda e, cs=cs, cbi=cbi, cbh=cbh: e.activation(out=cs, in_=bank_ap(cbi, cbh), func=AF.Copy),
                         reads=[r_bank[cbi][cbh]], writes=[r_cs])
                    S.op("dve", lambda e, cs=cs, ubi=ubi, ubh=ubh, tt=tt: e.tensor_tensor(
                        out=c_wrow[:, 1 + tt * 512: 1 + (tt + 1) * 512], in0=bank_ap(ubi, ubh), in1=cs, op=ALU.mult),
                        reads=[r_bank[ubi][ubh], r_cs], writes=[r_T[0][tt], r_T[0][tt + 1]])
                    S.op("act", lambda e, bbi=bbi, bbh=bbh, tt=tt: e.activation(
                        out=c_brow[:, tt * 512:(tt + 1) * 512], in_=bank_ap(bbi, bbh), func=AF.Copy),
                        reads=[r_bank[bbi][bbh]], writes=[r_T[1][5 + tt]])
                rw = r_T[0][0:5]
                ry = r_T[0][5:9]
                S.op("dve", lambda e, j=j: e.tensor_scalar(out=c_yrow, in0=c_wrow[:, 0:2048], scalar1=vcol(V_MCW + j), scalar2=None,
                                                           op0=ALU.mult), reads=rw + [*r_consts], writes=ry)
                S.op("dve", lambda e, j=j: e.scalar_tensor_tensor(out=c_yrow, in0=c_wrow[:, 1:2049], scalar=vcol(V_MCW + 8 + j),
                                                                  in1=c_yrow, op0=ALU.mult, op1=ALU.add),
                     reads=rw + ry + [*r_consts], writes=ry)
                S.op("dve", lambda e, j=j: e.scalar_tensor_tensor(out=c_yrow, in0=c_wrow[:, 2:2050], scalar=vcol(V_MCW + 16 + j),
                                                                  in1=c_yrow, op0=ALU.mult, op1=ALU.add),
                     reads=rw + ry + [*r_consts], writes=ry)
                S.op("dve", lambda e, j=j: e.tensor_tensor(out=p_ap(j), in0=c_yrow, in1=c_brow, op=ALU.mult),
                     reads=ry + r_T[1][5:9], writes=p_res(j))
            qk_flush()
            ring_pinned.discard(sl_q)
            for blk in range(2):
                S.dma("pool", lambda e, blk=blk: e.dma_start(
                    out=tabs_b[:, blk * 4096:(blk + 1) * 4096].rearrange("p (a b) -> p a b", b=1024),
                    in_=wo_d[blk].rearrange("p (a b) -> p a b", b=1024)), f"ld_wo{blk}",
                    writes=[r_wo[blk], r_tabs])

            if s == 0:
                for d in range(8):
                    S.dma("pool", lambda e, d=d: e.dma_start(
                        out=wdnb_d[d].rearrange("p (a b) -> p a b", b=704),
                        in_=wdn_d[d].rearrange("p (a b) -> p a b", b=704)), f"cv_wdn{d}", reads=[r_O[7][3]], writes=[r_wdnb[d]])

            pairs = [(hh, m2) for hh in range(NH) for m2 in range(2)]

            def actx(p):
                hh, m2 = pairs[p]
                qts = (2 * m2, 2 * m2 + 1)
                return dict(kvh=hh // 4,
                            qaps=[O[:, hh * SEQ + qt * 512: hh * SEQ + (qt + 1) * 512] for qt in qts],
                            r_qs=[r_O[hh][qt] for qt in qts])

            def emit_qk(cx, st, kp):
                kvh, qap = cx["kvh"], cx["qaps"][st]
                for q in range(2):
                    kt = kp * 2 + q
                    S.op("pe", lambda e, st=st, q=q, kt=kt, kvh=kvh, qap=qap: e.matmul(
                        PS[st][:, q * 512:(q + 1) * 512], lhsT=kT(kvh)[:, kt * 128:(kt + 1) * 128], rhs=qap,
                        start=True, stop=True),
                        reads=[r_R1[kvh], cx["r_qs"][st]], writes=[r_bank[st][0], r_bank[st][1]], inc=(q == 1))

            def pt_buf(st, kp):
                b = st * 2 + (kp % 2)
                return T_b[0][:, b * 1024:(b + 1) * 1024], r_T[0][b]

            def pa_buf(st, kp):
                b = 4 + st * 2 + (kp % 2)
                return T_b[0][:, b * 1024:b * 1024 + 512], r_T[0][b]

            def emit_exp(st, kp):
                PT, r_PT = pt_buf(st, kp)
                S.op("act", lambda e, PT=PT, st=st: e.activation(out=PT, in_=PS[st][:, :], func=AF.Exp, scale=SCALE),
                     reads=[r_bank[st][0], r_bank[st][1]], writes=[r_PT])
                pa, r_pa = pa_buf(st, kp)
                S.op("dve", lambda e, PT=PT, pa=pa: e.tensor_tensor(out=pa, in0=PT[:, 0:512], in1=PT[:, 512:1024],
                                                                   op=ALU.add), reads=[r_PT], writes=[r_pa])

            def emit_pv(cx, st, kp):
                PT, r_PT = pt_buf(st, kp)
                oacc = bank_ap(2, st)
                kvh = cx["kvh"]
                for q in range(2):
                    kt = kp * 2 + q
                    S.op("pe", lambda e, q=q, kt=kt, oacc=oacc, kvh=kvh, PT=PT: e.matmul(
                        oacc, lhsT=Vt[:, kt * 256 + kvh * 128: kt * 256 + (kvh + 1) * 128],
                        rhs=PT[:, q * 512:(q + 1) * 512], start=(kt == 0), stop=(kt == 15)),
                        reads=[r_PT, r_R1[2 + kt // 8]], writes=[r_bank[2][st]], inc=(q == 1))

            def emit_den(st, kp):
                pa, r_pa = pa_buf(st, kp)
                den = bank_ap(3, st)
                S.op("pe", lambda e, pa=pa, den=den, kp=kp: e.matmul(den, lhsT=ones1, rhs=pa, start=(kp == 0), stop=(kp == 7)),
                     reads=[r_pa, *r_consts], writes=[r_bank[3][st]], inc=True)

            def emit_epilogue(cx, st):
                rd = T[1][:, st * 512:(st + 1) * 512]
                r_rd = r_T[1][st]
                S.op("act", lambda e, rd=rd, st=st: e.activation(out=rd, in_=bank_ap(3, st), func=AF.Ln),
                     reads=[r_bank[3][st]], writes=[r_rd])
                S.op("act", lambda e, rd=rd: e.activation(out=rd, in_=rd, func=AF.Exp, scale=-1.0),
                     reads=[r_rd], writes=[r_rd])
                S.op("dve", lambda e, rd=rd, st=st, qap=cx["qaps"][st]: e.tensor_tensor(
                    out=qap, in0=bank_ap(2, st), in1=rd, op=ALU.mult),
                    reads=[r_bank[2][st], r_rd], writes=[cx["r_qs"][st]])

            cx = actx(0)
            for st in range(2):
                emit_qk(cx, st, 0)
            for p in range(len(pairs)):
                cx = actx(p)
                cx_next = actx(p + 1) if p + 1 < len(pairs) else None
                for kp in range(8):
                    if kp == 0 and p >= 1:
                        cxp = actx(p - 1)
                        emit_exp(0, 0)
                        emit_epilogue(cxp, 0)
                        emit_exp(1, 0)
                        emit_epilogue(cxp, 1)
                    else:
                        for st in range(2):
                            emit_exp(st, kp)
                    for st in range(2):
                        if kp + 1 < 8:
                            emit_qk(cx, st, kp + 1)
                        elif cx_next is not None:
                            emit_qk(cx_next, st, 0)
                        emit_pv(cx, st, kp)
                        if kp >= 1:
                            emit_den(st, kp - 1)
                        if kp == 7:
                            emit_den(st, 7)
            cxl = actx(len(pairs) - 1)
            emit_epilogue(cxl, 0)
            emit_epilogue(cxl, 1)

            def mg_ap(j, tt):
                if j < 4:
                    return R1[:, j * SEQ + tt * 512: j * SEQ + (tt + 1) * 512]
                return M[:, (j - 4) * SEQ + tt * 512:(j - 4) * SEQ + (tt + 1) * 512]

            def mg_res(j, tt):
                return [r_R1[j]] if j < 4 else [r_Mb[((j - 4) * 4 + tt) // 2]]

            def p_tile_res(c, tt):
                return [r_R1[4 + c]] if c < 4 else [r_Mb[(c * 4 + tt) // 2]]

            for j in range(8):
                sl = load_slot(wpj_d[j], 4096)
                rg = ring[sl]
                for tt in range(TT):
                    tsl = lambda c, tt=tt: slice(c * SEQ + tt * 512, c * SEQ + (tt + 1) * 512)
                    src_fns = (
                        (lambda c, tt=tt: O[:, tsl(c)], lambda c, tt=tt: [r_O[c][tt]]),
                        (lambda c, tt=tt: p_ap(c)[:, tt * 512:(tt + 1) * 512], lambda c, tt=tt: p_tile_res(c, tt)),
                        (lambda c, tt=tt: A[:, tsl(c)], lambda c, tt=tt: [r_A[tt]]),
                        (lambda c, tt=tt: A[:, tsl(c)], lambda c, tt=tt: [r_A[tt]]),
                    )
                    bk = []
                    for k, (ap_fn, rs_fn) in enumerate(src_fns):
                        bi, bh = next_bank()
                        rds = [r_ring[sl]]
                        for c in range(8):
                            for r_ in rs_fn(c):
                                if r_ not in rds:
                                    rds.append(r_)
                        mm_group(bank_ap(bi, bh), r_bank[bi][bh],
                                 [(rg[:, c * 512 + k * 128: c * 512 + (k + 1) * 128], ap_fn(c)) for c in range(8)], rds)
                        bk.append((bi, bh))
                    sa = T[tt % 2][:, 0:512]
                    sbg = T[tt % 2][:, 512:1024]
                    m1 = T[tt % 2][:, 1024:1536]
                    r_sa, r_sb, r_m1 = r_T[tt % 2][0], r_T[tt % 2][1], r_T[tt % 2][2]
                    S.op("act", lambda e, sa=sa, b=bk[2], j=j: e.activation(out=sa, in_=bank_ap(*b), func=AF.Sigmoid,
                                                                             bias=vcol(V_GBA + j)),
                         reads=[r_bank[bk[2][0]][bk[2][1]], *r_consts], writes=[r_sa])
                    S.op("act", lambda e, sbg=sbg, b=bk[3], j=j: e.activation(out=sbg, in_=bank_ap(*b), func=AF.Sigmoid,
                                                                               bias=vcol(V_GBB + j)),
                         reads=[r_bank[bk[3][0]][bk[3][1]], *r_consts], writes=[r_sb])
                    S.op("dve", lambda e, sa=sa, m1=m1, b=bk[0]: e.tensor_tensor(out=m1, in0=bank_ap(*b), in1=sa, op=ALU.mult),
                         reads=[r_bank[bk[0][0]][bk[0][1]], r_sa], writes=[r_m1])
                    S.op("dve", lambda e, sbg=sbg, b=bk[1]: e.tensor_tensor(out=sbg, in0=bank_ap(*b), in1=sbg, op=ALU.mult),
                         reads=[r_bank[bk[1][0]][bk[1][1]], r_sb], writes=[r_sb])
                    S.op("dve", lambda e, sbg=sbg, m1=m1, j=j, tt=tt: e.tensor_tensor(
                        out=mg_ap(j, tt), in0=m1, in1=sbg, op=ALU.add),
                        reads=[r_m1, r_sb], writes=mg_res(j, tt))

            wo_res = lambda d: [r_wo[d // 4], r_tabs]
            out_state = {}

            def mo_tile(tt):
                if tt % 2 == 0:
                    return T[0][:, 0:4096], r_T[0][0:8]
                return O_f[:, 4096:8192], [r_O[h][t] for h in range(4, 8) for t in range(TT)]

            def mo_buf(tt, d):
                if tt % 2 == 0:
                    return T[0][:, d * 512:(d + 1) * 512], [r_T[0][d]]
                return (O_f[:, 4096 + d * 512: 4096 + (d + 1) * 512],
                        [r_O[4 + d // 2][(d % 2) * 2], r_O[4 + d // 2][(d % 2) * 2 + 1]])

            SQB = (4, 7, 8)

            def out_ssq(tt, d):
                xt2, xres, sbi, sbh = out_state[tt]
                b = SQB[d % 3]
                S.op("pe", lambda e: e.matmul(bank_ap(sbi, sbh), lhsT=ones1024, rhs=T_b[1][:, b * 1024: b * 1024 + 512],
                                              start=(d == 0), stop=(d == 7)),
                     reads=[r_T[1][b], *r_consts], writes=[r_bank[sbi][sbh]], inc=True)

            def out_A(tt, d0, d1):
                hb = tt % 2
                xt2 = R1_f[:, 4096:8192] if hb == 0 else M_f[:, 4096:8192]
                xres = r_R1[4:8] if hb == 0 else r_Mb[8:16]
                if d0 == 0:
                    sbi, sbh = next_bank()
                    bank_reserved.add((sbi, sbh))
                    out_state[tt] = (xt2, xres, sbi, sbh)
                if d1 == 8:
                    S.dma("sp", lambda e, xt2=xt2, s=s, tt=tt: e.dma_start(out=xt2, in_=xT_d[s, tt]), f"ld_y{hb}", writes=xres)
                for d in range(d0, d1):
                    bi, bh = next_bank()
                    mm_group(bank_ap(bi, bh), r_bank[bi][bh],
                             [(tabs_b[:, (d // 4) * 4096 + c * 512 + (d % 4) * 128: (d // 4) * 4096 + c * 512 + (d % 4 + 1) * 128],
                               mg_ap(c, tt)) for c in range(8)],
                             wo_res(d) + [r_ for c in range(8) for r_ in mg_res(c, tt)])
                    if d >= 1:
                        out_ssq(tt, d - 1)
                    mo_ap, mo_res = mo_buf(tt, d)
                    S.op("act", lambda e, mo_ap=mo_ap, bi=bi, bh=bh, d=d: e.activation(
                        out=mo_ap, in_=bank_ap(bi, bh), func=AF.Copy, scale=vcol(V_POST + d)),
                        reads=[r_bank[bi][bh], *r_consts], writes=mo_res)
                    b = SQB[d % 3]
                    S.op("act", lambda e, b=b, bi=bi, bh=bh: e.activation(out=T_b[1][:, b * 1024: b * 1024 + 512],
                                                                          in_=bank_ap(bi, bh), func=AF.Square),
                         reads=[r_bank[bi][bh]], writes=[r_T[1][b]])
                if d1 == 8:
                    out_ssq(tt, 7)

            def out_B1(tt):
                xt2, xres, sbi, sbh = out_state[tt]
                hb = tt % 2
                rs = T[1][:, 2560:3072]
                rstd_from(bank_ap(sbi, sbh), r_bank[sbi][sbh], rs, r_T[1][5])
                bank_reserved.discard((sbi, sbh))
                mt, mt_res = mo_tile(tt)
                mt3 = mt.rearrange("p (c n) -> p c n", n=512)
                rs3 = rs.unsqueeze(1).to_broadcast([128, 8, 512])
                S.op("dve", lambda e, mt3=mt3, rs3=rs3: e.tensor_tensor(out=mt3, in0=mt3, in1=rs3, op=ALU.mult),
                     reads=mt_res + [r_T[1][5]], writes=mt_res)
                S.op("dve", lambda e, mt=mt, xt2=xt2: e.tensor_tensor(out=xt2, in0=mt, in1=xt2, op=ALU.add),
                     reads=mt_res + xres, writes=xres)
                S.dma("sp", lambda e, xt2=xt2, s=s, tt=tt: e.dma_start(out=x1_d[s, tt], in_=xt2), f"st_x1{hb}",
                      reads=xres, writes=[r_x1d[s][tt]])

            def out_C1(tt):
                xt2, xres, sbi, sbh = out_state[tt]
                S.op("act", lambda e, xt2=xt2: e.activation(out=T_b[1][:, 0:4096], in_=xt2, func=AF.Square),
                     reads=xres, writes=r_T[1][0:4])

            def out_C2(tt):
                bi, bh = next_bank()
                bank_reserved.add((bi, bh))
                mm_group(bank_ap(bi, bh), r_bank[bi][bh],
                         [(ones1024, T_b[1][:, c * 512:(c + 1) * 512]) for c in range(8)], r_T[1][0:4] + [*r_consts])
                out_state[("c", tt)] = (bi, bh)

            def out_C3(tt):
                xt2, xres, sbi, sbh = out_state[tt]
                bi, bh = out_state[("c", tt)]
                rs2 = T[1][:, 3072:3584]
                rstd_from(bank_ap(bi, bh), r_bank[bi][bh], rs2, r_T[1][6])
                bank_reserved.discard((bi, bh))
                for c in range(8):
                    S.op("dve", lambda e, c=c, xt2=xt2, tt=tt: e.scalar_tensor_tensor(
                        out=A[:, c * SEQ + tt * 512: c * SEQ + (tt + 1) * 512], in0=xt2[:, c * 512:(c + 1) * 512],
                        scalar=vcol(V_FPRE + c), in1=rs2, op0=ALU.mult, op1=ALU.mult),
                        reads=list(xres) + [r_T[1][6], *r_consts], writes=[r_A[tt]])

            out_A(0, 0, 8)
            out_B1(0)
            out_A(1, 0, 8)
            out_C1(0)
            out_B1(1)
            for tt in range(2, TT):
                out_A(tt, 0, 4)
                out_C2(tt - 2)
                out_C3(tt - 2)
                out_A(tt, 4, 8)
                out_C1(tt - 1)
                out_B1(tt)
            out_C2(TT - 2)
            out_C3(TT - 2)
            out_C1(TT - 1)
            out_C2(TT - 1)
            out_C3(TT - 1)
            if s + 1 < NSEQ:
                S.dma("sp", lambda e: e.dma_start(out=tabs[:], in_=tab_d[:, :]), "ld_c1r", writes=[r_tabs])

            hid_buf = lambda j: (O, R1, M)[j // 8]

            def hid_res(j, tt):
                if j < 8:
                    return [r_O[j][tt]]
                if j < 16:
                    return [r_R1[j % 8]]
                return [r_Mb[((j % 8) * 4 + tt) // 2]]

            for i in range(2):
                S.op("dve", lambda e, i=i: e.memset(T[i][:, 0:1], 0.0), writes=[r_T[i][0]])
                S.op("dve", lambda e, i=i: e.memset(T[i][:, 2049:2050], 0.0), writes=[r_T[i][4]])
            for g in range(11):
                sl = load_slot(wup_d[g], 4096)
                rg = ring[sl]
                for jj in range(2):
                    j = 2 * g + jj
                    ti = j % 2
                    arow = T[ti]
                    yrow = T[ti][:, 2560:4608]
                    ra = r_T[ti][0:5]
                    ry = r_T[ti][5:9]
                    for tt in range(TT):
                        bi, bh = next_bank()
                        mm_group(bank_ap(bi, bh), r_bank[bi][bh],
                                 [(rg[:, c * 512 + jj * 128: c * 512 + (jj + 1) * 128],
                                   A[:, c * SEQ + tt * 512: c * SEQ + (tt + 1) * 512]) for c in range(8)],
                                 [r_ring[sl], r_A[tt]])
                        S.op("act", lambda e, bi=bi, bh=bh, tt=tt, arow=arow: e.activation(
                            out=arow[:, 1 + tt * 512: 1 + (tt + 1) * 512], in_=bank_ap(bi, bh), func=AF.Copy),
                            reads=[r_bank[bi][bh]], writes=[r_T[ti][tt], r_T[ti][tt + 1]])
                    bb = []
                    for tt in range(TT):
                        bi, bh = next_bank()
                        mm_group(bank_ap(bi, bh), r_bank[bi][bh],
                                 [(rg[:, c * 512 + 256 + jj * 128: c * 512 + 256 + (jj + 1) * 128],
                                   A[:, c * SEQ + tt * 512: c * SEQ + (tt + 1) * 512]) for c in range(8)],
                                 [r_ring[sl], r_A[tt]])
                        bb.append((bi, bh))
                    S.op("dve", lambda e, j=j, arow=arow, yrow=yrow: e.tensor_scalar(
                        out=yrow, in0=arow[:, 0:2048], scalar1=vcol(V_FCW + j), scalar2=None, op0=ALU.mult),
                        reads=ra + [*r_consts], writes=ry)
                    S.op("dve", lambda e, j=j, arow=arow, yrow=yrow: e.scalar_tensor_tensor(
                        out=yrow, in0=arow[:, 1:2049], scalar=vcol(V_FCW + NJ + j), in1=yrow, op0=ALU.mult, op1=ALU.add),
                        reads=ra + ry + [*r_consts], writes=ry)
                    S.op("dve", lambda e, j=j, arow=arow, yrow=yrow: e.scalar_tensor_tensor(
                        out=yrow, in0=arow[:, 2:2050], scalar=vcol(V_FCW + 2 * NJ + j), in1=yrow, op0=ALU.mult, op1=ALU.add),
                        reads=ra + ry + [*r_consts], writes=ry)
                    S.op("act", lambda e, yrow=yrow: e.activation(out=yrow, in_=yrow, func=AF.Gelu_apprx_tanh),
                         reads=ry, writes=ry)
                    hb_ = hid_buf(j)
                    for tt in range(TT):
                        bi, bh = bb[tt]
                        S.op("dve", lambda e, bi=bi, bh=bh, tt=tt, yrow=yrow, hb_=hb_, j=j: e.tensor_tensor(
                            out=hb_[:, (j % 8) * SEQ + tt * 512:(j % 8) * SEQ + (tt + 1) * 512],
                            in0=bank_ap(bi, bh), in1=yrow[:, tt * 512:(tt + 1) * 512], op=ALU.mult),
                            reads=[r_bank[bi][bh], r_T[ti][5 + tt]], writes=hid_res(j, tt))

            def o2_buf(tt, d):
                if tt % 2 == 0:
                    return T[0][:, d * 512:(d + 1) * 512], [r_T[0][d]]
                if d < 4:
                    return M_f[:, (12 + d) * 512:(13 + d) * 512], [r_Mb[12 + d]]
                b = 4 if d == 7 else 6 + (d - 4)
                return T[1][:, b * 512:(b + 1) * 512], [r_T[1][b]]

            r_x1h = [Res(f"x1h{s}_0"), Res(f"x1h{s}_1")]
            r_pace = Res(f"pace{s}")

            def x1_load(t_, extra_reads=()):
                hb_ = t_ % 2
                x1t_ = A_f[:, hb_ * 4096:(hb_ + 1) * 4096]
                S.dma("sp", lambda e, x1t_=x1t_, s=s, t_=t_: e.dma_start(out=x1t_, in_=x1_d[s, t_]), f"ld_x1{hb_}",
                      reads=[r_x1d[s][t_]] + list(extra_reads), writes=[r_x1h[hb_]] + (r_A if t_ < 2 else []))

            x1_load(0)
            for tt in range(TT):
                hb = tt % 2
                x1t = A_f[:, hb * 4096:(hb + 1) * 4096]
                x1res = [r_x1h[hb]]
                sqd = T_b[1][:, 0:4096]
                sbi, sbh = next_bank()
                for d in range(8):
                    sl = load_slot(wdnb_d[d], 2816, reads=[r_wdnb[d]], nb=1)
                    rg = ring[sl]
                    bi, bh = next_bank()
                    if (bi, bh) == (sbi, sbh):
                        bi, bh = next_bank()
                    rds = [r_ring[sl]]
                    for j in range(NJ):
                        for r_ in hid_res(j, tt):
                            if r_ not in rds:
                                rds.append(r_)
                    mm_group(bank_ap(bi, bh), r_bank[bi][bh],
                             [(rg[:, j * 128:(j + 1) * 128],
                               hid_buf(j)[:, (j % 8) * SEQ + tt * 512:(j % 8) * SEQ + (tt + 1) * 512]) for j in range(NJ)], rds)
                    if d >= 1:
                        emit_ssq(d - 1, sbi, sbh, sqd)
                    o2_ap, o2_res = o2_buf(tt, d)
                    S.op("act", lambda e, o2_ap=o2_ap, bi=bi, bh=bh: e.activation(out=o2_ap, in_=bank_ap(bi, bh), func=AF.Copy),
                         reads=[r_bank[bi][bh]], writes=o2_res)
                    S.op("act", lambda e, d=d, bi=bi, bh=bh: e.activation(out=sqd[:, d * 512:(d + 1) * 512], in_=bank_ap(bi, bh),
                                                                          func=AF.Square),
                         reads=[r_bank[bi][bh]], writes=[r_T[1][d // 2]] + ([r_pace] if d == 4 else []))
                    if d == 4 and tt + 1 < TT:
                        x1_load(tt + 1, extra_reads=[r_pace])
                emit_ssq(7, sbi, sbh, sqd)
                if tt == TT - 1 and s + 1 < NSEQ:
                    emit_x_loads(s + 1, "pool")
                rs = T[1][:, 2560:3072]
                rstd_from(bank_ap(sbi, sbh), r_bank[sbi][sbh], rs, r_T[1][5])
                for d in range(8):
                    o2_ap, o2_res = o2_buf(tt, d)
                    S.op("dve", lambda e, d=d, o2_ap=o2_ap: e.scalar_tensor_tensor(
                        out=o2_ap, in0=o2_ap, scalar=vcol(V_FPOST + d), in1=rs,
                        op0=ALU.mult, op1=ALU.mult), reads=o2_res + [r_T[1][5], *r_consts], writes=o2_res)
                    S.op("dve", lambda e, d=d, x1t=x1t, o2_ap=o2_ap: e.tensor_tensor(
                        out=x1t[:, d * 512:(d + 1) * 512], in0=o2_ap, in1=x1t[:, d * 512:(d + 1) * 512],
                        op=ALU.add), reads=o2_res + x1res, writes=x1res)
                out_evs.append(S.dma("sp", lambda e, x1t=x1t, s=s, tt=tt: e.dma_start(out=yT_d[s, tt], in_=x1t),
                                     f"st_y{hb}", reads=x1res + r_A))

        S.wait_all("sp", out_evs)
        S.build(nc, es)
    return nc


def _host_layout(inp):
    f = lambda a: np.ascontiguousarray(np.asarray(a, dtype=np.float32))
    w_in = f(inp["w_in"])[0]
    w_ap = f(inp["w_attn_proj"])[0]
    w_cp = f(inp["w_conv_proj"])[0]
    w_out = f(inp["w_out"])[0]
    w_up = f(inp["w_up"])[0]
    w_down = f(inp["w_down"])[0]
    sh = {}
    sh["w_qkv"] = f(w_in[:, 0:1536].reshape(8, 128, 3, 512).transpose(2, 1, 0, 3).reshape(3, 128, 4096))
    sh["w_ubc"] = f(w_in[:, 1536:4608].reshape(8, 128, 3, 8, 128).transpose(3, 1, 0, 2, 4).reshape(8, 128, 3072))
    W4 = np.stack([w_ap, w_cp, w_in[:, 4608:5632], w_in[:, 5632:6656]], axis=0)
    sh["w_pj"] = f(W4.reshape(4, 8, 128, 8, 128).transpose(3, 2, 1, 0, 4).reshape(8, 128, 4096))
    sh["w_o"] = f(w_out.reshape(8, 128, 2, 512).transpose(2, 1, 0, 3).reshape(2, 128, 4096))
    sh["w_upg"] = f(w_up.reshape(8, 128, 2, 11, 2, 128).transpose(3, 1, 0, 2, 4, 5).reshape(11, 128, 4096))
    sh["w_dn"] = f(w_down.reshape(22, 128, 8, 128).transpose(2, 1, 0, 3).reshape(8, 128, 2816))
    vec = np.zeros((128, NV), np.float32)
    col8 = lambda v: f(v).reshape(8, 128).T
    vec[:, V_PRE:V_PRE + 8] = col8(inp["mix_pre_g"][0])
    vec[:, V_POST:V_POST + 8] = col8(inp["mix_post_g"][0])
    vec[:, V_FPRE:V_FPRE + 8] = col8(inp["ffn_pre_g"][0])
    vec[:, V_FPOST:V_FPOST + 8] = col8(inp["ffn_post_g"][0])
    gb = f(inp["gate_b"])[0]
    vec[:, V_GBA:V_GBA + 8] = col8(gb[:D])
    vec[:, V_GBB:V_GBB + 8] = col8(gb[D:])
    perm = np.arange(128)
    perm = np.where((perm % 64) < 32, perm + 32, perm - 32)
    qg = f(inp["q_norm_g"])[0]
    kg = f(inp["k_norm_g"])[0]
    vec[:, V_QG] = qg
    vec[:, V_QGP] = qg[perm]
    vec[:, V_KG] = kg
    vec[:, V_KGP] = kg[perm]
    mcw = f(inp["mix_conv_w"])[0]
    for k in range(3):
        vec[:, V_MCW + 8 * k:V_MCW + 8 * (k + 1)] = mcw[k].reshape(8, 128).T
    fcw = f(inp["ffn_conv_w"])[0]
    for k in range(3):
        vec[:, V_FCW + NJ * k:V_FCW + NJ * (k + 1)] = fcw[k].reshape(NJ, 128).T
    sh["vecs"] = vec
    t = np.arange(SEQ)
    row_pos = (t // 64).astype(np.float64)
    col_pos = (t % 64).astype(np.float64)
    freqs = 10000.0 ** (-(np.arange(32, dtype=np.float64) / 32.0))
    tabs = np.zeros((128, 2 * SEQ), np.float32)
    for d in range(128):
        pos = row_pos if d < 64 else col_pos
        ang = pos * freqs[d % 32]
        tabs[d, 0:SEQ] = np.cos(ang)
        tabs[d, SEQ:] = np.sin(ang)
    sh["tabs"] = tabs
    cm = np.zeros((128, 512), np.float32)
    cm[:, 0:128] = 1.0 / 1024.0
    cm[:, 128:256] = 1.0 / 128.0
    cm[:, 256:384] = 1.0
    for m in range(128):
        if (m % 64) < 32:
            cm[m + 32, 384 + m] = -1.0
        else:
            cm[m - 32, 384 + m] = 1.0
    sh["cmat"] = cm
    return sh


_NC_CACHE = {}


def kernel(**inputs):
    x = np.asarray(inputs["x"], dtype=np.float32)
    shared = _host_layout(inputs)
    in_maps = []
    for core in range(NCORES):
        xs = x[core * NSEQ:(core + 1) * NSEQ]
        xT = np.ascontiguousarray(xs.reshape(NSEQ, TT, 512, 8, 128).transpose(0, 1, 4, 3, 2)).reshape(NSEQ, TT, 128, 4096)
        m = dict(shared)
        m["xT"] = xT
        in_maps.append(m)
    if "nc" not in _NC_CACHE:
        _NC_CACHE["nc"] = build_nc()
    nc = _NC_CACHE["nc"]
    res = run_bass_kernel_spmd(nc, in_maps, core_ids=list(range(NCORES)))
    outs = []
    for core in range(NCORES):
        yT = np.asarray(res.results[core]["yT"]).reshape(NSEQ, TT, 128, 8, 512)
        outs.append(yT.transpose(0, 1, 4, 3, 2).reshape(NSEQ, SEQ, D))
    return np.ascontiguousarray(np.concatenate(outs, axis=0).astype(np.float32))
```
